# Optimizing a Trainium2 kernel written in Bass

```python
import math
import jax, jax.numpy as jnp
from jax import lax
import numpy as np

D_MODEL = 1024
BATCH = 8
SEQ = 2048
DEPTH = 2

N_EVEN = (DEPTH + 1) // 2
N_ODD = DEPTH // 2
D_MIX = 2 * D_MODEL
HEAD_DIM = 64
SSD_WIDTH = D_MIX // 2
SSD_HEADS = SSD_WIDTH // HEAD_DIM
SSD_GROUPS = 2
SSD_STATE = 128
SSD_CONV = 4
SSD_CHUNK = 128
SSD_CONV_CH = SSD_WIDTH + 2 * SSD_GROUPS * SSD_STATE
MOBA_WIDTH = D_MIX - SSD_WIDTH
MOBA_HEADS = MOBA_WIDTH // HEAD_DIM
MOBA_BLOCK = 256
MOBA_TOPK = 3
MOBA_Q_CHUNK = 32
FOX_WIDTH = 3 * D_MIX // 4
FOX_HEADS = FOX_WIDTH // HEAD_DIM
FOX_Q_BLOCK = 128
S5_WIDTH = D_MIX - FOX_WIDTH
S5_GROUP = 16
S5_GROUPS = S5_WIDTH // S5_GROUP
S5_STATE = 64
EVEN_PROJ = D_MIX + SSD_CONV_CH + SSD_HEADS + 3 * MOBA_WIDTH
ODD_PROJ = D_MIX + 3 * FOX_WIDTH + FOX_HEADS + S5_WIDTH
RMS_EPS = 1e-6

kernel_name = "hybrid_ssd_moba_fox_s5_trunk"


def rmsnorm(x, g):
    xf = x.astype(jnp.float32)
    xf = xf * lax.rsqrt(jnp.mean(xf * xf, axis=-1, keepdims=True) + RMS_EPS)
    return (xf * g.astype(jnp.float32)).astype(x.dtype)


def split_cols(t, widths):
    offs = np.cumsum(widths)[:-1].tolist()
    return jnp.split(t, offs, axis=-1)


def causal_dwconv(u, w, b):
    ch = u.shape[-1]
    out = lax.conv_general_dilated(u, w[:, None, :].astype(u.dtype), window_strides=(1,),
                                   padding=[(w.shape[0] - 1, 0)],
                                   dimension_numbers=('NWC', 'WIO', 'NWC'),
                                   feature_group_count=ch)
    return out + b.astype(u.dtype)


def segsum(a):
    T = a.shape[-1]
    ar = jnp.broadcast_to(a[..., None], a.shape + (T,))
    ar = jnp.where(jnp.tril(jnp.ones((T, T), bool), -1), ar, 0.0)
    cs = jnp.cumsum(ar, axis=-2)
    return jnp.where(jnp.tril(jnp.ones((T, T), bool)), cs, -jnp.inf)


def ssd_chunked(xs, dt, a, b_in, c_in):
    Bsz, L, H, P = xs.shape
    G, N = b_in.shape[2], b_in.shape[3]
    R = H // G
    nc = L // SSD_CHUNK
    x = (xs.astype(jnp.float32) * dt[..., None]).reshape(Bsz, nc, SSD_CHUNK, G, R, P)
    adt = (dt * a).reshape(Bsz, nc, SSD_CHUNK, G, R).transpose(0, 3, 4, 1, 2)
    Bc = b_in.astype(jnp.float32).reshape(Bsz, nc, SSD_CHUNK, G, N)
    Cc = c_in.astype(jnp.float32).reshape(Bsz, nc, SSD_CHUNK, G, N)
    a_cum = jnp.cumsum(adt, axis=-1)
    decay_in = jnp.exp(segsum(adt))
    cb = jnp.einsum('bclgn,bcsgn->bgcls', Cc, Bc)[:, :, None]
    y_diag = jnp.einsum('bgrcls,bcsgrp->bclgrp', cb * decay_in, x)
    decay_states = jnp.exp(a_cum[..., -1:] - a_cum)
    states = jnp.einsum('bclgn,bgrcl,bclgrp->bcgrpn', Bc, decay_states, x)
    a_last = jnp.pad(a_cum[..., -1], ((0, 0), (0, 0), (0, 0), (1, 0)))
    decay_chunk = jnp.exp(segsum(a_last))
    states0 = jnp.concatenate([jnp.zeros_like(states[:, :1]), states], axis=1)
    new_states = jnp.einsum('bgrzc,bcgrpn->bzgrpn', decay_chunk, states0)
    prev_states = new_states[:, :-1]
    y_off = jnp.einsum('bclgn,bcgrpn,bgrcl->bclgrp', Cc, prev_states, jnp.exp(a_cum))
    return (y_diag + y_off).reshape(Bsz, L, H, P)


def moba_attention(q, k, v):
    Bsz, L, H, Dh = q.shape
    nb = -(-L // MOBA_BLOCK)
    pad = nb * MOBA_BLOCK - L
    kp = jnp.pad(k, ((0, 0), (0, pad), (0, 0), (0, 0)))
    vp = jnp.pad(v, ((0, 0), (0, pad), (0, 0), (0, 0)))
    kb = kp.reshape(Bsz, nb, MOBA_BLOCK, H, Dh).transpose(0, 3, 1, 2, 4)
    vb = vp.reshape(Bsz, nb, MOBA_BLOCK, H, Dh).transpose(0, 3, 1, 2, 4)
    kbar = jnp.mean(kb.astype(jnp.float32), axis=3)
    qh = q.transpose(0, 2, 1, 3)
    gate = jnp.einsum('bhqd,bhnd->bhqn', qh.astype(jnp.float32), kbar)
    pos = jnp.arange(L, dtype=jnp.int32)
    qblk = pos // MOBA_BLOCK
    past = jnp.arange(nb, dtype=jnp.int32)[None, :] < qblk[:, None]
    gate = jnp.where(past, gate, -jnp.inf)
    n_top = min(MOBA_TOPK, nb)
    _, top_idx = lax.top_k(gate, n_top)
    own = jnp.broadcast_to(qblk[None, None, :, None], (Bsz, H, L, 1)).astype(top_idx.dtype)
    sel = jnp.concatenate([top_idx, own], axis=-1)
    nsel = n_top + 1
    sel_ok = jnp.concatenate([jnp.arange(n_top, dtype=jnp.int32)[None, :] < qblk[:, None],
                              jnp.ones((L, 1), bool)], axis=-1)
    nc = L // MOBA_Q_CHUNK
    scale = 1.0 / math.sqrt(Dh)
    gather = jax.vmap(jax.vmap(lambda blocks, ids: blocks[ids]))
    offs = jnp.arange(MOBA_BLOCK, dtype=jnp.int32)

    def chunk(args):
        qc, selc, okc, posc = args
        kg = gather(kb, selc)
        vg = gather(vb, selc)
        s = jnp.einsum('bhqd,bhqjpd->bhqjp', qc, kg).astype(jnp.float32) * scale
        kpos = selc[..., None] * MOBA_BLOCK + offs
        mask = okc[None, None, :, :, None] & (kpos <= posc[None, None, :, None, None])
        s = jnp.where(mask, s, -jnp.inf)
        p = jax.nn.softmax(s.reshape(Bsz, H, MOBA_Q_CHUNK, nsel * MOBA_BLOCK), axis=-1).reshape(s.shape)
        return jnp.einsum('bhqjp,bhqjpd->bhqd', p.astype(vg.dtype), vg)

    xs = (qh.reshape(Bsz, H, nc, MOBA_Q_CHUNK, Dh).transpose(2, 0, 1, 3, 4),
          sel.reshape(Bsz, H, nc, MOBA_Q_CHUNK, nsel).transpose(2, 0, 1, 3, 4),
          sel_ok.reshape(nc, MOBA_Q_CHUNK, nsel),
          pos.reshape(nc, MOBA_Q_CHUNK))
    out = lax.map(chunk, xs)
    return out.transpose(1, 0, 3, 2, 4).reshape(Bsz, L, H, Dh)


def forgetting_attention(q, k, v, log_f):
    Bsz, L, H, Dh = q.shape
    F = jnp.cumsum(log_f, axis=1).transpose(0, 2, 1)
    qh = q.transpose(0, 2, 1, 3)
    kh = k.transpose(0, 2, 1, 3)
    vh = v.transpose(0, 2, 1, 3)
    nq = L // FOX_Q_BLOCK
    kpos = jnp.arange(L, dtype=jnp.int32)
    scale = 1.0 / math.sqrt(Dh)

    def block(args):
        qc, Fc, posc = args
        s = jnp.einsum('bhqd,bhsd->bhqs', qc, kh).astype(jnp.float32) * scale
        s = s + Fc[..., None] - F[:, :, None, :]
        s = jnp.where(posc[:, None] >= kpos[None, :], s, -jnp.inf)
        p = jax.nn.softmax(s, axis=-1)
        return jnp.einsum('bhqs,bhsd->bhqd', p.astype(vh.dtype), vh)

    xs = (qh.reshape(Bsz, H, nq, FOX_Q_BLOCK, Dh).transpose(2, 0, 1, 3, 4),
          F.reshape(Bsz, H, nq, FOX_Q_BLOCK).transpose(2, 0, 1, 3),
          kpos.reshape(nq, FOX_Q_BLOCK))
    out = lax.map(block, xs)
    return out.transpose(1, 0, 3, 2, 4).reshape(Bsz, L, H, Dh)


def _complex_affine_combine(e1, e2):
    a1r, a1i, b1r, b1i = e1
    a2r, a2i, b2r, b2i = e2
    return (a2r * a1r - a2i * a1i,
            a2r * a1i + a2i * a1r,
            a2r * b1r - a2i * b1i + b2r,
            a2r * b1i + a2i * b1r + b2i)


def s5_ssm(u, lam_re, lam_im, log_dt, b_re, b_im, c_re, c_im, d_skip):
    Bsz, L, W = u.shape
    uf = u.astype(jnp.float32).reshape(Bsz, L, S5_GROUPS, S5_GROUP)
    dt = jnp.exp(log_dt.astype(jnp.float32))[:, None]
    lr = lam_re.astype(jnp.float32)
    li = lam_im.astype(jnp.float32)
    mag = jnp.exp(lr * dt)
    ar = mag * jnp.cos(li * dt)
    ai = mag * jnp.sin(li * dt)
    den = lr * lr + li * li
    qr = ((ar - 1.0) * lr + ai * li) / den
    qi = (ai * lr - (ar - 1.0) * li) / den
    br = b_re.astype(jnp.float32)
    bi = b_im.astype(jnp.float32)
    bbr = qr[..., None] * br - qi[..., None] * bi
    bbi = qr[..., None] * bi + qi[..., None] * br
    bu_r = jnp.einsum('blgc,gnc->blgn', uf, bbr)
    bu_i = jnp.einsum('blgc,gnc->blgn', uf, bbi)
    a_r = jnp.broadcast_to(ar[None, None], (1, L, S5_GROUPS, S5_STATE))
    a_i = jnp.broadcast_to(ai[None, None], (1, L, S5_GROUPS, S5_STATE))
    _, _, sr, si = lax.associative_scan(_complex_affine_combine, (a_r, a_i, bu_r, bu_i), axis=1)
    y = (jnp.einsum('blgn,gcn->blgc', sr, c_re.astype(jnp.float32))
         - jnp.einsum('blgn,gcn->blgc', si, c_im.astype(jnp.float32)))
    return y.reshape(Bsz, L, W) + d_skip.astype(jnp.float32) * uf.reshape(Bsz, L, W)


def even_mixer(h, in_w, conv_w, conv_b, dt_bias, a_log, d_skip, norm_g, out_w):
    Bsz, L, _ = h.shape
    proj = h @ in_w
    z_a, z_b, xbc, dt_raw, q, k, v = split_cols(
        proj, [SSD_WIDTH, MOBA_WIDTH, SSD_CONV_CH, SSD_HEADS, MOBA_WIDTH, MOBA_WIDTH, MOBA_WIDTH])
    xbc = jax.nn.silu(causal_dwconv(xbc, conv_w, conv_b))
    xs, b_in, c_in = split_cols(xbc, [SSD_WIDTH, SSD_GROUPS * SSD_STATE, SSD_GROUPS * SSD_STATE])
    xs = xs.reshape(Bsz, L, SSD_HEADS, HEAD_DIM)
    b_in = b_in.reshape(Bsz, L, SSD_GROUPS, SSD_STATE)
    c_in = c_in.reshape(Bsz, L, SSD_GROUPS, SSD_STATE)
    dt = jax.nn.softplus(dt_raw.astype(jnp.float32) + dt_bias.astype(jnp.float32))
    a = -jnp.exp(a_log.astype(jnp.float32))
    y_a = ssd_chunked(xs, dt, a, b_in, c_in) + d_skip.astype(jnp.float32)[:, None] * xs.astype(jnp.float32)
    y_a = y_a.reshape(Bsz, L, SSD_WIDTH) * jax.nn.silu(z_a.astype(jnp.float32))
    y_a = rmsnorm(y_a, norm_g).astype(h.dtype)
    y_b = moba_attention(q.reshape(Bsz, L, MOBA_HEADS, HEAD_DIM),
                         k.reshape(Bsz, L, MOBA_HEADS, HEAD_DIM),
                         v.reshape(Bsz, L, MOBA_HEADS, HEAD_DIM)).reshape(Bsz, L, MOBA_WIDTH)
    y_b = y_b.astype(h.dtype) * jax.nn.silu(z_b)
    return jnp.concatenate([y_a, y_b], axis=-1) @ out_w


def odd_mixer(h, in_w, fgate_b, lam_re, lam_im, log_dt, b_re, b_im, c_re, c_im, d_skip, glu_w, glu_b, out_w):
    Bsz, L, _ = h.shape
    proj = h @ in_w
    z_c, z_d, q, k, v, f_raw, u = split_cols(
        proj, [FOX_WIDTH, S5_WIDTH, FOX_WIDTH, FOX_WIDTH, FOX_WIDTH, FOX_HEADS, S5_WIDTH])
    log_f = jax.nn.log_sigmoid(f_raw.astype(jnp.float32) + fgate_b.astype(jnp.float32))
    y_c = forgetting_attention(q.reshape(Bsz, L, FOX_HEADS, HEAD_DIM),
                               k.reshape(Bsz, L, FOX_HEADS, HEAD_DIM),
                               v.reshape(Bsz, L, FOX_HEADS, HEAD_DIM), log_f).reshape(Bsz, L, FOX_WIDTH)
    y_c = y_c.astype(h.dtype) * jax.nn.silu(z_c)
    y_d = jax.nn.gelu(s5_ssm(u, lam_re, lam_im, log_dt, b_re, b_im, c_re, c_im, d_skip))
    y_d = y_d * jax.nn.sigmoid(y_d @ glu_w.astype(jnp.float32) + glu_b.astype(jnp.float32))
    y_d = y_d.astype(h.dtype) * jax.nn.silu(z_d)
    return jnp.concatenate([y_c, y_d], axis=-1) @ out_w


def setup_inputs(seed: int = 0) -> dict:
    key = jax.random.key(seed)
    ks = iter(jax.random.split(key, 40))
    f32 = jnp.float32

    def nrm(shape, s):
        return s * jax.random.normal(next(ks), shape, f32)

    def unif(shape, lo, hi):
        return jax.random.uniform(next(ks), shape, f32, lo, hi)

    x = nrm((BATCH, SEQ, D_MODEL), 1.0)
    c = nrm((BATCH, D_MODEL), 1.0)
    ada_w = nrm((DEPTH, D_MODEL, 3 * D_MODEL), 0.5 * D_MODEL ** -0.5)
    ada_b = nrm((DEPTH, 3 * D_MODEL), 0.02)
    pre_g = 1.0 + nrm((DEPTH, D_MODEL), 0.02)
    post_g = 1.0 + nrm((DEPTH, D_MODEL), 0.02)
    even_in_w = nrm((N_EVEN, D_MODEL, EVEN_PROJ), D_MODEL ** -0.5)
    even_conv_w = nrm((N_EVEN, SSD_CONV, SSD_CONV_CH), SSD_CONV ** -0.5)
    even_conv_b = nrm((N_EVEN, SSD_CONV_CH), 0.02)
    dt0 = jnp.exp(unif((N_EVEN, SSD_HEADS), math.log(1e-3), math.log(1e-1)))
    even_dt_bias = dt0 + jnp.log(-jnp.expm1(-dt0))
    even_a_log = jnp.log(unif((N_EVEN, SSD_HEADS), 1.0, 16.0))
    even_d_skip = 1.0 + nrm((N_EVEN, SSD_HEADS), 0.1)
    even_norm_g = 1.0 + nrm((N_EVEN, SSD_WIDTH), 0.02)
    even_out_w = nrm((N_EVEN, D_MIX, D_MODEL), D_MIX ** -0.5)
    odd_in_w = nrm((N_ODD, D_MODEL, ODD_PROJ), D_MODEL ** -0.5)
    odd_fgate_b = 1.0 + nrm((N_ODD, FOX_HEADS), 0.5)
    odd_lam_re = -0.5 + nrm((N_ODD, S5_GROUPS, S5_STATE), 0.01)
    odd_lam_im = jnp.pi * jnp.arange(S5_STATE, dtype=f32) + nrm((N_ODD, S5_GROUPS, S5_STATE), 0.01)
    odd_log_dt = unif((N_ODD, S5_GROUPS), math.log(1e-3), math.log(1e-1))
    odd_b_re = nrm((N_ODD, S5_GROUPS, S5_STATE, S5_GROUP), (2 * S5_GROUP) ** -0.5)
    odd_b_im = nrm((N_ODD, S5_GROUPS, S5_STATE, S5_GROUP), (2 * S5_GROUP) ** -0.5)
    odd_c_re = nrm((N_ODD, S5_GROUPS, S5_GROUP, S5_STATE), (2 * S5_STATE) ** -0.5)
    odd_c_im = nrm((N_ODD, S5_GROUPS, S5_GROUP, S5_STATE), (2 * S5_STATE) ** -0.5)
    odd_d_skip = nrm((N_ODD, S5_WIDTH), 1.0)
    odd_glu_w = nrm((N_ODD, S5_WIDTH, S5_WIDTH), S5_WIDTH ** -0.5)
    odd_glu_b = nrm((N_ODD, S5_WIDTH), 0.02)
    odd_out_w = nrm((N_ODD, D_MIX, D_MODEL), D_MIX ** -0.5)
    return {"x": x, "c": c, "ada_w": ada_w, "ada_b": ada_b, "pre_g": pre_g, "post_g": post_g,
            "even_in_w": even_in_w, "even_conv_w": even_conv_w, "even_conv_b": even_conv_b,
            "even_dt_bias": even_dt_bias, "even_a_log": even_a_log, "even_d_skip": even_d_skip,
            "even_norm_g": even_norm_g, "even_out_w": even_out_w,
            "odd_in_w": odd_in_w, "odd_fgate_b": odd_fgate_b, "odd_lam_re": odd_lam_re,
            "odd_lam_im": odd_lam_im, "odd_log_dt": odd_log_dt, "odd_b_re": odd_b_re,
            "odd_b_im": odd_b_im, "odd_c_re": odd_c_re, "odd_c_im": odd_c_im,
            "odd_d_skip": odd_d_skip, "odd_glu_w": odd_glu_w, "odd_glu_b": odd_glu_b,
            "odd_out_w": odd_out_w}


def reference(x, c, ada_w, ada_b, pre_g, post_g,
              even_in_w, even_conv_w, even_conv_b, even_dt_bias, even_a_log, even_d_skip,
              even_norm_g, even_out_w,
              odd_in_w, odd_fgate_b, odd_lam_re, odd_lam_im, odd_log_dt, odd_b_re, odd_b_im,
              odd_c_re, odd_c_im, odd_d_skip, odd_glu_w, odd_glu_b, odd_out_w):
    cond = jax.nn.silu(c)
    for layer in range(DEPTH):
        mod = cond @ ada_w[layer] + ada_b[layer]
        shift, scale, gate = jnp.split(mod, 3, axis=-1)
        h = rmsnorm(x, pre_g[layer]) * (1.0 + scale[:, None, :]) + shift[:, None, :]
        i = layer // 2
        if layer % 2 == 0:
            y = even_mixer(h, even_in_w[i], even_conv_w[i], even_conv_b[i], even_dt_bias[i],
                           even_a_log[i], even_d_skip[i], even_norm_g[i], even_out_w[i])
        else:
            y = odd_mixer(h, odd_in_w[i], odd_fgate_b[i], odd_lam_re[i], odd_lam_im[i], odd_log_dt[i],
                          odd_b_re[i], odd_b_im[i], odd_c_re[i], odd_c_im[i], odd_d_skip[i],
                          odd_glu_w[i], odd_glu_b[i], odd_out_w[i])
        x = x + gate[:, None, :] * rmsnorm(y, post_g[layer])
    return x
```

```python
import numpy as np
from contextlib import ExitStack
import concourse.bass as bass
import concourse.mybir as mybir
from concourse.bass_utils import run_bass_kernel_spmd

F32 = mybir.dt.float32
BF16 = mybir.dt.bfloat16
I32 = mybir.dt.int32
AF = mybir.ActivationFunctionType
ALU = mybir.AluOpType
AX = mybir.AxisListType

ENGS = ["pe", "act", "dve", "pool", "sp"]
SAME_ENGINE_SYNC = {"pe": False, "act": True, "dve": True, "pool": True, "sp": False}
SELF_DIST = 10 ** 9


class Buf:
    __slots__ = ("name", "w", "r")

    def __init__(self, name=""):
        self.name = name
        self.w = None
        self.r = {}


class Sched:
    def __init__(self, nc, st):
        self.nc = nc
        self.st = st
        self.ops = {e: [] for e in ENGS}
        self.cnt = {e: 0 for e in ENGS}
        self.known = {e: {} for e in ENGS}
        self.snap = {}
        self.dcnt = {}
        self.nops = 0

    def sb(self, name, shape, dtype):
        self.nalloc = getattr(self, "nalloc", 0) + 1
        return self.st.enter_context(self.nc.sbuf_tensor("%s_%d" % (name, self.nalloc), list(shape), dtype))

    def ps(self, name, shape, dtype):
        return self.st.enter_context(self.nc.psum_tensor(name, list(shape), dtype))

    def _deps(self, eng, reads, writes):
        deps = {}
        for b in reads:
            if b.w is not None and deps.get(b.w[0], 0) < b.w[1]:
                deps[b.w[0]] = b.w[1]
        for b in writes:
            if b.w is not None and deps.get(b.w[0], 0) < b.w[1]:
                deps[b.w[0]] = b.w[1]
            for k, v in b.r.items():
                if deps.get(k, 0) < v:
                    deps[k] = v
        kn = self.known[eng]
        waits = []
        for k, v in deps.items():
            if k == eng and (not SAME_ENGINE_SYNC[eng] or (eng != "pool" and v < self.cnt[eng] - (SELF_DIST - 1))):
                continue
            if kn.get(k, 0) >= v:
                continue
            waits.append((k, v))
        for k, v in waits:
            sn = self.snap.get((k, v))
            if sn:
                for kk, vv in sn.items():
                    if kn.get(kk, 0) < vv:
                        kn[kk] = vv
            if kn.get(k, 0) < v:
                kn[k] = v
        return waits

    def _mark(self, me, reads, writes):
        k, v = me
        for b in reads:
            if b.r.get(k, 0) < v:
                b.r[k] = v
        for b in writes:
            b.w = me
            b.r = {}

    def op(self, eng, fn, reads=(), writes=()):
        waits = self._deps(eng, reads, writes)
        self.cnt[eng] += 1
        me = (eng, self.cnt[eng])
        self.snap[me] = dict(self.known[eng])
        self.ops[eng].append((waits, fn, eng, self.cnt[eng]))
        self._mark(me, reads, writes)
        self.nops += 1

    def dma(self, eng, out, in_, reads=(), writes=(), sem=None):
        if sem is None or sem in ("misc", "wg", "modbc", "modrows", "dbg"):
            self.nuniq = getattr(self, "nuniq", 0) + 1
            sem = "u%d" % self.nuniq
        waits = self._deps(eng, reads, writes)
        self.dcnt[sem] = self.dcnt.get(sem, 0) + 16
        me = (sem, self.dcnt[sem])
        self.snap[me] = dict(self.known[eng])
        self.ops[eng].append((waits, lambda e: e.dma_start(out=out, in_=in_), sem, 16))
        self._mark(me, reads, writes)
        self.nops += 1

    def finish(self):
        nc = self.nc
        fin = [(k, v) for k, v in self.dcnt.items()]
        keys = list(ENGS) + list(self.dcnt.keys())
        sems = {k: self.st.enter_context(nc.semaphore("s_" + k)) for k in keys}
        targets = {e: set() for e in ENGS}
        for eng in ENGS:
            for waits, fn, key, amt in self.ops[eng]:
                for k, v in waits:
                    if k in targets:
                        targets[k].add(v)
        rank = {e: {v: i + 1 for i, v in enumerate(sorted(targets[e]))} for e in ENGS}
        self.nsignal = {e: len(rank[e]) for e in ENGS}
        block = self.st.enter_context(nc.Block())
        names = {"pe": "tensor", "act": "scalar", "dve": "vector", "pool": "gpsimd", "sp": "sync"}
        for eng in ENGS:
            lst = self.ops[eng]

            def body(e, lst=lst, eng=eng):
                for waits, fn, key, amt in lst:
                    for k, v in waits:
                        e.wait_ge(sems[k], rank[k][v] if k in rank else v)
                    if fn is None:
                        continue
                    ins = fn(e)
                    if key in rank:
                        if amt in rank[key]:
                            ins.then_inc(sems[key], 1)
                    else:
                        ins.then_inc(sems[key], amt)
                if eng == "sp":
                    for k, v in fin:
                        e.wait_ge(sems[k], v)
            getattr(block, names[eng])(body)

    def barrier(self):
        for e in ENGS:
            kn = self.known[e]
            waits = []
            for k in ENGS:
                if k != e and kn.get(k, 0) < self.cnt[k]:
                    waits.append((k, self.cnt[k])); kn[k] = self.cnt[k]
            for k, v in self.dcnt.items():
                if kn.get(k, 0) < v:
                    waits.append((k, v)); kn[k] = v
            if waits:
                self.ops[e].append((waits, None, None, 0))

    def act(self, out, in_, func, r, w, **kw):
        self.op("act", lambda e: e.activation(out=out, in_=in_, func=func, **kw), r, w)

    def tt(self, eng, out, in0, in1, op, r, w):
        self.op(eng, lambda e: e.tensor_tensor(out=out, in0=in0, in1=in1, op=op), r, w)

    def ts(self, eng, out, in0, s1, s2, op0, op1, r, w, **kw):
        if s2 is None:
            self.op(eng, lambda e: e.tensor_scalar(out=out, in0=in0, scalar1=s1, scalar2=None, op0=op0, **kw), r, w)
        else:
            self.op(eng, lambda e: e.tensor_scalar(out=out, in0=in0, scalar1=s1, scalar2=s2, op0=op0, op1=op1, **kw), r, w)

    def stt(self, out, in0, scalar, in1, op0, op1, r, w):
        self.op("dve", lambda e: e.scalar_tensor_tensor(out=out, in0=in0, scalar=scalar, in1=in1, op0=op0, op1=op1), r, w)

    def copy(self, eng, out, in_, r, w):
        if eng == "act":
            self.op("act", lambda e: e.activation(out=out, in_=in_, func=AF.Copy), r, w)
        else:
            self.op(eng, lambda e: e.tensor_copy(out=out, in_=in_), r, w)

    def mm(self, out, lhsT, rhs, start, stop, r, w, skip=False):
        self.op("pe", lambda e: e.matmul(out, lhsT=lhsT, rhs=rhs, start=start, stop=stop, skip_group_check=skip), r, w)

    def tr(self, out, in_, ident, r, w):
        self.op("pe", lambda e: e.transpose(out=out, in_=in_, identity=ident), r, w)

    def memset(self, eng, ap, val, w):
        self.op(eng, lambda e: e.memset(ap, val), (), w)

    def asel(self, out, in_, pattern, cmp, fill, base, cm, r, w):
        self.op("pool", lambda e: e.affine_select(out=out, in_=in_, pattern=pattern, compare_op=cmp, fill=fill,
                                                  base=base, channel_multiplier=cm), r, w)


L = 2048
D = 1024
NT = 16
BIG = 30000.0
EPS = 1e-6

E_ZA, E_ZB, E_XS, E_B, E_C, E_DT, E_Q, E_K, E_V = 0, 1024, 2048, 3072, 3328, 3584, 3600, 4624, 5648
O_ZC, O_ZD, O_Q, O_K, O_V, O_F, O_U = 0, 1536, 2048, 3584, 5120, 6656, 6680


class Prog:
    def __init__(self, dbg=None):
        self.dbg = dbg or []
        self.nc = bass.Bass("TRN2", target_bir_lowering=False)
        self.st = ExitStack()
        self.S = Sched(self.nc, self.st)
        self.wslot_i = 0
        self.skip_ssd = False
        self.skip_s5 = False

    def din(self, name, shape, dtype=F32):
        return self.nc.dram_tensor(name, list(shape), dtype, kind="ExternalInput").ap()

    def declare(self):
        self.x = self.din("x", [L, D])
        self.c_t = self.din("c_t", [128, 8])
        self.ada_w = self.din("ada_w", [2, D, 3 * D])
        self.ada_b = self.din("ada_b", [2, 3 * D])
        self.pre_g = self.din("pre_g", [2, D])
        self.post_g = self.din("post_g", [2, D])
        self.in_w = [self.din("even_in_w", [D, 6672]), self.din("odd_in_w", [D, 7192])]
        self.out_w = [self.din("even_out_w", [2 * D, D]), self.din("odd_out_w", [2 * D, D])]
        self.fgate_b = self.din("odd_fgate_b", [1, 24])
        self.conv_p = self.din("conv_p", [128, 60])
        self.ssd_rows = self.din("ssd_rows", [1, 48])
        self.even_norm_g = self.din("even_norm_g", [1, 1024])
        self.s5_state = self.din("s5_state", [128, 48])
        self.s5_vec = self.din("s5_vec", [128, 8])
        self.s5_B = self.din("s5_B", [128, 512])
        self.s5_C = self.din("s5_C", [128, 512])
        self.glu_w = self.din("odd_glu_w", [512, 512])
        self.out = self.nc.dram_tensor("out", [L, D], F32, kind="ExternalOutput").ap()
        self.x1 = self.nc.dram_tensor("x1_scratch", [L, D], F32).ap()
        self.modrows = self.nc.dram_tensor("modrows", [2, 3, D], F32).ap()
        self.dbg_out = {}
        for name, shape, dtype in self.dbg:
            self.dbg_out[name] = self.nc.dram_tensor("dbg_" + name, list(shape), dtype, kind="ExternalOutput").ap()

    def alloc_persistent(self):
        S = self.S
        self.hT = S.sb("hT", [128, 8 * L], BF16)
        self.hT3 = self.hT[:].rearrange("p (c t) -> p c t", c=8)
        self.hT_b = [Buf("hT%d" % i) for i in range(NT)]
        self.yT = S.sb("yT", [128, 16 * L], BF16)
        self.yT3 = self.yT[:].rearrange("p (c t) -> p c t", c=16)
        self.yT_b = [[Buf("yT%d_%d" % (c, i)) for i in range(2)] for c in range(16)]
        self.ident = S.sb("ident", [128, 128], BF16)
        self.causT = S.sb("causT", [128, 128], BF16)
        self.T_f = S.sb("T_f", [128, 128], F32)
        self.U_f = S.sb("U_f", [128, 128], F32)
        self.ones_f = S.sb("ones_f", [128, 128], F32)
        self.mhalf = S.sb("mhalf", [128, 1], F32)
        self.cst_b = Buf("consts")
        self.wslot = [S.sb("wslot%d" % i, [128, 8 * 512], BF16) for i in range(3)]
        self.wslot_b = [Buf("wslot%d" % i) for i in range(3)]
        self.xin_b = [Buf("xin%d" % i) for i in range(2)]
        self.modbc_b = Buf("modbc")
        self.stat = S.sb("stat", [128, 128], F32)
        self.stat_bs = [Buf("stat%d" % i) for i in range(8)]
        self.stat_b = self.stat_bs[0]
        self.pb = [S.ps("pb%d" % i, [128, 512], F32) for i in range(8)]
        self.pb_b = [Buf("pb%d" % i) for i in range(8)]

    def consts(self):
        S = self.S
        w = [self.cst_b]
        S.memset("pool", self.ident[:], 0.0, w)
        S.asel(self.ident[:], self.ident[:], [[-1, 128]], ALU.not_equal, 1.0, 0, 1, w, w)
        S.memset("pool", self.causT[:], 0.0, w)
        S.asel(self.causT[:], self.causT[:], [[1, 128]], ALU.is_ge, -BIG, 0, -1, w, w)
        S.memset("pool", self.T_f[:], 1.0, w)
        S.asel(self.T_f[:], self.T_f[:], [[1, 128]], ALU.is_ge, 0.0, 0, -1, w, w)
        S.memset("pool", self.U_f[:], 1.0, w)
        S.asel(self.U_f[:], self.U_f[:], [[-1, 128]], ALU.is_ge, 0.0, -1, 1, w, w)
        S.memset("pool", self.ones_f[:], 1.0, w)
        S.memset("pool", self.mhalf[:], -0.5, w)

    def load_w(self, l, col0, ncols):
        i = self.wslot_i % 3
        self.wslot_i += 1
        v = self.wslot[i][:, 0:8 * ncols].rearrange("p (c n) -> p c n", c=8)
        src = self.in_w[l].rearrange("(c p) n -> p c n", p=128)[:, :, col0:col0 + ncols]
        self.S.dma("pool", v, src, reads=[], writes=[self.wslot_b[i]], sem="w%d" % i)
        return v, self.wslot_b[i]

    def scol(self, slot, i):
        return self.stat[:, 16 * slot + i:16 * slot + i + 1]

    def rstd_from_ss(self, ss_ap, n, r, slot=0):
        S = self.S
        b = self.stat_bs[slot]
        c8, c9, c10 = self.scol(slot, 8), self.scol(slot, 9), self.scol(slot, 10)
        S.ts("dve", c8, ss_ap, 1.0 / n, EPS, ALU.mult, ALU.add, r + [b], [b])
        S.tt("pool", c10, c8, self.mhalf[:], ALU.pow, [b, self.cst_b], [b])
        return c10

    def adaln_items(self, l, nstage):
        S = self.S
        stage = [S.sb("adast%d" % i, [128, 8 * 512], F32) for i in range(nstage)]
        stage_b = [Buf("adast%d" % i) for i in range(nstage)]
        row = S.sb("row", [1, 3 * D], F32)
        rows2 = S.sb("rows2", [1, 2 * D], F32)
        row_b = Buf("row")
        cond, cond_b = self.cond, self.cond_b
        items = []

        def first():
            S.dma("sp", row[:], self.ada_b[l:l + 1, :], [], [row_b], sem="misc")
            S.dma("sp", rows2[:, 0:D], self.pre_g[l:l + 1, :], [], [row_b], sem="misc")
            S.dma("sp", rows2[:, D:2 * D], self.post_g[l:l + 1, :], [], [row_b], sem="misc")
        items.append(first)

        def ld(cb):
            sl = cb % nstage
            v = stage[sl][:].rearrange("p (c n) -> p c n", c=8)
            src = self.ada_w[l].rearrange("(c p) n -> p c n", p=128)[:, :, cb * 512:(cb + 1) * 512]
            S.dma("sp", v, src, [], [stage_b[sl]], sem="adast%d_%d" % (l, sl))

        def blk(cb):
            def f():
                if cb == 0:
                    for i in range(min(nstage, 6)):
                        ld(i)
                sl = cb % nstage
                v = stage[sl][:].rearrange("p (c n) -> p c n", c=8)
                pbi = 3
                for kc in range(8):
                    S.mm(self.pb[pbi][0:1, :], cond[:, kc:kc + 1], v[:, kc, :], kc == 0, kc == 7,
                         [cond_b, stage_b[sl]], [self.pb_b[pbi]])
                S.tt("dve", row[:, cb * 512:(cb + 1) * 512], row[:, cb * 512:(cb + 1) * 512], self.pb[pbi][0:1, :],
                     ALU.add, [row_b, self.pb_b[pbi]], [row_b])
                if cb + nstage < 6:
                    ld(cb + nstage)
            return f
        for cb in range(6):
            items.append(blk(cb))

        def last():
            S.stt(row[:, D:2 * D], row[:, D:2 * D], 1.0, rows2[:, 0:D], ALU.add, ALU.mult, [row_b], [row_b])
            S.tt("dve", row[:, 2 * D:3 * D], row[:, 2 * D:3 * D], rows2[:, D:2 * D], ALU.mult, [row_b], [row_b])
            S.dma("sp", self.modrows[l].rearrange("a d -> (a d)").rearrange("(o n) -> o n", o=1), row[:], [row_b], [],
                  sem="modrows")
        items.append(last)
        return items

    def adaln(self, layers):
        S = self.S
        if not hasattr(self, "cond"):
            self.cond = S.sb("cond", [128, 8], F32)
            self.cond_b = Buf("cond")
            S.dma("sp", self.cond[:], self.c_t, [], [self.cond_b], sem="misc")
            S.act(self.cond[:], self.cond[:], AF.Silu, [self.cond_b], [self.cond_b])
        with ExitStack() as st:
            S.st, old = st, S.st
            for l in layers:
                with ExitStack() as st2:
                    S.st = st2
                    for it in self.adaln_items(l, 2):
                        it()
                    S.st = st
                    S.barrier()
            S.st = old
            S.barrier()
        S.barrier()

    def load_modbc(self, l, c0, c1):
        t = self.S.sb("modbc", [128, (c1 - c0) * D], F32)
        b = Buf("modbc")
        src = self.modrows[l].rearrange("a d -> (a d)").rearrange("(o n) -> o n", o=1)[:, c0 * D:c1 * D]
        self.S.dma("sp", t[:], src.to_broadcast([128, (c1 - c0) * D]), [], [b], sem="modbc")
        return t, b

    def phase_c(self, l, xsrc, xdst, wout, wout_b, res, gp, gp_b, PA=None):
        S = self.S
        wsrc = self.out_w[l].rearrange("(c p) n -> p c n", p=128)
        wq_b = [Buf("wout_q%d" % q) for q in range(4)]
        for q in range(4):
            S.dma("pool", wout[:, q * 4:(q + 1) * 4, :], wsrc[:, q * 4:(q + 1) * 4, :], [wout_b], [wq_b[q]], sem="wout%d" % q)
        rb = [Buf("res%d" % i) for i in range(3)]
        xb3 = [Buf("xinc%d" % i) for i in range(3)]

        def st_mm(tt):
            sl = tt % 3
            sb_ = self.stat_bs[2 + sl]
            xi, xb = self.xin[sl], xb3[sl]
            S.dma("sp", xi[:], xsrc[tt * 128:(tt + 1) * 128, :], [], [xb], sem="xin%d" % sl)
            banks = [(2 * (tt % 3)) + 0, (2 * (tt % 3)) + 1]
            for nb in range(2):
                bi = banks[nb]
                for kc in range(16):
                    S.mm(self.pb[bi][:], self.yT3[:, kc, tt * 128:(tt + 1) * 128], wout[:, kc, nb * 512:(nb + 1) * 512],
                         kc == 0, kc == 15, [self.yT_b[kc][tt // 8], wq_b[kc // 4]], [self.pb_b[bi]])

        def st_post(tt):
            sl = tt % 3
            sb_ = self.stat_bs[2 + sl]
            xi, xb = self.xin[sl], xb3[sl]
            banks = [(2 * (tt % 3)) + 0, (2 * (tt % 3)) + 1]
            for nb in range(2):
                bi = banks[nb]
                S.act(res[sl][:, nb * 512:(nb + 1) * 512], self.pb[bi][:], AF.Square, [self.pb_b[bi]], [rb[sl], sb_],
                      accum_out=self.scol(2 + sl, nb))
            S.tt("dve", self.scol(2 + sl, 2), self.scol(2 + sl, 0), self.scol(2 + sl, 1), ALU.add, [sb_], [sb_])
            self.rstd_from_ss(self.scol(2 + sl, 2), D, [], slot=2 + sl)

        def st_post2(tt):
            sl = tt % 3
            sb_ = self.stat_bs[2 + sl]
            xi, xb = self.xin[sl], xb3[sl]
            banks = [(2 * (tt % 3)) + 0, (2 * (tt % 3)) + 1]
            rstd = self.scol(2 + sl, 10)
            for nb in range(2):
                bi = banks[nb]
                S.stt(res[sl][:, nb * 512:(nb + 1) * 512], self.pb[bi][:], rstd, gp[:, nb * 512:(nb + 1) * 512],
                      ALU.mult, ALU.mult, [self.pb_b[bi], sb_, gp_b], [rb[sl]])
            S.tt("dve", res[sl][:], res[sl][:], xi[:], ALU.add, [rb[sl], xb], [rb[sl]])
            S.dma("sp", xdst[tt * 128:(tt + 1) * 128, :], res[sl][:], [rb[sl]], [], sem="xout%d" % sl)

        st_mm(0)
        st_mm(1)
        for tt in range(NT):
            if tt + 2 < NT:
                st_mm(tt + 2)
            st_post(tt)
            if PA is not None and tt > 0:
                self.phase_a_rest1(PA, tt - 1)
            st_post2(tt)
            if PA is not None:
                self.phase_a_sq(PA, tt, res[tt % 3], rb[tt % 3])
                if tt > 0:
                    self.phase_a_rest2(PA, tt - 1, res[(tt - 1) % 3], rb[(tt - 1) % 3])
        if PA is not None:
            self.phase_a_rest(PA, NT - 1, res[(NT - 1) % 3], rb[(NT - 1) % 3])

    def attn_alloc(self, kind):
        S = self.S
        A = type("A", (), {})()
        A.kind = kind
        A.Vaug = S.sb("Vaug", [128, NT * 8 * 65], BF16)
        A.Vaug4 = A.Vaug[:].rearrange("p (t h d) -> p t h d", t=NT, h=8)
        A.V_b = Buf("Vaug")
        A.sZ = S.sb("sZ", [128, NT * 512], BF16)
        A.sZ3 = A.sZ[:].rearrange("p (t n) -> p t n", t=NT)
        A.sZ_b = Buf("sZ")
        A.og = [S.sb("og%d" % i, [128, NT * 128], BF16) for i in range(2)]
        A.og_b = [Buf("og%d" % i) for i in range(2)]
        A.QA = [S.sb("QA%d" % i, [128, L], BF16) for i in range(2)]
        A.KA = [S.sb("KA%d" % i, [128, L], BF16) for i in range(2)]
        A.QA_b = [Buf("QA%d" % i) for i in range(2)]
        A.KA_b = [Buf("KA%d" % i) for i in range(2)]
        A.Taug = S.sb("Taug", [128, NT * 128], BF16)
        A.Taug3 = A.Taug[:].rearrange("p (t n) -> p t n", t=NT)
        A.Taug_b = Buf("Taug")
        A.PT = [S.sb("PT%d" % i, [128, 512], BF16) for i in range(3)]
        A.PT_b = [Buf("PT%d" % i) for i in range(3)]
        A.rec = S.sb("rec", [128, 8], F32)
        A.rec_b = Buf("rec")
        A.pt_i = 0
        A.st_i = 0
        A.o_i = 0
        S.memset("pool", A.Vaug[:], 1.0, [A.V_b])
        S.memset("pool", A.Taug[:], 0.0, [A.Taug_b])
        for i in range(2):
            S.memset("pool", A.QA[i][:], 0.0, [A.QA_b[i]])
            S.memset("pool", A.KA[i][:], 0.0, [A.KA_b[i]])
        if kind == "fox":
            for i in range(2):
                S.memset("pool", A.QA[i][96:99, :], 1.0, [A.QA_b[i]])
                S.memset("pool", A.KA[i][64:67, :], 1.0, [A.KA_b[i]])
        else:
            for n in range(8):
                S.memset("pool", A.Taug3[:, 2 * n:2 * n + 2, 64 + n:65 + n], 1.0, [A.Taug_b])
            for half in range(2):
                pbi = 6 + half
                pv = self.pb[pbi][:].bitcast(BF16)
                for t8 in range(8):
                    tt = half * 8 + t8
                    S.tr(pv[0:72, t8 * 128:(t8 + 1) * 128], A.Taug3[:, tt, 0:72], self.ident[:], [A.Taug_b, self.cst_b],
                         [self.pb_b[pbi]])
                for i in range(2):
                    S.copy("dve", A.KA[i][64:72, half * 1024:(half + 1) * 1024], pv[64:72, :], [self.pb_b[pbi]], [A.KA_b[i]])
            S.memset("pool", A.Taug[:], 0.0, [A.Taug_b])
            A.gt = S.sb("gt", [128, 64], F32)
            A.gt3 = A.gt[:].rearrange("p (t n) -> p t n", t=8)
            A.mx = S.sb("mx", [128, 64], F32)
            A.mx3 = A.mx[:].rearrange("p (t n) -> p t n", t=8)
            A.selb = S.sb("selb", [128, 64], F32)
            A.selb3 = A.selb[:].rearrange("p (t n) -> p t n", t=8)
            A.kb_f = S.sb("kb_f", [64, 8], F32)
            A.kbT = S.sb("kbT", [64, 8], BF16)
            A.g_b = Buf("gate")
            S.memset("pool", A.gt[:], -BIG, [A.g_b])
        return A

    def attn_group(self, A, l, g, zcol, qcol, kcol, vcol, ychunk0, fox=None):
        S = self.S
        hT3 = self.hT3
        Wv, Wv_b = self.load_w(l, vcol + g * 512, 512)
        Wz, Wz_b = self.load_w(l, zcol + g * 512, 512)
        for tt in range(NT):
            b0, b1 = 4 + 2 * (tt % 2), 5 + 2 * (tt % 2)
            for kc in range(8):
                S.mm(self.pb[b0][:], hT3[:, kc, tt * 128:(tt + 1) * 128], Wv[:, kc, :], kc == 0, kc == 7,
                     [self.hT_b[tt], Wv_b], [self.pb_b[b0]])
            S.copy("dve", A.Vaug4[:, tt, :, 0:64], self.pb[b0][:].rearrange("p (h d) -> p h d", h=8), [self.pb_b[b0]], [A.V_b])
            for kc in range(8):
                S.mm(self.pb[b1][:], hT3[:, kc, tt * 128:(tt + 1) * 128], Wz[:, kc, :], kc == 0, kc == 7,
                     [self.hT_b[tt], Wz_b], [self.pb_b[b1]])
            S.act(A.sZ3[:, tt, :], self.pb[b1][:], AF.Silu, [self.pb_b[b1]], [A.sZ_b])
        Wq, Wq_b = self.load_w(l, qcol + g * 512, 512)
        Wk, Wk_b = self.load_w(l, kcol + g * 512, 512)
        for hh in range(8):
            h = g * 8 + hh
            sl = h % 2
            QA, KA, QA_b, KA_b = A.QA[sl], A.KA[sl], A.QA_b[sl], A.KA_b[sl]
            for tb in range(4):
                hr = [self.hT_b[4 * tb + i] for i in range(4)]
                for kc in range(8):
                    S.mm(self.pb[4][0:64, :], Wq[:, kc, hh * 64:(hh + 1) * 64], hT3[:, kc, tb * 512:(tb + 1) * 512],
                         kc == 0, kc == 7, hr + [Wq_b], [self.pb_b[4]])
                S.act(QA[0:64, tb * 512:(tb + 1) * 512], self.pb[4][0:64, :], AF.Copy, [self.pb_b[4]], [QA_b], scale=0.125)
                for kc in range(8):
                    S.mm(self.pb[5][0:64, :], Wk[:, kc, hh * 64:(hh + 1) * 64], hT3[:, kc, tb * 512:(tb + 1) * 512],
                         kc == 0, kc == 7, hr + [Wk_b], [self.pb_b[5]])
                S.copy("dve", KA[0:64, tb * 512:(tb + 1) * 512], self.pb[5][0:64, :], [self.pb_b[5]], [KA_b])
            if A.kind == "fox":
                self.fox_aug(A, fox, h, QA, KA, QA_b, KA_b)
            else:
                self.moba_aug(A, QA, KA, QA_b, KA_b)
            self.attn_core(A, hh, QA, KA, QA_b, KA_b)
            if hh % 2 == 1:
                pr = hh // 2
                osl = pr % 2
                ch = ychunk0 + pr
                for half in range(2):
                    pbi = 6 + half
                    pv = self.pb[pbi][:].bitcast(BF16)
                    for t8 in range(8):
                        tt = half * 8 + t8
                        S.tr(pv[:, t8 * 128:(t8 + 1) * 128], A.og[osl][:, tt * 128:(tt + 1) * 128], self.ident[:],
                             [A.og_b[osl], self.cst_b], [self.pb_b[pbi]])
                    S.copy("dve", self.yT3[:, ch, half * 1024:(half + 1) * 1024], pv, [self.pb_b[pbi]], [self.yT_b[ch][half]])

    def attn_core(self, A, hh, QA, KA, QA_b, KA_b):
        S = self.S
        osl = (hh // 2) % 2
        og3 = A.og[osl][:].rearrange("p (t n) -> p t n", t=NT)
        for qc in range(4):
            ob = 2 + (A.o_i % 2)
            A.o_i += 1
            first = True
            O3 = self.pb[ob][:, 0:4 * 65].rearrange("p (j d) -> p j d", j=4)
            for kt in range(4 * qc + 4):
                j0 = max(kt, 4 * qc)
                ncols = (4 * qc + 4 - j0) * 128
                sb = A.st_i % 2
                A.st_i += 1
                diag = kt >= 4 * qc
                S.mm(self.pb[sb][:, 0:ncols], KA[:, kt * 128:(kt + 1) * 128], QA[:, j0 * 128:(4 * qc + 4) * 128],
                     True, not diag, [KA_b, QA_b], [self.pb_b[sb]])
                if diag:
                    S.mm(self.pb[sb][:, 0:128], self.ident[:], self.causT[:], False, True, [self.cst_b], [self.pb_b[sb]])
                pi = A.pt_i % 3
                A.pt_i += 1
                S.act(A.PT[pi][:, 0:ncols], self.pb[sb][:, 0:ncols], AF.Exp, [self.pb_b[sb]], [A.PT_b[pi]])
                for j in range(j0, 4 * qc + 4):
                    c0 = (j - j0) * 128
                    jj = j - 4 * qc
                    S.mm(O3[:, jj, :], A.PT[pi][:, c0:c0 + 128], A.Vaug4[:, kt, hh, :], first, kt == j,
                         [A.PT_b[pi], A.V_b], [self.pb_b[ob]], skip=True)
                    first = False
            S.op("dve", lambda e, O3=O3: e.reciprocal(out=A.rec[:, 0:4], in_=O3[:, :, 64]), [self.pb_b[ob]], [A.rec_b])
            for jj in range(4):
                tt = 4 * qc + jj
                S.stt(og3[:, tt, (hh % 2) * 64:(hh % 2) * 64 + 64], O3[:, jj, 0:64], A.rec[:, jj:jj + 1],
                      A.sZ3[:, tt, hh * 64:(hh + 1) * 64], ALU.mult, ALU.mult, [self.pb_b[ob], A.rec_b, A.sZ_b], [A.og_b[osl]])

    def moba_aug(self, A, QA, KA, QA_b, KA_b):
        S = self.S
        S.op("dve", lambda e: e.tensor_reduce(out=A.kb_f[:], in_=KA[0:64, :].rearrange("p (n t) -> p n t", n=8),
                                              axis=AX.X, op=ALU.add), [KA_b], [A.g_b])
        S.ts("dve", A.kbT[:], A.kb_f[:], 1.0 / 256.0, None, ALU.mult, None, [A.g_b], [A.g_b])
        for i in range(8):
            tt = 8 + i
            S.mm(self.pb[7][:, i * 8:(i + 1) * 8], QA[0:64, tt * 128:(tt + 1) * 128], A.kbT[:], True, True,
                 [QA_b, A.g_b], [self.pb_b[7]])
        g3 = self.pb[7][:, 0:64].rearrange("p (t n) -> p t n", t=8)
        for bp in range(4):
            qb = 4 + bp
            S.copy("dve", A.gt3[:, 2 * bp:2 * bp + 2, 0:qb], g3[:, 2 * bp:2 * bp + 2, 0:qb], [self.pb_b[7]], [A.g_b])
        for i in range(8):
            S.op("dve", lambda e, i=i: e.max(out=A.mx3[:, i, :], in_=A.gt3[:, i, :]), [A.g_b], [A.g_b])
        S.tt("dve", A.selb3, A.gt3, A.mx3[:, :, 2:3].to_broadcast([128, 8, 8]), ALU.is_ge, [A.g_b], [A.g_b])
        S.ts("dve", A.selb[:], A.selb[:], -1.0, BIG, ALU.add, ALU.mult, [A.g_b], [A.g_b])
        for bp in range(4):
            qb = 4 + bp
            S.copy("dve", A.Taug3[:, 8 + 2 * bp:10 + 2 * bp, 64:64 + qb], A.selb3[:, 2 * bp:2 * bp + 2, 0:qb], [A.g_b], [A.Taug_b])
        pv = self.pb[7][:].bitcast(BF16)
        for i in range(8):
            tt = 8 + i
            S.tr(pv[0:72, i * 128:(i + 1) * 128], A.Taug3[:, tt, 0:72], self.ident[:], [A.Taug_b, self.cst_b], [self.pb_b[7]])
        S.copy("dve", QA[64:72, 1024:2048], pv[64:72, :], [self.pb_b[7]], [QA_b])

    def fox_prep(self, l):
        S = self.S
        Fx = type("F", (), {})()
        Fx.P = S.sb("Fp", [128, NT * 24 * 3], BF16)
        Fx.N = S.sb("Fn", [128, NT * 24 * 3], BF16)
        Fx.P4 = Fx.P[:].rearrange("p (t h k) -> p t h k", t=NT, h=24)
        Fx.N4 = Fx.N[:].rearrange("p (t h k) -> p t h k", t=NT, h=24)
        Fx.b = Buf("Fpieces")
        with ExitStack() as st:
            S.st, old = st, S.st
            lf = S.sb("lf", [128, NT * 24], F32)
            t1 = S.sb("lf_t1", [128, NT * 24], F32)
            t2 = S.sb("lf_t2", [128, NT * 24], F32)
            fb = S.sb("fb", [128, 24], F32)
            hb = S.sb("lf_hb", [128, NT * 24], BF16)
            b = Buf("lf")
            S.dma("sp", fb[:], self.fgate_b.to_broadcast([128, 24]), [], [b], sem="misc")
            Wf, Wf_b = self.load_w(l, O_F, 24)
            for tt in range(NT):
                for kc in range(8):
                    S.mm(self.pb[7][:, tt * 24:(tt + 1) * 24], self.hT3[:, kc, tt * 128:(tt + 1) * 128], Wf[:, kc, :],
                         kc == 0, kc == 7, [self.hT_b[tt], Wf_b], [self.pb_b[7]])
            lf3 = lf[:].rearrange("p (t h) -> p t h", t=NT)
            S.tt("dve", lf3, self.pb[7][:, 0:NT * 24].rearrange("p (t h) -> p t h", t=NT),
                 fb[:].unsqueeze(1).to_broadcast([128, NT, 24]), ALU.add, [self.pb_b[7], b], [b])
            self.dump("fraw", lf[:], [b])
            S.ts("dve", t2[:], lf[:], -1.0, None, ALU.mult, None, [b], [b])
            S.tt("dve", t1[:], t2[:], lf[:], ALU.max, [b], [b])
            S.act(t1[:], t1[:], AF.Exp, [b], [b], scale=-1.0)
            S.act(t1[:], t1[:], AF.Ln, [b], [b], bias=1.0)
            S.ts("dve", t2[:], lf[:], 0.0, None, ALU.min, None, [b], [b])
            S.tt("dve", lf[:], t2[:], t1[:], ALU.subtract, [b], [b])
            for tt in range(NT):
                for t0 in range(tt + 1):
                    lhs = self.T_f[:] if t0 == tt else self.ones_f[:]
                    S.mm(self.pb[6][:, tt * 24:(tt + 1) * 24], lhs, lf[:, t0 * 24:(t0 + 1) * 24], t0 == 0, t0 == tt,
                         [b, self.cst_b], [self.pb_b[6]])
            self.dump("logf", lf[:], [b])
            S.copy("dve", lf[:], self.pb[6][:, 0:NT * 24], [self.pb_b[6]], [b])
            self.dump("Fcum", lf[:], [b])
            hb3 = hb[:].rearrange("p (t h) -> p t h", t=NT)
            srcs = [lf, t2, lf]
            for k in range(3):
                cur = srcs[k]
                S.copy("dve", hb[:], cur[:], [b], [b])
                S.copy("dve", Fx.P4[:, :, :, k], hb3, [b], [Fx.b])
                S.ts("dve", Fx.N4[:, :, :, k], hb3, -1.0, None, ALU.mult, None, [b], [Fx.b])
                if k < 2:
                    S.copy("dve", t1[:], hb[:], [b], [b])
                    dst = t2 if k == 0 else lf
                    S.tt("dve", dst[:], cur[:], t1[:], ALU.subtract, [b], [b])
            S.st = old
            S.barrier()
        return Fx

    def fox_aug(self, A, Fx, h, QA, KA, QA_b, KA_b):
        S = self.S
        S.copy("pool", A.Taug3[:, :, 64:67], Fx.P4[:, :, h, :], [Fx.b], [A.Taug_b])
        S.copy("pool", A.Taug3[:, :, 96:99], Fx.N4[:, :, h, :], [Fx.b], [A.Taug_b])
        for half in range(2):
            pbi = 6 + half
            pv = self.pb[pbi][:].bitcast(BF16)
            for t8 in range(8):
                tt = half * 8 + t8
                S.tr(pv[0:99, t8 * 128:(t8 + 1) * 128], A.Taug3[:, tt, 0:99], self.ident[:], [A.Taug_b, self.cst_b],
                     [self.pb_b[pbi]])
            S.copy("dve", QA[64:67, half * 1024:(half + 1) * 1024], pv[64:67, :], [self.pb_b[pbi]], [QA_b])
            S.copy("dve", KA[96:99, half * 1024:(half + 1) * 1024], pv[96:99, :], [self.pb_b[pbi]], [KA_b])

    def zero_chunks(self, chunks):
        for ch in chunks:
            for half in range(2):
                self.S.memset("pool", self.yT3[:, ch, half * 1024:(half + 1) * 1024], 0.0, [self.yT_b[ch][half]])

    def phase_a_alloc(self, l):
        S = self.S
        P = type("PA", (), {})()
        P.mod, P.mod_b = self.load_modbc(l, 0, 2)
        P.junk = S.sb("pa_junk", [128, D], BF16)
        P.tmpf = [S.sb("pa_tmpf%d" % i, [128, D], F32) for i in range(2)]
        P.hb = [S.sb("pa_hb%d" % i, [128, D], BF16) for i in range(2)]
        P.jb = Buf("junk")
        P.tb = [Buf("tmpf%d" % i) for i in range(2)]
        P.hbb = [Buf("hb%d" % i) for i in range(2)]
        return P

    def phase_a_sq(self, P, tt, xi, xb):
        sl = tt % 2
        sb_ = self.stat_bs[sl]
        self.S.act(P.junk[:], xi[:], AF.Square, [xb], [P.jb, sb_], accum_out=self.scol(sl, 0))

    def phase_a_rest(self, P, tt, xi, xb):
        self.phase_a_rest1(P, tt)
        self.phase_a_rest2(P, tt, xi, xb)

    def phase_a_rest1(self, P, tt):
        sl = tt % 2
        self.rstd_from_ss(self.scol(sl, 0), D, [], slot=sl)

    def phase_a_rest2(self, P, tt, xi, xb):
        S = self.S
        sl = tt % 2
        sb_ = self.stat_bs[sl]
        rstd = self.scol(sl, 10)
        S.stt(P.tmpf[sl][:], xi[:], rstd, P.mod[:, D:2 * D], ALU.mult, ALU.mult, [xb, sb_, P.mod_b], [P.tb[sl]])
        S.tt("dve", P.hb[sl][:], P.tmpf[sl][:], P.mod[:, 0:D], ALU.add, [P.tb[sl], P.mod_b], [P.hbb[sl]])
        pbi = 6 + (tt % 2)
        pv = self.pb[pbi][:].bitcast(BF16)
        for fc in range(8):
            S.tr(pv[:, fc * 128:(fc + 1) * 128], P.hb[sl][:, fc * 128:(fc + 1) * 128], self.ident[:],
                 [P.hbb[sl], self.cst_b], [self.pb_b[pbi]])
        S.copy("act", self.hT3[:, :, tt * 128:(tt + 1) * 128], pv.rearrange("p (c t) -> p c t", c=8),
               [self.pb_b[pbi]], [self.hT_b[tt]])

    def run_phase_a(self, l, xsrc):
        S = self.S
        with ExitStack() as st:
            S.st, old = st, S.st
            P = self.phase_a_alloc(l)
            xin = [S.sb("xin%d" % i, [128, D], F32) for i in range(3)]
            xb3 = [Buf("xina%d" % i) for i in range(3)]

            def ld3(tt):
                sl = tt % 3
                S.dma("sp", xin[sl][:], xsrc[tt * 128:(tt + 1) * 128, :], [], [xb3[sl]], sem="xina%d" % sl)
            ld3(0)
            ld3(1)
            self.phase_a_sq(P, 0, xin[0], xb3[0])
            for tt in range(NT):
                if tt + 2 < NT:
                    ld3(tt + 2)
                if tt + 1 < NT:
                    self.phase_a_sq(P, tt + 1, xin[(tt + 1) % 3], xb3[(tt + 1) % 3])
                self.phase_a_rest(P, tt, xin[tt % 3], xb3[tt % 3])
            S.st = old
            S.barrier()
        S.barrier()

    def run_phase_c(self, l, xsrc, xdst, fuse_next=False):
        S = self.S
        S.barrier()
        with ExitStack() as st:
            S.st, old = st, S.st
            if fuse_next:
                wt = S.sb("wout", [128, 16 * D], BF16)
                wout = wt[:].rearrange("p (c n) -> p c n", c=16)
            else:
                wout = self.hT[:].rearrange("p (c n) -> p c n", c=16)
            wout_b = Buf("wout")
            res = [S.sb("pc_res%d" % i, [128, D], F32) for i in range(3)]
            self.xin = [S.sb("xin%d" % i, [128, D], F32) for i in range(3)]
            gp, gp_b = self.load_modbc(l, 2, 3)
            PA = self.phase_a_alloc(l + 1) if fuse_next else None
            self.phase_c(l, xsrc, xdst, wout, wout_b, res, gp, gp_b, PA)
            S.st = old
            S.barrier()
        S.barrier()

    def layer0(self):
        S = self.S
        self.run_phase_a(0, self.x)
        with ExitStack() as st:
            S.st, old = st, S.st
            if self.skip_ssd:
                self.zero_chunks(range(0, 8))
            else:
                with ExitStack() as st2:
                    S.st = st2
                    self.ssd_layer(0)
                    S.st = st
                S.barrier()
            A = self.attn_alloc("moba")
            for g in range(2):
                self.attn_group(A, 0, g, E_ZB, E_Q, E_K, E_V, 8 + 4 * g)
            S.st = old
            S.barrier()
        self.run_phase_c(0, self.x, self.x1, fuse_next=self.fuse_ca)

    def layer1(self):
        S = self.S
        if not self.fuse_ca:
            self.run_phase_a(1, self.x1)
        with ExitStack() as st:
            S.st, old = st, S.st
            if self.skip_s5:
                self.zero_chunks(range(12, 16))
            else:
                with ExitStack() as st2:
                    S.st = st2
                    self.s5_layer(1)
                    S.st = st
                S.barrier()
            Fx = self.fox_prep(1)
            A = self.attn_alloc("fox")
            for g in range(3):
                self.attn_group(A, 1, g, O_ZC, O_Q, O_K, O_V, 4 * g, fox=Fx)
            self.dump("Fp", Fx.P[:], [Fx.b])
            S.st = old
            S.barrier()
        self.run_phase_c(1, self.x1, self.out)

    def dump(self, name, ap, r):
        if name in self.dbg_out:
            self.S.dma("sp", self.dbg_out[name], ap, r, [], sem="dbg")

    def build(self, stop_after=None, mode="full"):
        self.fuse_ca = (mode == "full")
        self.declare()
        self.alloc_persistent()
        self.consts()
        self.defer_ada1 = (mode == "full") and not self.skip_ssd
        self.adaln([0] if self.defer_ada1 else [0, 1])
        if mode == "l1":
            self.x1 = self.x
        if mode == "s5":
            pass
        if mode == "l0":
            self.x1 = self.out
        if mode not in ("l1", "s5"):
            self.layer0()
            self.dump("yT0", self.yT[:], [b for bb in self.yT_b for b in bb])
        if mode == "s5":
            self.x1 = self.x
            self.run_phase_a(1, self.x1)
            with ExitStack() as st2:
                self.S.st, old = st2, self.S.st
                self.s5_layer(1)
                self.S.st = old
            self.S.barrier()
            self.dump("yT1", self.yT[:], [b for bb in self.yT_b for b in bb])
        elif mode != "l0":
            self.layer1()
            self.dump("yT1", self.yT[:], [b for bb in self.yT_b for b in bb])
        self.S.finish()
        self.st.close()
        return self.nc


def host_inputs(inputs, b):
    f = lambda a: np.ascontiguousarray(np.asarray(a, dtype=np.float32))
    m = {
        "x": f(inputs["x"][b]),
        "c_t": f(np.asarray(inputs["c"][b]).reshape(8, 128).T),
        "ada_w": f(inputs["ada_w"]), "ada_b": f(inputs["ada_b"]),
        "pre_g": f(inputs["pre_g"]), "post_g": f(inputs["post_g"]),
        "even_in_w": f(inputs["even_in_w"][0]), "odd_in_w": f(inputs["odd_in_w"][0]),
        "even_out_w": f(inputs["even_out_w"][0]), "odd_out_w": f(inputs["odd_out_w"][0]),
        "odd_fgate_b": f(inputs["odd_fgate_b"]).reshape(1, 24),
    }
    cw = f(inputs["even_conv_w"][0])
    cwl = cw.T.reshape(12, 128, 4).transpose(1, 0, 2).reshape(128, 48)
    cbl = f(inputs["even_conv_b"][0]).reshape(12, 128).T
    m["conv_p"] = f(np.concatenate([cwl, cbl], axis=1))
    m["ssd_rows"] = f(np.concatenate([inputs["even_dt_bias"][0], inputs["even_a_log"][0], inputs["even_d_skip"][0]])).reshape(1, 48)
    m["even_norm_g"] = f(inputs["even_norm_g"][0]).reshape(1, 1024)
    lre = f(inputs["odd_lam_re"][0]).reshape(16, 128).T
    lim = f(inputs["odd_lam_im"][0]).reshape(16, 128).T
    ldt = np.repeat(f(inputs["odd_log_dt"][0]), 64).reshape(16, 128).T
    m["s5_state"] = f(np.concatenate([lre, lim, ldt], axis=1))
    m["s5_vec"] = f(np.concatenate([f(inputs["odd_d_skip"][0]).reshape(4, 128).T, f(inputs["odd_glu_b"][0]).reshape(4, 128).T], axis=1))
    br = f(inputs["odd_b_re"][0]).reshape(16, 128, 16).transpose(1, 0, 2).reshape(128, 256)
    bi = f(inputs["odd_b_im"][0]).reshape(16, 128, 16).transpose(1, 0, 2).reshape(128, 256)
    m["s5_B"] = f(np.concatenate([br, bi], axis=1))
    cr = f(inputs["odd_c_re"][0]).reshape(16, 2, 16, 64).transpose(1, 3, 0, 2).reshape(128, 256)
    ci = f(inputs["odd_c_im"][0]).reshape(16, 2, 16, 64).transpose(1, 3, 0, 2).reshape(128, 256)
    m["s5_C"] = f(np.concatenate([cr, ci], axis=1))
    m["odd_glu_w"] = f(inputs["odd_glu_w"][0])
    return m


def kernel(**inputs):
    prog = Prog()
    nc = prog.build()
    in_maps = [host_inputs(inputs, b) for b in range(8)]
    res = run_bass_kernel_spmd(nc, in_maps, core_ids=list(range(8)))
    return np.stack([np.asarray(res.results[b]["out"], dtype=np.float32) for b in range(8)], axis=0)


def ssd_layer(self, l=0):
    S = self.S
    hT3 = self.hT3
    xsT = self.yT[:, 8 * L:16 * L].rearrange("p (t n) -> p t n", t=NT)
    xs_b = Buf("xsT")
    Bt = S.sb("Bt", [128, NT * 256], BF16); Bt3 = Bt[:].rearrange("p (t n) -> p t n", t=NT)
    BT = S.sb("BT", [128, 2 * L], BF16); BT3 = BT[:].rearrange("p (g t) -> p g t", g=2)
    CT = S.sb("CT", [128, 2 * L], BF16); CT3 = CT[:].rearrange("p (g t) -> p g t", g=2)
    bc_b = Buf("BC")
    dt = S.sb("dt", [128, NT * 16], F32); dt3 = dt[:].rearrange("p (t h) -> p t h", t=NT)
    adt = S.sb("adt", [128, NT * 16], F32); adt3 = adt[:].rearrange("p (t h) -> p t h", t=NT)
    dt_b = Buf("dt")
    prm = S.sb("ssd_prm", [128, 12 * 4 + 12 + 16 * 3], F32)
    prm_b = Buf("ssd_prm")
    S.dma("sp", prm[:, 0:60], self.conv_p, [], [prm_b], sem="misc")
    S.dma("sp", prm[:, 60:108], self.ssd_rows.to_broadcast([128, 48]), [], [prm_b], sem="misc")
    cw = lambda ch, k: prm[:, ch * 4 + k:ch * 4 + k + 1]
    cb = lambda ch: prm[:, 48 + ch:49 + ch]
    dtb, aneg, Dh = prm[:, 60:76], prm[:, 76:92], prm[:, 92:108]
    S.act(aneg, aneg, AF.Exp, [prm_b], [prm_b])
    S.ts("dve", aneg, aneg, -1.0, None, ALU.mult, None, [prm_b], [prm_b])
    Wd, Wd_b = self.load_w(l, E_DT, 16)
    for tt in range(NT):
        for kc in range(8):
            S.mm(self.pb[7][:, tt * 16:(tt + 1) * 16], hT3[:, kc, tt * 128:(tt + 1) * 128], Wd[:, kc, :], kc == 0, kc == 7,
                 [self.hT_b[tt], Wd_b], [self.pb_b[7]])
    S.tt("dve", dt3, self.pb[7][:, 0:256].rearrange("p (t h) -> p t h", t=NT), dtb.unsqueeze(1).to_broadcast([128, NT, 16]),
         ALU.add, [self.pb_b[7], prm_b], [dt_b])
    S.ts("dve", adt[:], dt[:], -1.0, None, ALU.mult, None, [dt_b], [dt_b])
    S.tt("dve", adt[:], adt[:], dt[:], ALU.max, [dt_b], [dt_b])
    S.act(adt[:], adt[:], AF.Exp, [dt_b], [dt_b], scale=-1.0)
    S.act(adt[:], adt[:], AF.Ln, [dt_b], [dt_b], bias=1.0)
    S.ts("dve", dt[:], dt[:], 0.0, None, ALU.max, None, [dt_b], [dt_b])
    S.tt("dve", dt[:], dt[:], adt[:], ALU.add, [dt_b], [dt_b])
    S.tt("dve", adt3, dt3, aneg.unsqueeze(1).to_broadcast([128, NT, 16]), ALU.mult, [dt_b, prm_b], [dt_b])
    self.dump("dt", dt[:], [dt_b])
    self.dump("adt", adt[:], [dt_b])
    with ExitStack() as st:
        S.st, old = st, S.st
        U = [S.sb("convU%d" % i, [128, 3 + 1024], F32) for i in range(2)]
        U_b = [Buf("U%d" % i) for i in range(2)]
        acc = S.sb("convacc", [128, 1024], F32); acc_b = Buf("acc")
        co = [S.sb("convo%d" % i, [128, 1024], BF16) for i in range(2)]
        co_b = [Buf("co%d" % i) for i in range(2)]
        Ws = [self.load_w(l, E_XS + wt * 512, 512) for wt in range(3)]
        work = [(wt, c4, half) for wt in range(3) for c4 in range(4) for half in range(2)]

        def st_mm(k):
            wt, c4, half = work[k]
            W, W_b = Ws[wt]
            u, ub = U[k % 2], U_b[k % 2]
            pu, pub = U[(k + 1) % 2], U_b[(k + 1) % 2]
            if half == 0:
                S.memset("pool", u[:, 0:3], 0.0, [ub])
            else:
                S.copy("pool", u[:, 0:3], pu[:, 1024:1027], [pub], [ub])
            for t2 in range(2):
                tb = half * 2 + t2
                pbi = 4 + (tb % 2)
                for kc in range(8):
                    S.mm(self.pb[pbi][:], W[:, kc, c4 * 128:(c4 + 1) * 128], hT3[:, kc, tb * 512:(tb + 1) * 512],
                         kc == 0, kc == 7, [self.hT_b[4 * tb + i] for i in range(4)] + [W_b], [self.pb_b[pbi]])
                S.copy("act", u[:, 3 + t2 * 512:3 + (t2 + 1) * 512], self.pb[pbi][:], [self.pb_b[pbi]], [ub])

        def st_conv(k):
            wt, c4, half = work[k]
            ch = wt * 4 + c4
            u, ub = U[k % 2], U_b[k % 2]
            S.ts("pool", acc[:], u[:, 0:1024], cw(ch, 0), cb(ch), ALU.mult, ALU.add, [ub, prm_b], [acc_b])
            for kk in range(1, 4):
                S.stt(acc[:], u[:, kk:kk + 1024], cw(ch, kk), acc[:], ALU.mult, ALU.add, [ub, prm_b, acc_b], [acc_b])
            if ch < 8:
                S.act(co[k % 2][:], acc[:], AF.Silu, [acc_b], [co_b[k % 2]])
            elif ch < 10:
                S.act(BT3[:, ch - 8, half * 1024:(half + 1) * 1024], acc[:], AF.Silu, [acc_b], [bc_b])
            else:
                S.act(CT3[:, ch - 10, half * 1024:(half + 1) * 1024], acc[:], AF.Silu, [acc_b], [bc_b])

        def st_tail(k):
            wt, c4, half = work[k]
            ch = wt * 4 + c4
            if ch >= 10:
                return
            pbi = 6 + (k % 2)
            pv = self.pb[pbi][:].bitcast(BF16)
            if ch < 8:
                o, ob = co[k % 2], co_b[k % 2]
                for t8 in range(8):
                    S.tr(pv[:, t8 * 128:(t8 + 1) * 128], o[:, t8 * 128:(t8 + 1) * 128], self.ident[:], [ob, self.cst_b],
                         [self.pb_b[pbi]])
                S.copy("dve", xsT[:, half * 8:(half + 1) * 8, ch * 128:(ch + 1) * 128],
                       pv.rearrange("p (t n) -> p t n", t=8), [self.pb_b[pbi]], [xs_b])
            else:
                g = ch - 8
                for t8 in range(8):
                    tt = half * 8 + t8
                    S.tr(pv[:, t8 * 128:(t8 + 1) * 128], BT3[:, g, tt * 128:(tt + 1) * 128], self.ident[:],
                         [bc_b, self.cst_b], [self.pb_b[pbi]])
                S.copy("dve", Bt3[:, half * 8:(half + 1) * 8, g * 128:(g + 1) * 128],
                       pv.rearrange("p (t n) -> p t n", t=8), [self.pb_b[pbi]], [bc_b])

        nw = len(work)
        ada = self.adaln_items(1, 1) if self.defer_ada1 else []
        st_mm(0)
        for k in range(nw):
            if k + 1 < nw:
                st_mm(k + 1)
            st_conv(k)
            if k > 0:
                st_tail(k - 1)
            if ada and k % 3 == 1:
                ada.pop(0)()
        st_tail(nw - 1)
        for it in ada:
            it()
        S.st = old
        S.barrier()
    self.dump("xsT", self.yT[:, 8 * L:16 * L], [xs_b])
    self.dump("BT", BT[:], [bc_b])
    self.dump("CT", CT[:], [bc_b])
    self.dump("Bt", Bt[:], [bc_b])
    with ExitStack() as st:
        S.st, old = st, S.st
        Wz0, Wz0_b = self.load_w(l, E_ZA, 512)
        Wz1, Wz1_b = self.load_w(l, E_ZA + 512, 512)
        ng = S.sb("ngbc", [128, 1024], F32); ng_b = Buf("ng")
        S.dma("sp", ng[:], self.even_norm_g.to_broadcast([128, 1024]), [], [ng_b], sem="misc")
        sz = S.sb("ssd_sz", [128, 1024], BF16); sz_b = Buf("sz")
        rhs_all = S.sb("rhs_all", [128, 8 * 128], F32); rhs_b = Buf("rhs_all")
        Ef = S.sb("Ef", [128, 512], F32); Ef_b = Buf("Ef")
        M = [S.sb("Mh%d" % i, [128, 512], BF16) for i in range(2)]; M_b = [Buf("M%d" % i) for i in range(2)]
        CBm = S.sb("CBm", [128, 128], F32); CBm_b = Buf("CBm")
        xdt = S.sb("xdt", [128, 1024], BF16); xdt_b = Buf("xdt")
        xds = S.sb("xds", [128, 1024], BF16); xds_b = Buf("xds")
        t1 = S.sb("ssd_t1", [128, 512], F32); t1_b = Buf("t1")
        yraw = S.sb("yraw", [128, 1024], F32); yraw_b = Buf("yraw")
        St = S.sb("Sstate", [128, 1024], F32); St_b = Buf("S")
        Sb = S.sb("Sbf", [128, 1024], BF16); Sb_b = Buf("Sb")
        tS = S.sb("tmpS", [128, 512], F32); tS_b = Buf("tS")
        ya = S.sb("ssd_ya", [128, 1024], BF16); ya_b = Buf("ya")
        sm = S.sb("ssd_small", [128, 16 * 6], F32); sm_b = Buf("ssd_small")
        acum, tot, eac, ds, dch, wds = [sm[:, i * 16:(i + 1) * 16] for i in range(6)]
        S.memset("pool", St[:], 0.0, [St_b])
        S.memset("pool", Sb[:], 0.0, [Sb_b])
        def front(c):
            k = c % 2
            tl = slice(c * 128, (c + 1) * 128)
            acum, tot, eac, ds, dch, wds = [smk[k][:, i * 16:(i + 1) * 16] for i in range(6)]
            for nb, (Wz, Wz_b) in enumerate([(Wz0, Wz0_b), (Wz1, Wz1_b)]):
                pbi = 4 + nb
                for kc in range(8):
                    S.mm(self.pb[pbi][:], hT3[:, kc, tl], Wz[:, kc, :], kc == 0, kc == 7, [self.hT_b[c], Wz_b], [self.pb_b[pbi]])
                S.act(thb[nb][:], self.pb[pbi][:], AF.Tanh, [self.pb_b[pbi]], [thb_b[nb]], scale=0.5)
                S.stt(szk[k][:, nb * 512:(nb + 1) * 512], thb[nb][:], 1.0, self.pb[pbi][:], ALU.add, ALU.mult,
                      [thb_b[nb], self.pb_b[pbi]], [szk_b[k]])
            S.mm(self.pb[7][:, 0:16], self.T_f[:], adt3[:, c, :], True, True, [dt_b, self.cst_b], [self.pb_b[7]])
            S.mm(self.pb[7][:, 16:32], self.ones_f[:], adt3[:, c, :], True, True, [dt_b, self.cst_b], [self.pb_b[7]])
            S.copy("dve", smk[k][:, 0:32], self.pb[7][:, 0:32], [self.pb_b[7]], [smk_b[k]])
            S.act(eac, acum, AF.Exp, [smk_b[k]], [smk_b[k]])
            S.tt("dve", ds, tot, acum, ALU.subtract, [smk_b[k]], [smk_b[k]])
            S.act(ds, ds, AF.Exp, [smk_b[k]], [smk_b[k]])
            S.act(dch, tot, AF.Exp, [smk_b[k]], [smk_b[k]])
            S.tt("dve", wds, ds, dt3[:, c, :], ALU.mult, [smk_b[k], dt_b], [smk_b[k]])
            xs_c = xsT[:, c, :].rearrange("p (h d) -> p h d", h=16)
            S.tt("pool", xdtk[k][:].rearrange("p (h d) -> p h d", h=16), xs_c, dt3[:, c, :].unsqueeze(2).to_broadcast([128, 16, 64]),
                 ALU.mult, [xs_b, dt_b], [xdtk_b[k]])
            S.tt("dve", xdsk[k][:].rearrange("p (h d) -> p h d", h=16), xs_c, wds.unsqueeze(2).to_broadcast([128, 16, 64]),
                 ALU.mult, [xs_b, smk_b[k]], [xdsk_b[k]])
            for g in range(2):
                S.mm(self.pb[7][:, 128:256], BT3[:, g, tl], CT3[:, g, tl], True, True, [bc_b], [self.pb_b[7]])
                S.tt("dve", CBm[:], self.pb[7][:, 128:256], self.T_f[:], ALU.mult, [self.pb_b[7], self.cst_b], [CBm_b])
                S.tt("pool", rhs_all[:].rearrange("p (h n) -> p h n", h=8), self.T_f[:].unsqueeze(1).to_broadcast([128, 8, 128]),
                     adt3[:, c, g * 8:(g + 1) * 8].unsqueeze(2).to_broadcast([128, 8, 128]), ALU.mult,
                     [dt_b, self.cst_b], [rhs_b])
                for h4 in range(2):
                    pbi = h4
                    for i in range(4):
                        hh = h4 * 4 + i
                        S.mm(self.pb[pbi][:, i * 128:(i + 1) * 128], self.U_f[:], rhs_all[:, hh * 128:(hh + 1) * 128], True, True,
                             [rhs_b, self.cst_b], [self.pb_b[pbi]])
                    S.act(Ef[:], self.pb[pbi][:], AF.Exp, [self.pb_b[pbi]], [Ef_b])
                    mi = k * 4 + g * 2 + h4
                    S.tt("dve", Mk[mi][:].rearrange("p (h n) -> p h n", h=4), Ef[:].rearrange("p (h n) -> p h n", h=4),
                         CBm[:].unsqueeze(1).to_broadcast([128, 4, 128]), ALU.mult, [Ef_b, CBm_b], [Mk_b[mi]])

        def back(c):
            k = c % 2
            tl = slice(c * 128, (c + 1) * 128)
            acum, tot, eac, ds, dch, wds = [smk[k][:, i * 16:(i + 1) * 16] for i in range(6)]
            xs_c = xsT[:, c, :].rearrange("p (h d) -> p h d", h=16)
            for g in range(2):
                gs = slice(g * 512, (g + 1) * 512)
                for hh in range(8):
                    h = g * 8 + hh
                    mi = k * 4 + g * 2 + hh // 4
                    S.mm(self.pb[2][:, hh * 64:(hh + 1) * 64], Mk[mi][:, (hh % 4) * 128:(hh % 4 + 1) * 128],
                         xdtk[k][:, h * 64:(h + 1) * 64], True, True, [Mk_b[mi], xdtk_b[k]], [self.pb_b[2]])
                if c > 0:
                    S.mm(self.pb[3][:], CT3[:, g, tl], Sb[:, gs], True, True, [bc_b, Sb_b], [self.pb_b[3]])
                    S.tt("dve", t1[:].rearrange("p (h d) -> p h d", h=8), self.pb[3][:].rearrange("p (h d) -> p h d", h=8),
                         eac[:, g * 8:(g + 1) * 8].unsqueeze(2).to_broadcast([128, 8, 64]), ALU.mult,
                         [self.pb_b[3], smk_b[k]], [t1_b])
                    S.tt("dve", yraw[:, gs], self.pb[2][:], t1[:], ALU.add, [self.pb_b[2], t1_b], [yraw_b])
                else:
                    S.copy("dve", yraw[:, gs], self.pb[2][:], [self.pb_b[2]], [yraw_b])
                S.mm(self.pb[3][:], Bt3[:, c, g * 128:(g + 1) * 128], xdsk[k][:, gs], True, True, [bc_b, xdsk_b[k]], [self.pb_b[3]])
                S.tt("pool", tS[:].rearrange("p (h d) -> p h d", h=8), St[:, gs].rearrange("p (h d) -> p h d", h=8),
                     dch[:, g * 8:(g + 1) * 8].unsqueeze(2).to_broadcast([128, 8, 64]), ALU.mult, [St_b, smk_b[k]], [tS_b])
                S.tt("dve", St[:, gs], tS[:], self.pb[3][:], ALU.add, [tS_b, self.pb_b[3]], [St_b])
                S.copy("pool", Sb[:, gs], St[:, gs], [St_b], [Sb_b])
            S.tt("dve", xD[:].rearrange("p (h d) -> p h d", h=16), xs_c, Dh.unsqueeze(2).to_broadcast([128, 16, 64]), ALU.mult,
                 [xs_b, prm_b], [xD_b])
            S.tt("dve", yraw[:], yraw[:], xD[:], ALU.add, [yraw_b, xD_b], [yraw_b])
            S.tt("pool", yraw[:], yraw[:], szk[k][:], ALU.mult, [yraw_b, szk_b[k]], [yraw_b])
            S.act(yak[k][:], yraw[:], AF.Square, [yraw_b], [yak_b[k], self.stat_bs[k]], accum_out=self.scol(k, 0))
            S.ts("dve", self.scol(k, 8), self.scol(k, 0), 1.0 / 1024, 4.0 * EPS, ALU.mult, ALU.add, [self.stat_bs[k]], [self.stat_bs[k]])
            S.tt("pool", self.scol(k, 10), self.scol(k, 8), mhalf[:], ALU.pow, [self.stat_bs[k], prm_b], [self.stat_bs[k]])
            rstd = self.scol(k, 10)
            S.stt(yak[k][:], yraw[:], rstd, ng[:], ALU.mult, ALU.mult, [yraw_b, self.stat_bs[k], ng_b], [yak_b[k]])

        def back2(c):
            k = c % 2
            tl = slice(c * 128, (c + 1) * 128)
            pbi = 6
            pv = self.pb[pbi][:].bitcast(BF16)
            for fc in range(8):
                S.tr(pv[:, fc * 128:(fc + 1) * 128], yak[k][:, fc * 128:(fc + 1) * 128], self.ident[:], [yak_b[k], self.cst_b],
                     [self.pb_b[pbi]])
            S.copy("dve", self.yT3[:, 0:8, tl], pv.rearrange("p (c t) -> p c t", c=8), [self.pb_b[pbi]],
                   [self.yT_b[ch][c // 8] for ch in range(8)])

        yak = [ya, S.sb("ssd_ya2", [128, 1024], BF16)]; yak_b = [Buf("ya0"), Buf("ya1")]
        thb = [S.sb("ssd_th%d" % i, [128, 512], BF16) for i in range(2)]; thb_b = [Buf("th0"), Buf("th1")]
        mhalf = S.sb("ssd_mhalf", [128, 1], F32)
        S.memset("pool", mhalf[:], -0.5, [prm_b])
        szk = [sz, S.sb("ssd_sz2", [128, 1024], BF16)]; szk_b = [Buf("sz0"), Buf("sz1")]
        smk = [sm, S.sb("ssd_small2", [128, 16 * 6], F32)]; smk_b = [Buf("sm0"), Buf("sm1")]
        xdtk = [xdt, S.sb("xdt2", [128, 1024], BF16)]; xdtk_b = [Buf("xdt0"), Buf("xdt1")]
        xdsk = [xds, S.sb("xds2", [128, 1024], BF16)]; xdsk_b = [Buf("xds0"), Buf("xds1")]
        Mk = M + [S.sb("Mh%d" % i, [128, 512], BF16) for i in range(2, 8)]; Mk_b = [Buf("M%d" % i) for i in range(8)]
        xD = S.sb("ssd_xD", [128, 1024], BF16); xD_b = Buf("xD")
        front(0)
        for c in range(NT):
            if c + 1 < NT:
                front(c + 1)
            if c > 0:
                back2(c - 1)
            back(c)
        back2(NT - 1)
        S.st = old
        S.barrier()
    S.barrier()


Prog.ssd_layer = ssd_layer


TWO_PI = 6.283185307179586
C1_2PI = 6.28125
C2_2PI = TWO_PI - 6.28125
MAGIC = 12582912.0


def s5_layer(self, l=1):
    S = self.S
    hT3 = self.hT3
    ydg3 = self.yT3[:, 12:16, :]
    ydg_b = Buf("ydg")
    prm = S.sb("s5_prm", [128, 48 + 8], F32); prm_b = Buf("s5prm")
    S.dma("sp", prm[:, 0:48], self.s5_state, [], [prm_b], sem="misc")
    S.dma("sp", prm[:, 48:56], self.s5_vec, [], [prm_b], sem="misc")
    lr, li, ldt = prm[:, 0:16], prm[:, 16:32], prm[:, 32:48]
    dsk, glb = prm[:, 48:52], prm[:, 52:56]
    BbT = S.sb("s5_BbT", [128, 2 * 16 * 128], BF16); BbT_b = Buf("BbT")
    BbT4 = BbT[:].rearrange("p (r j s) -> p r j s", r=2, j=16)
    CTp = S.sb("s5_CTp", [128, 2 * 16 * 128], BF16); CT_b = Buf("CTp")
    cs = S.sb("s5_cs", [128, 16 * 16], F32); cs_b = Buf("s5cs")
    _st_setup = ExitStack()
    S.st, _old_setup = _st_setup, S.st
    Bst = S.sb("s5_Bst", [128, 2 * 256], F32); Cst = S.sb("s5_Cst", [128, 2 * 256], F32)
    S.dma("sp", Bst[:], self.s5_B, [], [prm_b], sem="misc")
    S.dma("sp", Cst[:], self.s5_C, [], [prm_b], sem="misc")
    v = lambda i: cs[:, i * 16:(i + 1) * 16]
    dtv, mag, ang, cosA, sinA, ar1, ai, qr, qi, c128, s128, tA, tB, tC = [v(i) for i in range(14)]

    def reduce_inplace(X, K, r, w):
        S.ts("dve", K, X, 1.0 / TWO_PI, MAGIC, ALU.mult, ALU.add, r, w)
        S.ts("dve", K, K, -MAGIC, None, ALU.add, None, w, w)
        S.stt(X, K, -C1_2PI, X, ALU.mult, ALU.add, w, w)
        S.stt(X, K, -C2_2PI, X, ALU.mult, ALU.add, w, w)
        S.ts("dve", X, X, 3.14159, -3.14159, ALU.min, ALU.max, w, w)

    R, W = [prm_b, cs_b], [cs_b]
    S.act(dtv, ldt, AF.Exp, [prm_b], W)
    S.tt("dve", tA, lr, dtv, ALU.mult, R, W)
    S.act(mag, tA, AF.Exp, R, W)
    S.tt("dve", ang, li, dtv, ALU.mult, R, W)
    S.copy("dve", tA, ang, R, W)
    reduce_inplace(tA, tC, R, W)
    S.act(sinA, tA, AF.Sin, R, W)
    S.ts("dve", tA, ang, 1.5707963267948966, None, ALU.add, None, R, W)
    reduce_inplace(tA, tC, R, W)
    S.act(cosA, tA, AF.Sin, R, W)
    S.ts("dve", tA, ang, 128.0, None, ALU.mult, None, R, W)
    S.copy("dve", tB, tA, R, W)
    reduce_inplace(tA, tC, R, W)
    S.act(s128, tA, AF.Sin, R, W)
    S.ts("dve", tB, tB, 1.5707963267948966, None, ALU.add, None, R, W)
    reduce_inplace(tB, tC, R, W)
    S.act(c128, tB, AF.Sin, R, W)
    S.tt("dve", s128, s128, mag, ALU.mult, R, W)
    S.tt("dve", c128, c128, mag, ALU.mult, R, W)
    S.tt("dve", ar1, mag, cosA, ALU.mult, R, W)
    S.ts("dve", ar1, ar1, -1.0, None, ALU.add, None, R, W)
    S.tt("dve", ai, mag, sinA, ALU.mult, R, W)
    S.tt("dve", tA, lr, lr, ALU.mult, R, W)
    S.tt("dve", tB, li, li, ALU.mult, R, W)
    S.tt("dve", tA, tA, tB, ALU.add, R, W)
    S.op("dve", lambda e: e.reciprocal(out=tC, in_=tA), R, W)
    S.tt("dve", qr, ar1, lr, ALU.mult, R, W)
    S.tt("dve", tA, ai, li, ALU.mult, R, W)
    S.tt("dve", qr, qr, tA, ALU.add, R, W)
    S.tt("dve", qr, qr, tC, ALU.mult, R, W)
    S.tt("dve", qi, ai, lr, ALU.mult, R, W)
    S.tt("dve", tA, ar1, li, ALU.mult, R, W)
    S.tt("dve", qi, qi, tA, ALU.subtract, R, W)
    S.tt("dve", qi, qi, tC, ALU.mult, R, W)
    Bst3 = [Bst[:, i * 256:(i + 1) * 256].rearrange("p (j c) -> p j c", j=16) for i in range(2)]
    Cst3 = [Cst[:, i * 256:(i + 1) * 256].rearrange("p (j c) -> p j c", j=16) for i in range(2)]
    bb = S.sb("s5_bb", [128, 2 * 256], F32); bb_b = Buf("s5bb")
    bb3 = [bb[:, i * 256:(i + 1) * 256].rearrange("p (j c) -> p j c", j=16) for i in range(2)]
    tmp = S.sb("s5_tmp", [128, 256], F32)
    tmp3 = tmp[:].rearrange("p (j c) -> p j c", j=16)
    qrb = qr.unsqueeze(2).to_broadcast([128, 16, 16]); qib = qi.unsqueeze(2).to_broadcast([128, 16, 16])
    RB = [prm_b, cs_b, bb_b]
    S.tt("dve", bb3[0], Bst3[0], qrb, ALU.mult, RB, [bb_b])
    S.tt("dve", tmp3, Bst3[1], qib, ALU.mult, RB, [bb_b])
    S.tt("dve", bb3[0], bb3[0], tmp3, ALU.subtract, RB, [bb_b])
    S.tt("dve", bb3[1], Bst3[1], qrb, ALU.mult, RB, [bb_b])
    S.tt("dve", tmp3, Bst3[0], qib, ALU.mult, RB, [bb_b])
    S.tt("dve", bb3[1], bb3[1], tmp3, ALU.add, RB, [bb_b])
    bpad = S.sb("s5_bpad", [128, 2 * 16 * 128], BF16)
    bpad6 = bpad[:].rearrange("p (r q m2 m a c) -> p r q m2 m a c", r=2, q=4, m2=4, m=4, a=2)
    S.memset("pool", bpad[:], 0.0, [bb_b])
    for ri in range(2):
        src5 = bb[:, ri * 256:(ri + 1) * 256].rearrange("p (q m c) -> p q m c", q=4, m=4)
        for a in range(2):
            for jm in range(4):
                S.copy("dve", bpad6[a * 64:(a + 1) * 64, ri, :, jm, jm, a, :], src5[a * 64:(a + 1) * 64, :, jm, :], [bb_b], [bb_b])
    for ri in range(2):
        for j4 in range(4):
            pbi = 4 + (ri * 4 + j4) % 4
            pv = self.pb[pbi][:].bitcast(BF16)
            for jj in range(4):
                j = j4 * 4 + jj
                S.tr(pv[:, jj * 128:(jj + 1) * 128], bpad[:, (ri * 16 + j) * 128:(ri * 16 + j + 1) * 128], self.ident[:],
                     [bb_b, self.cst_b], [self.pb_b[pbi]])
            S.copy("dve", BbT[:, (ri * 16 + j4 * 4) * 128:(ri * 16 + j4 * 4 + 4) * 128], pv[:, 0:512], [self.pb_b[pbi]], [BbT_b])
    CTp5 = CTp[:].rearrange("p (r j m a c) -> p r j m a c", r=2, j=16, m=4, a=2)
    S.memset("pool", CTp[:], 0.0, [CT_b])
    for ri in range(2):
        for a in range(2):
            for jm in range(4):
                srcv = Cst3[ri][a * 64:(a + 1) * 64].rearrange("p (q m) c -> p q m c", m=4)[:, :, jm, :]
                dstv = CTp5[a * 64:(a + 1) * 64, ri].rearrange("p (q m2) m a c -> p q m2 m a c", m2=4)[:, :, jm, jm, a, :]
                if ri == 0:
                    S.copy("dve", dstv, srcv, [prm_b], [CT_b])
                else:
                    S.ts("dve", dstv, srcv, -1.0, None, ALU.mult, None, [prm_b], [CT_b])
    CTp4 = CTp[:].rearrange("p (r j n) -> p r j n", r=2, j=16)
    S.st = _old_setup
    _st_setup.close()
    S.barrier()
    bidx = S.sb("s5_bidx", [128, 128], F32)
    S.op("dve", lambda e: e.tensor_tensor_scan(out=bidx[:], data0=self.ones_f[:], data1=self.ones_f[:], initial=-1.0,
                                                op0=ALU.mult, op1=ALU.add), [self.cst_b], [cs_b])
    stop = getattr(self, "s5_stop", 9)
    if stop <= 1:
        self.zero_chunks(range(12, 16))
        S.barrier()
        return
    with ExitStack() as st:
        S.st, old = st, S.st
        uT = S.sb("s5_uT", [128, 2 * L], BF16); uT3 = uT[:].rearrange("p (c t) -> p c t", c=2); uT_b = Buf("uT")
        cosT = S.sb("s5_cosT", [128, 1024], F32); sinT = S.sb("s5_sinT", [128, 1024], F32); rhoT = S.sb("s5_rhoT", [128, 1024], F32)
        tab_b = Buf("s5tab")
        wr = S.sb("s5_wr", [128, 1024], F32); wi = S.sb("s5_wi", [128, 1024], F32)
        tf = [S.sb("s5_tf%d" % i, [128, 512], F32) for i in range(4)]; tf_b = [Buf("tf%d" % i) for i in range(4)]
        rr = [S.sb("s5_rr%d" % i, [128, 1024], F32) for i in range(2)]; rim = [S.sb("s5_ri%d" % i, [128, 1024], F32) for i in range(2)]
        rr_b = [Buf("rr%d" % i) for i in range(2)]; ri_b = [Buf("ri%d" % i) for i in range(2)]
        tbw = [S.sb("s5_tb%d" % i, [128, 1024], BF16) for i in range(4)]; tbw_b = [Buf("tb%d" % i) for i in range(4)]
        kk = rr[0]
        wr_b, wi_b = Buf("wr"), Buf("wi")
        sr = S.sb("s5_sr", [128, 1024], BF16); si = S.sb("s5_si", [128, 1024], BF16)
        yt = S.sb("s5_yt", [128, 256], F32)
        yg = S.sb("s5_yg", [128, 256], F32)
        ini = S.sb("s5_ini", [128, 80], F32)
        CCSS = S.sb("s5_ccss", [128, 32], F32)
        Dg = S.sb("s5_Dg", [128, 2 * 128], BF16)
        w_b, t_b, r_b, s_b, y_b, ini_b = Buf("w"), Buf("t"), Buf("r"), Buf("s"), Buf("yt"), Buf("ini")
        v3 = lambda t: t[:].rearrange("p (j b) -> p j b", j=8)
        for ps in range(2):
            js = slice(ps * 8, (ps + 1) * 8)
            Wu, Wu_b = self.load_w(l, O_U + ps * 256, 256)
            for c2 in range(2):
                for tb in range(4):
                    pbi = 4 + (tb % 2)
                    for kc in range(8):
                        S.mm(self.pb[pbi][:], Wu[:, kc, c2 * 128:(c2 + 1) * 128], hT3[:, kc, tb * 512:(tb + 1) * 512], kc == 0, kc == 7,
                             [self.hT_b[4 * tb + i] for i in range(4)] + [Wu_b], [self.pb_b[pbi]])
                    S.copy("act", uT3[:, c2, tb * 512:(tb + 1) * 512], self.pb[pbi][:], [self.pb_b[pbi]], [uT_b])
            TW = [tab_b]
            S.tt("dve", v3(sinT), bidx[:].unsqueeze(1).to_broadcast([128, 8, 128]), ang[:, js].unsqueeze(2).to_broadcast([128, 8, 128]),
                 ALU.mult, [cs_b], TW)
            S.ts("dve", cosT[:], sinT[:], 1.5707963267948966, None, ALU.add, None, TW, TW)
            reduce_inplace(sinT[:], kk[:], TW, TW)
            reduce_inplace(cosT[:], kk[:], TW, TW)
            S.act(sinT[:], sinT[:], AF.Sin, TW, TW)
            S.act(cosT[:], cosT[:], AF.Sin, TW, TW)
            S.copy("dve", v3(rhoT), mag[:, js].unsqueeze(2).to_broadcast([128, 8, 128]), [cs_b], TW)
            S.memset("pool", v3(rhoT)[:, :, 0:1], 0.0, TW)
            S.memset("pool", ini[:], 0.0, [ini_b])
            S.copy("dve", CCSS[:, 0:8], c128[:, js], [cs_b], [cs_b])
            S.copy("dve", CCSS[:, 8:16], c128[:, js], [cs_b], [cs_b])
            S.ts("dve", CCSS[:, 16:24], s128[:, js], -1.0, None, ALU.mult, None, [cs_b], [cs_b])
            S.copy("dve", CCSS[:, 24:32], s128[:, js], [cs_b], [cs_b])
            for c2 in range(2):
                S.ts("dve", Dg[:, c2 * 128:(c2 + 1) * 128], self.ident[:], dsk[:, 2 * ps + c2:2 * ps + c2 + 1], None, ALU.mult, None,
                     [self.cst_b, prm_b], [cs_b])
            if stop <= 2:
                continue
            def st_bu(a):
                tl = slice(a * 128, (a + 1) * 128)
                for ri in range(2):
                    for j8 in range(8):
                        j = ps * 8 + j8
                        jq = j // 4
                        pbi = 2 * ri + j8 // 4
                        S.mm(self.pb[pbi][:, (j8 % 4) * 128:(j8 % 4 + 1) * 128], BbT4[:, ri, j, :],
                             uT3[:, jq - 2 * ps, tl], True, True, [BbT_b, uT_b], [self.pb_b[pbi]])

            def st_fwd(a):
                for hf in range(2):
                    cs_ = slice(hf * 512, (hf + 1) * 512)
                    S.tt("dve", tf[0][:], self.pb[hf][:], cosT[:, cs_], ALU.mult, [self.pb_b[hf], tab_b], [tf_b[0]])
                    S.tt("dve", tf[1][:], self.pb[2 + hf][:], sinT[:, cs_], ALU.mult, [self.pb_b[2 + hf], tab_b], [tf_b[1]])
                    S.tt("dve", wr[:, cs_], tf[0][:], tf[1][:], ALU.add, [tf_b[0], tf_b[1]], [wr_b])
                    S.tt("dve", tf[2][:], self.pb[2 + hf][:], cosT[:, cs_], ALU.mult, [self.pb_b[2 + hf], tab_b], [tf_b[2]])
                    S.tt("dve", tf[3][:], self.pb[hf][:], sinT[:, cs_], ALU.mult, [self.pb_b[hf], tab_b], [tf_b[3]])
                    S.tt("dve", wi[:, cs_], tf[2][:], tf[3][:], ALU.subtract, [tf_b[2], tf_b[3]], [wi_b])

            def st_scan(a):
                k = a % 2
                if a > 0:
                    S.tt("dve", ini[:, 48:64], ini[:, 0:16], CCSS[:, 0:16], ALU.mult, [ini_b, cs_b], [ini_b])
                    S.tt("dve", ini[:, 64:80], ini[:, 16:32], CCSS[:, 16:32], ALU.mult, [ini_b, cs_b], [ini_b])
                    S.tt("dve", ini[:, 48:64], ini[:, 48:64], ini[:, 64:80], ALU.add, [ini_b], [ini_b])
                    S.tt("dve", v3(wr)[:, :, 0], v3(wr)[:, :, 0], ini[:, 48:56], ALU.add, [wr_b, ini_b], [wr_b])
                    S.tt("dve", v3(wi)[:, :, 0], v3(wi)[:, :, 0], ini[:, 56:64], ALU.add, [wi_b, ini_b], [wi_b])
                S.op("dve", lambda e: e.tensor_tensor_scan(out=rr[k][:], data0=rhoT[:], data1=wr[:], initial=0.0, op0=ALU.mult,
                                                           op1=ALU.add), [wr_b, tab_b], [rr_b[k]])
                S.op("dve", lambda e: e.tensor_tensor_scan(out=rim[k][:], data0=rhoT[:], data1=wi[:], initial=0.0, op0=ALU.mult,
                                                           op1=ALU.add), [wi_b, tab_b], [ri_b[k]])
                S.copy("pool", ini[:, 0:48].rearrange("p (a b) -> p a b", b=24)[:, :, 0:8],
                       v3(rr[k])[:, :, 127].unsqueeze(1).to_broadcast([128, 2, 8]), [rr_b[k]], [ini_b])
                S.copy("pool", ini[:, 8:24].rearrange("p (a b) -> p a b", b=8),
                       v3(rim[k])[:, :, 127].unsqueeze(1).to_broadcast([128, 2, 8]), [ri_b[k]], [ini_b])

            def st_bwdmul(a):
                k = a % 2
                S.tt("dve", tbw[0][:], rr[k][:], cosT[:], ALU.mult, [rr_b[k], tab_b], [tbw_b[0]])
                S.stt(tbw[1][:], rim[k][:], -1.0, sinT[:], ALU.mult, ALU.mult, [ri_b[k], tab_b], [tbw_b[1]])
                S.tt("dve", tbw[2][:], rr[k][:], sinT[:], ALU.mult, [rr_b[k], tab_b], [tbw_b[2]])
                S.tt("dve", tbw[3][:], rim[k][:], cosT[:], ALU.mult, [ri_b[k], tab_b], [tbw_b[3]])

            def st_out(a):
                tl = slice(a * 128, (a + 1) * 128)
                for c2 in range(2):
                    n = 0
                    for jm in range(4):
                        j8 = c2 * 4 + jm
                        for ri, ti in ((0, 0), (0, 1), (1, 2), (1, 3)):
                            S.mm(self.pb[6][:, c2 * 128:(c2 + 1) * 128], CTp4[:, ri, ps * 8 + j8, :], tbw[ti][:, j8 * 128:(j8 + 1) * 128],
                                 n == 0, False, [CT_b, tbw_b[ti]], [self.pb_b[6]])
                            n += 1
                    S.mm(self.pb[6][:, c2 * 128:(c2 + 1) * 128], Dg[:, c2 * 128:(c2 + 1) * 128], uT3[:, c2, tl], False, True,
                         [cs_b, uT_b], [self.pb_b[6]])
                x = self.pb[6][:, 0:256]
                S.act(yg[:], x, AF.Square, [self.pb_b[6]], [y_b])
                S.ts("dve", yg[:], yg[:], 0.044715, 1.0, ALU.mult, ALU.add, [y_b], [y_b])
                S.tt("dve", yg[:], yg[:], x, ALU.mult, [y_b, self.pb_b[6]], [y_b])
                S.act(yt[:], yg[:], AF.Sigmoid, [y_b], [y_b], scale=1.5957691216057308)
                S.tt("dve", ydg3[:, 2 * ps:2 * ps + 2, tl], yt[:].rearrange("p (c t) -> p c t", c=2),
                     x.rearrange("p (c t) -> p c t", c=2), ALU.mult, [y_b, self.pb_b[6]], [ydg_b])

            na = NT if stop > 3 else 1
            st_bu(0); st_fwd(0); st_scan(0)
            for a in range(na):
                if a + 1 < na:
                    st_bu(a + 1)
                    st_fwd(a + 1)
                st_bwdmul(a)
                if a + 1 < na:
                    st_scan(a + 1)
                st_out(a)
        S.st = old
        S.barrier()
    if stop <= 4:
        for ch in range(12, 16):
            for half in range(2):
                self.yT_b[ch][half].w = ydg_b.w
        S.barrier()
        return
    with ExitStack() as st:
        S.st, old = st, S.st
        sg = [S.sb("s5_sg%d" % i, [128, 512], BF16) for i in range(2)]; sg_b = [Buf("sg%d" % i) for i in range(2)]
        szd = [S.sb("s5_szd%d" % i, [128, 512], BF16) for i in range(2)]; szd_b = [Buf("szd%d" % i) for i in range(2)]
        yo = [S.sb("s5_yo%d" % i, [128, 512], BF16) for i in range(2)]; yo_b = [Buf("yo%d" % i) for i in range(2)]
        Wzd, Wzd_b = self.load_w(l, O_ZD, 512)
        Wg = S.sb("s5_Wg", [128, 4 * 512], BF16); Wg_b = Buf("Wg")
        Wg3 = Wg[:].rearrange("p (c n) -> p c n", c=4)
        S.dma("pool", Wg3, self.glu_w.rearrange("(c p) n -> p c n", p=128), [], [Wg_b], sem="wg")
        k = 0
        for tb in range(4):
            ts_ = slice(tb * 512, (tb + 1) * 512)
            outs = []
            for oc in range(4):
                i = k % 2
                k += 1
                pbi = 4 + i
                for kc in range(4):
                    S.mm(self.pb[pbi][:], Wg3[:, kc, oc * 128:(oc + 1) * 128], ydg3[:, kc, ts_], kc == 0, kc == 3,
                         [Wg_b, ydg_b], [self.pb_b[pbi]])
                S.act(sg[i][:], self.pb[pbi][:], AF.Sigmoid, [self.pb_b[pbi], prm_b], [sg_b[i]], bias=glb[:, oc:oc + 1])
                pz = 6 + i
                for kc in range(8):
                    S.mm(self.pb[pz][:], Wzd[:, kc, oc * 128:(oc + 1) * 128], hT3[:, kc, ts_], kc == 0, kc == 7,
                         [self.hT_b[4 * tb + i2] for i2 in range(4)] + [Wzd_b], [self.pb_b[pz]])
                S.act(szd[i][:], self.pb[pz][:], AF.Silu, [self.pb_b[pz]], [szd_b[i]])
                S.tt("dve", sg[i][:], sg[i][:], szd[i][:], ALU.mult, [sg_b[i], szd_b[i]], [sg_b[i]])
                if oc < 3:
                    o = S.sb("s5_hold%d_%d" % (tb, oc), [128, 512], BF16)
                    ob = Buf("hold")
                    S.tt("dve", o[:], sg[i][:], ydg3[:, oc, ts_], ALU.mult, [sg_b[i], ydg_b], [ob])
                    outs.append((oc, o, ob))
                else:
                    S.tt("dve", ydg3[:, oc, ts_], sg[i][:], ydg3[:, oc, ts_], ALU.mult, [sg_b[i], ydg_b], [ydg_b])
            for oc, o, ob in outs:
                S.copy("pool", ydg3[:, oc, ts_], o[:], [ob], [ydg_b])
        S.st = old
        S.barrier()
    for ch in range(12, 16):
        for half in range(2):
            self.yT_b[ch][half].w = ydg_b.w
    S.barrier()


Prog.s5_layer = s5_layer


def _prep_items(self, A, l, g, hh, Wq, Wq_b, Wk, Wk_b, fox):
    S = self.S
    hT3 = self.hT3
    h = g * 8 + hh
    sl = h % 2
    QA, KA, QA_b, KA_b = A.QA[sl], A.KA[sl], A.QA_b[sl], A.KA_b[sl]
    items = []

    def proj(tb, which):
        def f():
            hr = [self.hT_b[4 * tb + i] for i in range(4)]
            if which == 0:
                pbi = 4
                for kc in range(8):
                    S.mm(self.pb[pbi][0:64, :], Wq[:, kc, hh * 64:(hh + 1) * 64], hT3[:, kc, tb * 512:(tb + 1) * 512],
                         kc == 0, kc == 7, hr + [Wq_b], [self.pb_b[pbi]])
                S.ts("dve", QA[0:64, tb * 512:(tb + 1) * 512], self.pb[pbi][0:64, :], 0.125, None, ALU.mult, None,
                     [self.pb_b[pbi]], [QA_b])
            else:
                pbi = 5
                for kc in range(8):
                    S.mm(self.pb[pbi][0:64, :], Wk[:, kc, hh * 64:(hh + 1) * 64], hT3[:, kc, tb * 512:(tb + 1) * 512],
                         kc == 0, kc == 7, hr + [Wk_b], [self.pb_b[pbi]])
                S.copy("dve", KA[0:64, tb * 512:(tb + 1) * 512], self.pb[pbi][0:64, :], [self.pb_b[pbi]], [KA_b])
        return f
    for tb in range(4):
        items.append(proj(tb, 1))
    for tb in range(4):
        items.append(proj(tb, 0))
    if A.kind == "fox":
        def aug1():
            S.copy("pool", A.Taug3[:, :, 64:67], fox.P4[:, :, h, :], [fox.b], [A.Taug_b])
            S.copy("pool", A.Taug3[:, :, 96:99], fox.N4[:, :, h, :], [fox.b], [A.Taug_b])

        def aug2(half):
            def f():
                pbi = 6 + half
                pv = self.pb[pbi][:].bitcast(BF16)
                for t8 in range(8):
                    tt = half * 8 + t8
                    S.tr(pv[0:99, t8 * 128:(t8 + 1) * 128], A.Taug3[:, tt, 0:99], self.ident[:], [A.Taug_b, self.cst_b],
                         [self.pb_b[pbi]])
                S.copy("dve", QA[64:67, half * 1024:(half + 1) * 1024], pv[64:67, :], [self.pb_b[pbi]], [QA_b])
                S.copy("dve", KA[96:99, half * 1024:(half + 1) * 1024], pv[96:99, :], [self.pb_b[pbi]], [KA_b])
            return f
        items.insert(0, aug1)
        items.append(aug2(0))
        items.append(aug2(1))
    else:
        def gate1():
            S.op("dve", lambda e: e.tensor_reduce(out=A.kb_f[:], in_=KA[0:64, :].rearrange("p (n t) -> p n t", n=8),
                                                  axis=AX.X, op=ALU.add), [KA_b], [A.g_b])
            S.ts("dve", A.kbT[:], A.kb_f[:], 1.0 / 256.0, None, ALU.mult, None, [A.g_b], [A.g_b])

        def gate2():
            for i in range(8):
                tt = 8 + i
                S.mm(self.pb[7][:, i * 8:(i + 1) * 8], QA[0:64, tt * 128:(tt + 1) * 128], A.kbT[:], True, True,
                     [QA_b, A.g_b], [self.pb_b[7]])
            g3 = self.pb[7][:, 0:64].rearrange("p (t n) -> p t n", t=8)
            for bp in range(4):
                qb = 4 + bp
                S.copy("dve", A.gt3[:, 2 * bp:2 * bp + 2, 0:qb], g3[:, 2 * bp:2 * bp + 2, 0:qb], [self.pb_b[7]], [A.g_b])
            for i in range(8):
                S.op("dve", lambda e, i=i: e.max(out=A.mx3[:, i, :], in_=A.gt3[:, i, :]), [A.g_b], [A.g_b])
            S.tt("dve", A.selb3, A.gt3, A.mx3[:, :, 2:3].to_broadcast([128, 8, 8]), ALU.is_ge, [A.g_b], [A.g_b])
            S.ts("dve", A.selb[:], A.selb[:], -1.0, BIG, ALU.add, ALU.mult, [A.g_b], [A.g_b])
            for bp in range(4):
                qb = 4 + bp
                S.copy("dve", A.Taug3[:, 8 + 2 * bp:10 + 2 * bp, 64:64 + qb], A.selb3[:, 2 * bp:2 * bp + 2, 0:qb], [A.g_b],
                       [A.Taug_b])

        def gate3():
            pv = self.pb[7][:].bitcast(BF16)
            for i in range(8):
                tt = 8 + i
                S.tr(pv[0:72, i * 128:(i + 1) * 128], A.Taug3[:, tt, 0:72], self.ident[:], [A.Taug_b, self.cst_b], [self.pb_b[7]])
            S.copy("dve", QA[64:72, 1024:2048], pv[64:72, :], [self.pb_b[7]], [QA_b])
        items.insert(4, gate1)
        items.append(gate2)
        items.append(gate3)
    return items


def _core_items(self, A, hh, sl):
    S = self.S
    QA, KA, QA_b, KA_b = A.QA[sl], A.KA[sl], A.QA_b[sl], A.KA_b[sl]
    osl = (hh // 2) % 2
    og3 = A.og[osl][:].rearrange("p (t n) -> p t n", t=NT)
    stA, stB = [], []
    for qc in range(4):
        ob = 2 + (A.o_i % 2)
        A.o_i += 1
        O3 = self.pb[ob][:, 0:4 * 65].rearrange("p (j d) -> p j d", j=4)
        nk = 4 * qc + 4
        for kt in range(nk):
            j0 = max(kt, 4 * qc)
            ncols = (4 * qc + 4 - j0) * 128
            sb = A.st_i % 2
            A.st_i += 1
            pi = A.pt_i % 3
            A.pt_i += 1
            diag = kt >= 4 * qc

            def fa(kt=kt, j0=j0, ncols=ncols, sb=sb, pi=pi, diag=diag, qc=qc):
                S.mm(self.pb[sb][:, 0:ncols], KA[:, kt * 128:(kt + 1) * 128], QA[:, j0 * 128:(4 * qc + 4) * 128],
                     True, not diag, [KA_b, QA_b], [self.pb_b[sb]])
                if diag:
                    S.mm(self.pb[sb][:, 0:128], self.ident[:], self.causT[:], False, True, [self.cst_b], [self.pb_b[sb]])
                S.act(A.PT[pi][:, 0:ncols], self.pb[sb][:, 0:ncols], AF.Exp, [self.pb_b[sb]], [A.PT_b[pi]])

            def fb(kt=kt, j0=j0, pi=pi, qc=qc, ob=ob, O3=O3, nk=nk):
                for j in range(j0, 4 * qc + 4):
                    c0 = (j - j0) * 128
                    jj = j - 4 * qc
                    S.mm(O3[:, jj, :], A.PT[pi][:, c0:c0 + 128], A.Vaug4[:, kt, hh, :], (kt == 0 and j == j0), kt == j,
                         [A.PT_b[pi], A.V_b], [self.pb_b[ob]], skip=True)
                if kt == nk - 1:
                    S.op("dve", lambda e: e.reciprocal(out=A.rec[:, 0:4], in_=O3[:, :, 64]), [self.pb_b[ob]], [A.rec_b])
                    for jj in range(4):
                        tt = 4 * qc + jj
                        S.stt(og3[:, tt, (hh % 2) * 64:(hh % 2) * 64 + 64], O3[:, jj, 0:64], A.rec[:, jj:jj + 1],
                              A.sZ3[:, tt, hh * 64:(hh + 1) * 64], ALU.mult, ALU.mult, [self.pb_b[ob], A.rec_b, A.sZ_b],
                              [A.og_b[osl]])
            stA.append(fa)
            stB.append(fb)
    return stA, stB


def attn_group2(self, A, l, g, zcol, qcol, kcol, vcol, ychunk0, fox=None):
    S = self.S
    hT3 = self.hT3
    Wv, Wv_b = self.load_w(l, vcol + g * 512, 512)
    Wz, Wz_b = self.load_w(l, zcol + g * 512, 512)
    Wq, Wq_b = self.load_w(l, qcol + g * 512, 512)
    for tt in range(NT):
        b0, b1 = 4 + 2 * (tt % 2), 5 + 2 * (tt % 2)
        for kc in range(8):
            S.mm(self.pb[b0][:], hT3[:, kc, tt * 128:(tt + 1) * 128], Wv[:, kc, :], kc == 0, kc == 7,
                 [self.hT_b[tt], Wv_b], [self.pb_b[b0]])
        S.copy("dve", A.Vaug4[:, tt, :, 0:64], self.pb[b0][:].rearrange("p (h d) -> p h d", h=8), [self.pb_b[b0]], [A.V_b])
        for kc in range(8):
            S.mm(self.pb[b1][:], hT3[:, kc, tt * 128:(tt + 1) * 128], Wz[:, kc, :], kc == 0, kc == 7,
                 [self.hT_b[tt], Wz_b], [self.pb_b[b1]])
        S.act(A.sZ3[:, tt, :], self.pb[b1][:], AF.Silu, [self.pb_b[b1]], [A.sZ_b])
    Wk, Wk_b = self.load_w(l, kcol + g * 512, 512)

    def og_items(hh):
        pr = hh // 2
        osl = pr % 2
        ch = ychunk0 + pr
        its = []
        for half in range(2):
            def f(half=half):
                pbi = 6 + half
                pv = self.pb[pbi][:].bitcast(BF16)
                for t8 in range(8):
                    tt = half * 8 + t8
                    S.tr(pv[:, t8 * 128:(t8 + 1) * 128], A.og[osl][:, tt * 128:(tt + 1) * 128], self.ident[:],
                         [A.og_b[osl], self.cst_b], [self.pb_b[pbi]])
                S.copy("dve", self.yT3[:, ch, half * 1024:(half + 1) * 1024], pv, [self.pb_b[pbi]], [self.yT_b[ch][half]])
            its.append(f)
        return its

    for it in _prep_items(self, A, l, g, 0, Wq, Wq_b, Wk, Wk_b, fox):
        it()
    LOOK = 2
    for hh in range(8):
        sl = (g * 8 + hh) % 2
        stA, stB = _core_items(self, A, hh, sl)
        side = []
        if hh > 0 and hh % 2 == 0:
            side += og_items(hh - 1)
        if hh < 7:
            side += _prep_items(self, A, l, g, hh + 1, Wq, Wq_b, Wk, Wk_b, fox)
        n = len(stA)
        for i in range(min(LOOK, n)):
            stA[i]()
        for i in range(n):
            stB[i]()
            if i + LOOK < n:
                stA[i + LOOK]()
            if side and i % 2 == 1:
                side.pop(0)()
        for it in side:
            it()
    for it in og_items(7):
        it()


Prog.attn_group = attn_group2


def attn_alloc2(self, kind):
    S = self.S
    A = type("A", (), {})()
    A.kind = kind
    A.Vaug = S.sb("Vaug", [128, NT * 8 * 65], BF16)
    A.Vaug4 = A.Vaug[:].rearrange("p (t h d) -> p t h d", t=NT, h=8)
    A.V_b = Buf("Vaug")
    A.sZ = S.sb("sZ", [128, NT * 512], BF16)
    A.sZ3 = A.sZ[:].rearrange("p (t n) -> p t n", t=NT)
    A.sZ_b = Buf("sZ")
    A.og = [S.sb("og%d" % i, [128, NT * 128], BF16) for i in range(2)]
    A.og_b = [Buf("og%d" % i) for i in range(2)]
    A.QAe = S.sb("QAe", [128, L], BF16); A.KAe = S.sb("KAe", [128, L], BF16)
    A.QAe_b, A.KAe_b = Buf("QAe"), Buf("KAe")
    A.QAo = [S.sb("QAo%d" % i, [128, L], BF16) for i in range(2)]
    A.KAo = [S.sb("KAo%d" % i, [128, L], BF16) for i in range(2)]
    A.QAo_b = [Buf("QAo%d" % i) for i in range(2)]
    A.KAo_b = [Buf("KAo%d" % i) for i in range(2)]
    A.Tg = [S.sb("Taug%d" % i, [128, NT * 128], BF16) for i in range(2)]
    A.Tg3 = [t[:].rearrange("p (t n) -> p t n", t=NT) for t in A.Tg]
    A.Tg_b = [Buf("TaugE"), Buf("TaugO")]
    A.PT = [S.sb("PT%d" % i, [128, 512], BF16) for i in range(4)]
    A.PT_b = [Buf("PT%d" % i) for i in range(4)]
    A.rec = S.sb("rec", [128, 8], F32)
    A.rec_b = Buf("rec")
    A.pt_i = A.st_i = A.o_i = 0
    S.memset("pool", A.Vaug[:], 1.0, [A.V_b])
    for i in range(2):
        S.memset("pool", A.Tg[i][:], 0.0, [A.Tg_b[i]])
    allq = [(A.QAe, A.QAe_b), (A.KAe, A.KAe_b)] + [(A.QAo[i], A.QAo_b[i]) for i in range(2)] + [(A.KAo[i], A.KAo_b[i]) for i in range(2)]
    for t, b in allq:
        S.memset("pool", t[:], 0.0, [b])
    A.aoff = [64, 0]
    A.qoff = [0, 64]
    if kind == "fox":
        S.memset("pool", A.QAe[96:99, :], 1.0, [A.QAe_b])
        S.memset("pool", A.KAe[64:67, :], 1.0, [A.KAe_b])
        for i in range(2):
            S.memset("pool", A.QAo[i][32:35, :], 1.0, [A.QAo_b[i]])
            S.memset("pool", A.KAo[i][0:3, :], 1.0, [A.KAo_b[i]])
    else:
        for par in range(2):
            ao = A.aoff[par]
            for n in range(8):
                S.memset("pool", A.Tg3[par][:, 2 * n:2 * n + 2, ao + n:ao + n + 1], 1.0, [A.Tg_b[par]])
            dsts = [(A.KAe, A.KAe_b)] if par == 0 else [(A.KAo[i], A.KAo_b[i]) for i in range(2)]
            for half in range(2):
                pbi = 6 + half
                pv = self.pb[pbi][:].bitcast(BF16)
                for t8 in range(8):
                    tt = half * 8 + t8
                    S.tr(pv[0:72, t8 * 128:(t8 + 1) * 128], A.Tg3[par][:, tt, 0:72], self.ident[:], [A.Tg_b[par], self.cst_b],
                         [self.pb_b[pbi]])
                for t, b in dsts:
                    S.copy("dve", t[ao:ao + 8, half * 1024:(half + 1) * 1024], pv[ao:ao + 8, :], [self.pb_b[pbi]], [b])
            S.memset("pool", A.Tg[par][:], 0.0, [A.Tg_b[par]])
        A.gt = S.sb("gt", [128, 64], F32); A.gt3 = A.gt[:].rearrange("p (t n) -> p t n", t=8)
        A.mx = S.sb("mx", [128, 64], F32); A.mx3 = A.mx[:].rearrange("p (t n) -> p t n", t=8)
        A.selb = S.sb("selb", [128, 64], F32); A.selb3 = A.selb[:].rearrange("p (t n) -> p t n", t=8)
        A.kb_f = S.sb("kb_f", [128, 8], F32)
        A.kbT = S.sb("kbT", [128, 8], BF16)
        A.g_b = Buf("gate")
        S.memset("pool", A.gt[:], -BIG, [A.g_b])
    return A


def _head_bufs(A, gh):
    if gh % 2 == 0:
        return A.QAe, A.KAe, A.QAe_b, A.KAe_b, 0
    k = (gh // 2) % 2
    return A.QAo[k], A.KAo[k], A.QAo_b[k], A.KAo_b[k], 1


def _prep_pair_items(self, A, l, g, pr, Wq, Wq_b, Wk, Wk_b, fox):
    S = self.S
    hT3 = self.hT3
    heads = [g * 8 + 2 * pr, g * 8 + 2 * pr + 1]
    hb = [_head_bufs(A, gh) for gh in heads]
    items = []

    def proj(tb, which, part):
        def f():
            hr = [self.hT_b[4 * tb + i] for i in range(4)]
            W, W_b = (Wq, Wq_b) if which == 0 else (Wk, Wk_b)
            pbi = 5 if tb % 2 == 0 else 7
            for kc in range(4 * part, 4 * part + 4):
                S.mm(self.pb[pbi][:], W[:, kc, pr * 128:(pr + 1) * 128], hT3[:, kc, tb * 512:(tb + 1) * 512],
                     kc == 0, kc == 7, hr + [W_b], [self.pb_b[pbi]])
            if part == 0:
                return
            for par in range(2):
                QA, KA, QA_b, KA_b, _ = hb[par]
                r0 = 64 * par
                if which == 0:
                    S.ts("dve", QA[r0:r0 + 64, tb * 512:(tb + 1) * 512], self.pb[pbi][r0:r0 + 64, :], 0.125, None, ALU.mult, None,
                         [self.pb_b[pbi]], [QA_b])
                else:
                    S.copy("dve", KA[r0:r0 + 64, tb * 512:(tb + 1) * 512], self.pb[pbi][r0:r0 + 64, :], [self.pb_b[pbi]], [KA_b])
        return f
    for tb in range(4):
        items.append(proj(tb, 1, 0))
        items.append(proj(tb, 1, 1))
    if A.kind != "fox":
        def gate1(par):
            def f():
                QA, KA, QA_b, KA_b, _ = hb[par]
                r0 = 64 * par
                S.op("dve", lambda e: e.tensor_reduce(out=A.kb_f[r0:r0 + 64, :], in_=KA[r0:r0 + 64, :].rearrange("p (n t) -> p n t", n=8),
                                                      axis=AX.X, op=ALU.add), [KA_b], [A.g_b])
                S.ts("dve", A.kbT[r0:r0 + 64, :], A.kb_f[r0:r0 + 64, :], 1.0 / 256.0, None, ALU.mult, None, [A.g_b], [A.g_b])
            return f
    for tb in range(4):
        items.append(proj(tb, 0, 0))
        items.append(proj(tb, 0, 1))
    for par in range(2):
        gh = heads[par]
        QA, KA, QA_b, KA_b, _ = hb[par]
        ao, r0 = A.aoff[par], 64 * par
        Tg3, Tg_b = A.Tg3[par], A.Tg_b[par]
        if A.kind == "fox":
            def aug1(gh=gh, ao=ao, Tg3=Tg3, Tg_b=Tg_b):
                S.copy("pool", Tg3[:, :, ao:ao + 3], fox.P4[:, :, gh, :], [fox.b], [Tg_b])
                S.copy("pool", Tg3[:, :, ao + 32:ao + 35], fox.N4[:, :, gh, :], [fox.b], [Tg_b])

            def aug2(half, ao=ao, Tg3=Tg3, Tg_b=Tg_b, QA=QA, KA=KA, QA_b=QA_b, KA_b=KA_b):
                def f():
                    pbi = 6
                    pv = self.pb[pbi][:].bitcast(BF16)
                    for t8 in range(8):
                        tt = half * 8 + t8
                        S.tr(pv[0:ao + 35, t8 * 128:(t8 + 1) * 128], Tg3[:, tt, 0:ao + 35], self.ident[:], [Tg_b, self.cst_b],
                             [self.pb_b[pbi]])
                    S.copy("dve", QA[ao:ao + 3, half * 1024:(half + 1) * 1024], pv[ao:ao + 3, :], [self.pb_b[pbi]], [QA_b])
                    S.copy("dve", KA[ao + 32:ao + 35, half * 1024:(half + 1) * 1024], pv[ao + 32:ao + 35, :], [self.pb_b[pbi]], [KA_b])
                return f
            items.insert(0, aug1)
            items.append(aug2(0))
            items.append(aug2(1))
        else:
            def gate(par=par, ao=ao, r0=r0, Tg3=Tg3, Tg_b=Tg_b, QA=QA, KA=KA, QA_b=QA_b, KA_b=KA_b):
                S.op("dve", lambda e: e.tensor_reduce(out=A.kb_f[r0:r0 + 64, :], in_=KA[r0:r0 + 64, :].rearrange("p (n t) -> p n t", n=8),
                                                      axis=AX.X, op=ALU.add), [KA_b], [A.g_b])
                S.ts("dve", A.kbT[r0:r0 + 64, :], A.kb_f[r0:r0 + 64, :], 1.0 / 256.0, None, ALU.mult, None, [A.g_b], [A.g_b])
                for i in range(8):
                    tt = 8 + i
                    S.mm(self.pb[6][:, i * 8:(i + 1) * 8], QA[r0:r0 + 64, tt * 128:(tt + 1) * 128], A.kbT[r0:r0 + 64, :], True, True,
                         [QA_b, A.g_b], [self.pb_b[6]])
                g3 = self.pb[6][:, 0:64].rearrange("p (t n) -> p t n", t=8)
                for bp in range(4):
                    qb = 4 + bp
                    S.copy("dve", A.gt3[:, 2 * bp:2 * bp + 2, 0:qb], g3[:, 2 * bp:2 * bp + 2, 0:qb], [self.pb_b[6]], [A.g_b])
                for i in range(8):
                    S.op("dve", lambda e, i=i: e.max(out=A.mx3[:, i, :], in_=A.gt3[:, i, :]), [A.g_b], [A.g_b])
                S.tt("dve", A.selb3, A.gt3, A.mx3[:, :, 2:3].to_broadcast([128, 8, 8]), ALU.is_ge, [A.g_b], [A.g_b])
                S.ts("dve", A.selb[:], A.selb[:], -1.0, BIG, ALU.add, ALU.mult, [A.g_b], [A.g_b])
                for bp in range(4):
                    qb = 4 + bp
                    S.copy("dve", Tg3[:, 8 + 2 * bp:10 + 2 * bp, ao:ao + qb], A.selb3[:, 2 * bp:2 * bp + 2, 0:qb], [A.g_b], [Tg_b])

            def gate3(ao=ao, Tg3=Tg3, Tg_b=Tg_b, QA=QA, QA_b=QA_b):
                pv = self.pb[6][:].bitcast(BF16)
                for i in range(8):
                    tt = 8 + i
                    S.tr(pv[0:ao + 8, i * 128:(i + 1) * 128], Tg3[:, tt, 0:ao + 8], self.ident[:], [Tg_b, self.cst_b], [self.pb_b[6]])
                S.copy("dve", QA[ao:ao + 8, 1024:2048], pv[ao:ao + 8, :], [self.pb_b[6]], [QA_b])
            items.append(gate)
            items.append(gate3)
    return items


def _core_items2(self, A, hh, QA, KA, QA_b, KA_b):
    S = self.S
    osl = (hh // 2) % 2
    og3 = A.og[osl][:].rearrange("p (t n) -> p t n", t=NT)
    stA, stB = [], []
    for qc in range(4):
        ob = 2 + (A.o_i % 2)
        A.o_i += 1
        O3 = self.pb[ob][:, 0:4 * 65].rearrange("p (j d) -> p j d", j=4)
        nk = 4 * qc + 4
        for kt in range(nk):
            j0 = max(kt, 4 * qc)
            ncols = (4 * qc + 4 - j0) * 128
            sb = (0, 1, 4)[A.st_i % 3]
            A.st_i += 1
            pi = A.pt_i % 4
            A.pt_i += 1
            diag = kt >= 4 * qc

            def fa(kt=kt, j0=j0, ncols=ncols, sb=sb, pi=pi, diag=diag, qc=qc):
                S.mm(self.pb[sb][:, 0:ncols], KA[:, kt * 128:(kt + 1) * 128], QA[:, j0 * 128:(4 * qc + 4) * 128],
                     True, not diag, [KA_b, QA_b], [self.pb_b[sb]])
                if diag:
                    S.mm(self.pb[sb][:, 0:128], self.ident[:], self.causT[:], False, True, [self.cst_b], [self.pb_b[sb]])
                S.act(A.PT[pi][:, 0:ncols], self.pb[sb][:, 0:ncols], AF.Exp, [self.pb_b[sb]], [A.PT_b[pi]])

            def fb(kt=kt, j0=j0, pi=pi, qc=qc, ob=ob, O3=O3, nk=nk):
                for j in range(j0, 4 * qc + 4):
                    c0 = (j - j0) * 128
                    jj = j - 4 * qc
                    S.mm(O3[:, jj, :], A.PT[pi][:, c0:c0 + 128], A.Vaug4[:, kt, hh, :], (kt == 0 and j == j0), kt == j,
                         [A.PT_b[pi], A.V_b], [self.pb_b[ob]], skip=True)
                if kt == nk - 1:
                    S.op("dve", lambda e: e.reciprocal(out=A.rec[:, 0:4], in_=O3[:, :, 64]), [self.pb_b[ob]], [A.rec_b])
                    for jj in range(4):
                        tt = 4 * qc + jj
                        S.stt(og3[:, tt, (hh % 2) * 64:(hh % 2) * 64 + 64], O3[:, jj, 0:64], A.rec[:, jj:jj + 1],
                              A.sZ3[:, tt, hh * 64:(hh + 1) * 64], ALU.mult, ALU.mult, [self.pb_b[ob], A.rec_b, A.sZ_b],
                              [A.og_b[osl]])
            stA.append(fa)
            stB.append(fb)
    return stA, stB


def attn_group3(self, A, l, g, zcol, qcol, kcol, vcol, ychunk0, fox=None):
    S = self.S
    hT3 = self.hT3
    Wv, Wv_b = self.load_w(l, vcol + g * 512, 512)
    Wz, Wz_b = self.load_w(l, zcol + g * 512, 512)
    Wq, Wq_b = self.load_w(l, qcol + g * 512, 512)
    for tt in range(NT):
        b0, b1 = 4 + 2 * (tt % 2), 5 + 2 * (tt % 2)
        for kc in range(8):
            S.mm(self.pb[b0][:], hT3[:, kc, tt * 128:(tt + 1) * 128], Wv[:, kc, :], kc == 0, kc == 7,
                 [self.hT_b[tt], Wv_b], [self.pb_b[b0]])
        S.copy("dve", A.Vaug4[:, tt, :, 0:64], self.pb[b0][:].rearrange("p (h d) -> p h d", h=8), [self.pb_b[b0]], [A.V_b])
        for kc in range(8):
            S.mm(self.pb[b1][:], hT3[:, kc, tt * 128:(tt + 1) * 128], Wz[:, kc, :], kc == 0, kc == 7,
                 [self.hT_b[tt], Wz_b], [self.pb_b[b1]])
        S.act(A.sZ3[:, tt, :], self.pb[b1][:], AF.Silu, [self.pb_b[b1]], [A.sZ_b])
    Wk, Wk_b = self.load_w(l, kcol + g * 512, 512)

    def og_items(hh):
        pr = hh // 2
        osl = pr % 2
        ch = ychunk0 + pr
        its = []
        for half in range(2):
            def f(half=half):
                pbi = 6
                pv = self.pb[pbi][:].bitcast(BF16)
                for t8 in range(8):
                    tt = half * 8 + t8
                    S.tr(pv[:, t8 * 128:(t8 + 1) * 128], A.og[osl][:, tt * 128:(tt + 1) * 128], self.ident[:],
                         [A.og_b[osl], self.cst_b], [self.pb_b[pbi]])
                S.copy("dve", self.yT3[:, ch, half * 1024:(half + 1) * 1024], pv, [self.pb_b[pbi]], [self.yT_b[ch][half]])
            its.append(f)
        return its

    for it in _prep_pair_items(self, A, l, g, 0, Wq, Wq_b, Wk, Wk_b, fox):
        it()
    LOOK = 3
    for hh in range(8):
        QA, KA, QA_b, KA_b, par = _head_bufs(A, g * 8 + hh)
        stA, stB = _core_items2(self, A, hh, QA, KA, QA_b, KA_b)
        side = []
        if hh > 0 and hh % 2 == 0:
            side += og_items(hh - 1)
        if hh % 2 == 1 and hh < 7:
            side += _prep_pair_items(self, A, l, g, hh // 2 + 1, Wq, Wq_b, Wk, Wk_b, fox)
        n = len(stA)
        for i in range(min(LOOK, n)):
            stA[i]()
        for i in range(n):
            stB[i]()
            if i + LOOK < n:
                stA[i + LOOK]()
            if side:
                side.pop(0)()
        for it in side:
            it()
    for it in og_items(7):
        it()


Prog.attn_alloc = attn_alloc2
Prog.attn_group = attn_group3
```

```python
import numpy as np
from contextlib import ExitStack
import concourse.bass as bass
import concourse.mybir as mybir
from concourse.bass_utils import run_bass_kernel_spmd

F32 = mybir.dt.float32
BF16 = mybir.dt.bfloat16
I32 = mybir.dt.int32
AF = mybir.ActivationFunctionType
ALU = mybir.AluOpType
AX = mybir.AxisListType

ENGS = ["pe", "act", "dve", "pool", "sp"]
SAME_ENGINE_SYNC = {"pe": False, "act": True, "dve": True, "pool": True, "sp": False}
SELF_DIST = 10 ** 9


class Buf:
    __slots__ = ("name", "w", "r")

    def __init__(self, name=""):
        self.name = name
        self.w = None
        self.r = {}


class Sched:
    def __init__(self, nc, st):
        self.nc = nc
        self.st = st
        self.ops = {e: [] for e in ENGS}
        self.cnt = {e: 0 for e in ENGS}
        self.known = {e: {} for e in ENGS}
        self.snap = {}
        self.dcnt = {}
        self.nops = 0

    def sb(self, name, shape, dtype):
        self.nalloc = getattr(self, "nalloc", 0) + 1
        return self.st.enter_context(self.nc.sbuf_tensor("%s_%d" % (name, self.nalloc), list(shape), dtype))

    def ps(self, name, shape, dtype):
        return self.st.enter_context(self.nc.psum_tensor(name, list(shape), dtype))

    def _deps(self, eng, reads, writes):
        deps = {}
        for b in reads:
            if b.w is not None and deps.get(b.w[0], 0) < b.w[1]:
                deps[b.w[0]] = b.w[1]
        for b in writes:
            if b.w is not None and deps.get(b.w[0], 0) < b.w[1]:
                deps[b.w[0]] = b.w[1]
            for k, v in b.r.items():
                if deps.get(k, 0) < v:
                    deps[k] = v
        kn = self.known[eng]
        waits = []
        for k, v in deps.items():
            if k == eng and (not SAME_ENGINE_SYNC[eng] or (eng != "pool" and v < self.cnt[eng] - (SELF_DIST - 1))):
                continue
            if kn.get(k, 0) >= v:
                continue
            waits.append((k, v))
        for k, v in waits:
            sn = self.snap.get((k, v))
            if sn:
                for kk, vv in sn.items():
                    if kn.get(kk, 0) < vv:
                        kn[kk] = vv
            if kn.get(k, 0) < v:
                kn[k] = v
        return waits

    def _mark(self, me, reads, writes):
        k, v = me
        for b in reads:
            if b.r.get(k, 0) < v:
                b.r[k] = v
        for b in writes:
            b.w = me
            b.r = {}

    def op(self, eng, fn, reads=(), writes=()):
        waits = self._deps(eng, reads, writes)
        self.cnt[eng] += 1
        me = (eng, self.cnt[eng])
        self.snap[me] = dict(self.known[eng])
        self.ops[eng].append((waits, fn, eng, self.cnt[eng]))
        self._mark(me, reads, writes)
        self.nops += 1

    def dma(self, eng, out, in_, reads=(), writes=(), sem=None):
        if sem is None or sem in ("misc", "wg", "modbc", "modrows", "dbg"):
            self.nuniq = getattr(self, "nuniq", 0) + 1
            sem = "u%d" % self.nuniq
        waits = self._deps(eng, reads, writes)
        self.dcnt[sem] = self.dcnt.get(sem, 0) + 16
        me = (sem, self.dcnt[sem])
        self.snap[me] = dict(self.known[eng])
        self.ops[eng].append((waits, lambda e: e.dma_start(out=out, in_=in_), sem, 16))
        self._mark(me, reads, writes)
        self.nops += 1

    def finish(self):
        nc = self.nc
        fin = [(k, v) for k, v in self.dcnt.items()]
        keys = list(ENGS) + list(self.dcnt.keys())
        sems = {k: self.st.enter_context(nc.semaphore("s_" + k)) for k in keys}
        targets = {e: set() for e in ENGS}
        for eng in ENGS:
            for waits, fn, key, amt in self.ops[eng]:
                for k, v in waits:
                    if k in targets:
                        targets[k].add(v)
        rank = {e: {v: i + 1 for i, v in enumerate(sorted(targets[e]))} for e in ENGS}
        self.nsignal = {e: len(rank[e]) for e in ENGS}
        block = self.st.enter_context(nc.Block())
        names = {"pe": "tensor", "act": "scalar", "dve": "vector", "pool": "gpsimd", "sp": "sync"}
        for eng in ENGS:
            lst = self.ops[eng]

            def body(e, lst=lst, eng=eng):
                for waits, fn, key, amt in lst:
                    for k, v in waits:
                        e.wait_ge(sems[k], rank[k][v] if k in rank else v)
                    if fn is None:
                        continue
                    ins = fn(e)
                    if key in rank:
                        if amt in rank[key]:
                            ins.then_inc(sems[key], 1)
                    else:
                        ins.then_inc(sems[key], amt)
                if eng == "sp":
                    for k, v in fin:
                        e.wait_ge(sems[k], v)
            getattr(block, names[eng])(body)

    def barrier(self):
        for e in ENGS:
            kn = self.known[e]
            waits = []
            for k in ENGS:
                if k != e and kn.get(k, 0) < self.cnt[k]:
                    waits.append((k, self.cnt[k])); kn[k] = self.cnt[k]
            for k, v in self.dcnt.items():
                if kn.get(k, 0) < v:
                    waits.append((k, v)); kn[k] = v
            if waits:
                self.ops[e].append((waits, None, None, 0))

    def act(self, out, in_, func, r, w, **kw):
        self.op("act", lambda e: e.activation(out=out, in_=in_, func=func, **kw), r, w)

    def tt(self, eng, out, in0, in1, op, r, w):
        self.op(eng, lambda e: e.tensor_tensor(out=out, in0=in0, in1=in1, op=op), r, w)

    def ts(self, eng, out, in0, s1, s2, op0, op1, r, w, **kw):
        if s2 is None:
            self.op(eng, lambda e: e.tensor_scalar(out=out, in0=in0, scalar1=s1, scalar2=None, op0=op0, **kw), r, w)
        else:
            self.op(eng, lambda e: e.tensor_scalar(out=out, in0=in0, scalar1=s1, scalar2=s2, op0=op0, op1=op1, **kw), r, w)

    def stt(self, out, in0, scalar, in1, op0, op1, r, w):
        self.op("dve", lambda e: e.scalar_tensor_tensor(out=out, in0=in0, scalar=scalar, in1=in1, op0=op0, op1=op1), r, w)

    def copy(self, eng, out, in_, r, w):
        if eng == "act":
            self.op("act", lambda e: e.activation(out=out, in_=in_, func=AF.Copy), r, w)
        else:
            self.op(eng, lambda e: e.tensor_copy(out=out, in_=in_), r, w)

    def mm(self, out, lhsT, rhs, start, stop, r, w, skip=False):
        self.op("pe", lambda e: e.matmul(out, lhsT=lhsT, rhs=rhs, start=start, stop=stop, skip_group_check=skip), r, w)

    def tr(self, out, in_, ident, r, w):
        self.op("pe", lambda e: e.transpose(out=out, in_=in_, identity=ident), r, w)

    def memset(self, eng, ap, val, w):
        self.op(eng, lambda e: e.memset(ap, val), (), w)

    def asel(self, out, in_, pattern, cmp, fill, base, cm, r, w):
        self.op("pool", lambda e: e.affine_select(out=out, in_=in_, pattern=pattern, compare_op=cmp, fill=fill,
                                                  base=base, channel_multiplier=cm), r, w)


L = 2048
D = 1024
NT = 16
BIG = 30000.0
EPS = 1e-6

E_ZA, E_ZB, E_XS, E_B, E_C, E_DT, E_Q, E_K, E_V = 0, 1024, 2048, 3072, 3328, 3584, 3600, 4624, 5648
O_ZC, O_ZD, O_Q, O_K, O_V, O_F, O_U = 0, 1536, 2048, 3584, 5120, 6656, 6680


class Prog:
    def __init__(self, dbg=None):
        self.dbg = dbg or []
        self.nc = bass.Bass("TRN2", target_bir_lowering=False)
        self.st = ExitStack()
        self.S = Sched(self.nc, self.st)
        self.wslot_i = 0
        self.skip_ssd = False
        self.skip_s5 = False

    def din(self, name, shape, dtype=F32):
        return self.nc.dram_tensor(name, list(shape), dtype, kind="ExternalInput").ap()

    def declare(self):
        self.x = self.din("x", [L, D])
        self.c_t = self.din("c_t", [128, 8])
        self.ada_w = self.din("ada_w", [2, D, 3 * D])
        self.ada_b = self.din("ada_b", [2, 3 * D])
        self.pre_g = self.din("pre_g", [2, D])
        self.post_g = self.din("post_g", [2, D])
        self.in_w = [self.din("even_in_w", [D, 6672]), self.din("odd_in_w", [D, 7192])]
        self.out_w = [self.din("even_out_w", [2 * D, D]), self.din("odd_out_w", [2 * D, D])]
        self.fgate_b = self.din("odd_fgate_b", [1, 24])
        self.conv_p = self.din("conv_p", [128, 60])
        self.ssd_rows = self.din("ssd_rows", [1, 48])
        self.even_norm_g = self.din("even_norm_g", [1, 1024])
        self.s5_state = self.din("s5_state", [128, 48])
        self.s5_vec = self.din("s5_vec", [128, 8])
        self.s5_B = self.din("s5_B", [128, 512])
        self.s5_C = self.din("s5_C", [128, 512])
        self.glu_w = self.din("odd_glu_w", [512, 512])
        self.out = self.nc.dram_tensor("out", [L, D], F32, kind="ExternalOutput").ap()
        self.x1 = self.nc.dram_tensor("x1_scratch", [L, D], F32).ap()
        self.modrows = self.nc.dram_tensor("modrows", [2, 3, D], F32).ap()
        self.dbg_out = {}
        for name, shape, dtype in self.dbg:
            self.dbg_out[name] = self.nc.dram_tensor("dbg_" + name, list(shape), dtype, kind="ExternalOutput").ap()

    def alloc_persistent(self):
        S = self.S
        self.hT = S.sb("hT", [128, 8 * L], BF16)
        self.hT3 = self.hT[:].rearrange("p (c t) -> p c t", c=8)
        self.hT_b = [Buf("hT%d" % i) for i in range(NT)]
        self.yT = S.sb("yT", [128, 16 * L], BF16)
        self.yT3 = self.yT[:].rearrange("p (c t) -> p c t", c=16)
        self.yT_b = [[Buf("yT%d_%d" % (c, i)) for i in range(2)] for c in range(16)]
        self.ident = S.sb("ident", [128, 128], BF16)
        self.causT = S.sb("causT", [128, 128], BF16)
        self.T_f = S.sb("T_f", [128, 128], F32)
        self.U_f = S.sb("U_f", [128, 128], F32)
        self.ones_f = S.sb("ones_f", [128, 128], F32)
        self.mhalf = S.sb("mhalf", [128, 1], F32)
        self.cst_b = Buf("consts")
        self.wslot = [S.sb("wslot%d" % i, [128, 8 * 512], BF16) for i in range(3)]
        self.wslot_b = [Buf("wslot%d" % i) for i in range(3)]
        self.xin_b = [Buf("xin%d" % i) for i in range(2)]
        self.modbc_b = Buf("modbc")
        self.stat = S.sb("stat", [128, 64], F32)
        self.stat_bs = [Buf("stat%d" % i) for i in range(4)]
        self.stat_b = self.stat_bs[0]
        self.pb = [S.ps("pb%d" % i, [128, 512], F32) for i in range(8)]
        self.pb_b = [Buf("pb%d" % i) for i in range(8)]

    def consts(self):
        S = self.S
        w = [self.cst_b]
        S.memset("pool", self.ident[:], 0.0, w)
        S.asel(self.ident[:], self.ident[:], [[-1, 128]], ALU.not_equal, 1.0, 0, 1, w, w)
        S.memset("pool", self.causT[:], 0.0, w)
        S.asel(self.causT[:], self.causT[:], [[1, 128]], ALU.is_ge, -BIG, 0, -1, w, w)
        S.memset("pool", self.T_f[:], 1.0, w)
        S.asel(self.T_f[:], self.T_f[:], [[1, 128]], ALU.is_ge, 0.0, 0, -1, w, w)
        S.memset("pool", self.U_f[:], 1.0, w)
        S.asel(self.U_f[:], self.U_f[:], [[-1, 128]], ALU.is_ge, 0.0, -1, 1, w, w)
        S.memset("pool", self.ones_f[:], 1.0, w)
        S.memset("pool", self.mhalf[:], -0.5, w)

    def load_w(self, l, col0, ncols):
        i = self.wslot_i % 3
        self.wslot_i += 1
        v = self.wslot[i][:, 0:8 * ncols].rearrange("p (c n) -> p c n", c=8)
        src = self.in_w[l].rearrange("(c p) n -> p c n", p=128)[:, :, col0:col0 + ncols]
        self.S.dma("pool", v, src, reads=[], writes=[self.wslot_b[i]], sem="w%d" % i)
        return v, self.wslot_b[i]

    def scol(self, slot, i):
        return self.stat[:, 16 * slot + i:16 * slot + i + 1]

    def rstd_from_ss(self, ss_ap, n, r, slot=0):
        S = self.S
        b = self.stat_bs[slot]
        c8, c9, c10 = self.scol(slot, 8), self.scol(slot, 9), self.scol(slot, 10)
        S.ts("dve", c8, ss_ap, 1.0 / n, EPS, ALU.mult, ALU.add, r + [b], [b])
        S.tt("pool", c10, c8, self.mhalf[:], ALU.pow, [b, self.cst_b], [b])
        return c10

    def adaln_items(self, l, nstage):
        S = self.S
        stage = [S.sb("adast%d" % i, [128, 8 * 512], F32) for i in range(nstage)]
        stage_b = [Buf("adast%d" % i) for i in range(nstage)]
        row = S.sb("row", [1, 3 * D], F32)
        rows2 = S.sb("rows2", [1, 2 * D], F32)
        row_b = Buf("row")
        cond, cond_b = self.cond, self.cond_b
        items = []

        def first():
            S.dma("sp", row[:], self.ada_b[l:l + 1, :], [], [row_b], sem="misc")
            S.dma("sp", rows2[:, 0:D], self.pre_g[l:l + 1, :], [], [row_b], sem="misc")
            S.dma("sp", rows2[:, D:2 * D], self.post_g[l:l + 1, :], [], [row_b], sem="misc")
        items.append(first)

        def ld(cb):
            sl = cb % nstage
            v = stage[sl][:].rearrange("p (c n) -> p c n", c=8)
            src = self.ada_w[l].rearrange("(c p) n -> p c n", p=128)[:, :, cb * 512:(cb + 1) * 512]
            S.dma("sp", v, src, [], [stage_b[sl]], sem="adast%d_%d" % (l, sl))

        def blk(cb):
            def f():
                if cb == 0:
                    for i in range(min(nstage, 6)):
                        ld(i)
                sl = cb % nstage
                v = stage[sl][:].rearrange("p (c n) -> p c n", c=8)
                pbi = 3
                for kc in range(8):
                    S.mm(self.pb[pbi][0:1, :], cond[:, kc:kc + 1], v[:, kc, :], kc == 0, kc == 7,
                         [cond_b, stage_b[sl]], [self.pb_b[pbi]])
                S.tt("dve", row[:, cb * 512:(cb + 1) * 512], row[:, cb * 512:(cb + 1) * 512], self.pb[pbi][0:1, :],
                     ALU.add, [row_b, self.pb_b[pbi]], [row_b])
                if cb + nstage < 6:
                    ld(cb + nstage)
            return f
        for cb in range(6):
            items.append(blk(cb))

        def last():
            S.stt(row[:, D:2 * D], row[:, D:2 * D], 1.0, rows2[:, 0:D], ALU.add, ALU.mult, [row_b], [row_b])
            S.tt("dve", row[:, 2 * D:3 * D], row[:, 2 * D:3 * D], rows2[:, D:2 * D], ALU.mult, [row_b], [row_b])
            S.dma("sp", self.modrows[l].rearrange("a d -> (a d)").rearrange("(o n) -> o n", o=1), row[:], [row_b], [],
                  sem="modrows")
        items.append(last)
        return items

    def adaln(self, layers):
        S = self.S
        if not hasattr(self, "cond"):
            self.cond = S.sb("cond", [128, 8], F32)
            self.cond_b = Buf("cond")
            S.dma("sp", self.cond[:], self.c_t, [], [self.cond_b], sem="misc")
            S.act(self.cond[:], self.cond[:], AF.Silu, [self.cond_b], [self.cond_b])
        with ExitStack() as st:
            S.st, old = st, S.st
            for l in layers:
                with ExitStack() as st2:
                    S.st = st2
                    for it in self.adaln_items(l, 2):
                        it()
                    S.st = st
                    S.barrier()
            S.st = old
            S.barrier()
        S.barrier()

    def prefetch_wout(self, l):
        wsrc = self.out_w[l].rearrange("(c p) n -> p c n", p=128)
        self.wout_pre = []
        for q in range(3):
            v = self.wslot[q][:].rearrange("p (c n) -> p c n", c=4)
            self.S.dma("pool", v, wsrc[:, q * 4:(q + 1) * 4, :], [], [self.wslot_b[q]], sem="woutp%d_%d" % (l, q))
            self.wout_pre.append((v, self.wslot_b[q]))

    def load_modbc(self, l, c0, c1):
        t = self.S.sb("modbc", [128, (c1 - c0) * D], F32)
        b = Buf("modbc")
        src = self.modrows[l].rearrange("a d -> (a d)").rearrange("(o n) -> o n", o=1)[:, c0 * D:c1 * D]
        self.S.dma("sp", t[:], src.to_broadcast([128, (c1 - c0) * D]), [], [b], sem="modbc")
        return t, b

    def phase_c(self, l, xsrc, xdst, wout, wout_b, res, gp, gp_b, PA=None):
        S = self.S
        wsrc = self.out_w[l].rearrange("(c p) n -> p c n", p=128)
        wq_b = [Buf("wout_q%d" % q) for q in range(4)]
        wqv = [wout[:, q * 4:(q + 1) * 4, :] for q in range(4)]
        pre = getattr(self, "wout_pre", None)
        self.wout_pre = None
        for q in range(4):
            if pre is not None and q < 3:
                wqv[q], wq_b[q] = pre[q]
            else:
                S.dma("pool", wqv[q], wsrc[:, q * 4:(q + 1) * 4, :], [wout_b], [wq_b[q]], sem="wout%d" % q)
        rb = [Buf("res0"), Buf("res1")]

        def st_mm(tt):
            sl = tt % 2
            sb_ = self.stat_bs[2 + sl]
            xi, xb = self.xin[sl], self.xin_b[sl]
            S.dma("sp", xi[:], xsrc[tt * 128:(tt + 1) * 128, :], [], [xb], sem="xin%d" % sl)
            banks = [(2 * (tt % 2)) + 0, (2 * (tt % 2)) + 1]
            for nb in range(2):
                bi = banks[nb]
                for kc in range(16):
                    S.mm(self.pb[bi][:], self.yT3[:, kc, tt * 128:(tt + 1) * 128], wqv[kc // 4][:, kc % 4, nb * 512:(nb + 1) * 512],
                         kc == 0, kc == 15, [self.yT_b[kc][tt // 8], wq_b[kc // 4]], [self.pb_b[bi]])

        def st_post(tt):
            sl = tt % 2
            sb_ = self.stat_bs[2 + sl]
            xi, xb = self.xin[sl], self.xin_b[sl]
            banks = [(2 * (tt % 2)) + 0, (2 * (tt % 2)) + 1]
            for nb in range(2):
                bi = banks[nb]
                S.act(res[sl][:, nb * 512:(nb + 1) * 512], self.pb[bi][:], AF.Square, [self.pb_b[bi]], [rb[sl], sb_],
                      accum_out=self.scol(2 + sl, nb))
            S.tt("dve", self.scol(2 + sl, 2), self.scol(2 + sl, 0), self.scol(2 + sl, 1), ALU.add, [sb_], [sb_])
            self.rstd_from_ss(self.scol(2 + sl, 2), D, [], slot=2 + sl)

        def st_post2(tt):
            sl = tt % 2
            sb_ = self.stat_bs[2 + sl]
            xi, xb = self.xin[sl], self.xin_b[sl]
            banks = [(2 * (tt % 2)) + 0, (2 * (tt % 2)) + 1]
            rstd = self.scol(2 + sl, 10)
            for nb in range(2):
                bi = banks[nb]
                S.stt(res[sl][:, nb * 512:(nb + 1) * 512], self.pb[bi][:], rstd, gp[:, nb * 512:(nb + 1) * 512],
                      ALU.mult, ALU.mult, [self.pb_b[bi], sb_, gp_b], [rb[sl]])
            S.tt("dve", res[sl][:], res[sl][:], xi[:], ALU.add, [rb[sl], xb], [rb[sl]])
            S.dma("sp", xdst[tt * 128:(tt + 1) * 128, :], res[sl][:], [rb[sl]], [], sem="xout%d" % sl)

        st_mm(0)
        for tt in range(NT):
            if tt + 1 < NT:
                st_mm(tt + 1)
            st_post(tt)
            if PA is not None and tt > 0:
                self.phase_a_rest1(PA, tt - 1)
            st_post2(tt)
            if PA is not None:
                self.phase_a_sq(PA, tt, res[tt % 2], rb[tt % 2])
                if tt > 0:
                    self.phase_a_rest2(PA, tt - 1, res[(tt - 1) % 2], rb[(tt - 1) % 2])
        if PA is not None:
            self.phase_a_rest(PA, NT - 1, res[(NT - 1) % 2], rb[(NT - 1) % 2])

    def attn_alloc(self, kind):
        S = self.S
        A = type("A", (), {})()
        A.kind = kind
        A.Vaug = S.sb("Vaug", [128, NT * 8 * 65], BF16)
        A.Vaug4 = A.Vaug[:].rearrange("p (t h d) -> p t h d", t=NT, h=8)
        A.V_b = Buf("Vaug")
        A.sZ = S.sb("sZ", [128, NT * 512], BF16)
        A.sZ3 = A.sZ[:].rearrange("p (t n) -> p t n", t=NT)
        A.sZ_b = Buf("sZ")
        A.og = [S.sb("og%d" % i, [128, NT * 128], BF16) for i in range(2)]
        A.og_b = [Buf("og%d" % i) for i in range(2)]
        A.QA = [S.sb("QA%d" % i, [128, L], BF16) for i in range(2)]
        A.KA = [S.sb("KA%d" % i, [128, L], BF16) for i in range(2)]
        A.QA_b = [Buf("QA%d" % i) for i in range(2)]
        A.KA_b = [Buf("KA%d" % i) for i in range(2)]
        A.Taug = S.sb("Taug", [128, NT * 128], BF16)
        A.Taug3 = A.Taug[:].rearrange("p (t n) -> p t n", t=NT)
        A.Taug_b = Buf("Taug")
        A.PT = [S.sb("PT%d" % i, [128, 512], BF16) for i in range(3)]
        A.PT_b = [Buf("PT%d" % i) for i in range(3)]
        A.rec = S.sb("rec", [128, 8], F32)
        A.rec_b = Buf("rec")
        A.pt_i = 0
        A.st_i = 0
        A.o_i = 0
        S.memset("pool", A.Vaug[:], 1.0, [A.V_b])
        S.memset("pool", A.Taug[:], 0.0, [A.Taug_b])
        for i in range(2):
            S.memset("pool", A.QA[i][:], 0.0, [A.QA_b[i]])
            S.memset("pool", A.KA[i][:], 0.0, [A.KA_b[i]])
        if kind == "fox":
            for i in range(2):
                S.memset("pool", A.QA[i][96:99, :], 1.0, [A.QA_b[i]])
                S.memset("pool", A.KA[i][64:67, :], 1.0, [A.KA_b[i]])
        else:
            for n in range(8):
                S.memset("pool", A.Taug3[:, 2 * n:2 * n + 2, 64 + n:65 + n], 1.0, [A.Taug_b])
            for half in range(2):
                pbi = 6 + half
                pv = self.pb[pbi][:].bitcast(BF16)
                for t8 in range(8):
                    tt = half * 8 + t8
                    S.tr(pv[0:72, t8 * 128:(t8 + 1) * 128], A.Taug3[:, tt, 0:72], self.ident[:], [A.Taug_b, self.cst_b],
                         [self.pb_b[pbi]])
                for i in range(2):
                    S.copy("dve", A.KA[i][64:72, half * 1024:(half + 1) * 1024], pv[64:72, :], [self.pb_b[pbi]], [A.KA_b[i]])
            S.memset("pool", A.Taug[:], 0.0, [A.Taug_b])
            A.gt = S.sb("gt", [128, 64], F32)
            A.gt3 = A.gt[:].rearrange("p (t n) -> p t n", t=8)
            A.mx = S.sb("mx", [128, 64], F32)
            A.mx3 = A.mx[:].rearrange("p (t n) -> p t n", t=8)
            A.selb = S.sb("selb", [128, 64], F32)
            A.selb3 = A.selb[:].rearrange("p (t n) -> p t n", t=8)
            A.kb_f = S.sb("kb_f", [64, 8], F32)
            A.kbT = S.sb("kbT", [64, 8], BF16)
            A.g_b = Buf("gate")
            S.memset("pool", A.gt[:], -BIG, [A.g_b])
        return A

    def attn_group(self, A, l, g, zcol, qcol, kcol, vcol, ychunk0, fox=None):
        S = self.S
        hT3 = self.hT3
        Wv, Wv_b = self.load_w(l, vcol + g * 512, 512)
        Wz, Wz_b = self.load_w(l, zcol + g * 512, 512)
        for tt in range(NT):
            b0, b1 = 4 + 2 * (tt % 2), 5 + 2 * (tt % 2)
            for kc in range(8):
                S.mm(self.pb[b0][:], hT3[:, kc, tt * 128:(tt + 1) * 128], Wv[:, kc, :], kc == 0, kc == 7,
                     [self.hT_b[tt], Wv_b], [self.pb_b[b0]])
            S.copy("dve", A.Vaug4[:, tt, :, 0:64], self.pb[b0][:].rearrange("p (h d) -> p h d", h=8), [self.pb_b[b0]], [A.V_b])
            for kc in range(8):
                S.mm(self.pb[b1][:], hT3[:, kc, tt * 128:(tt + 1) * 128], Wz[:, kc, :], kc == 0, kc == 7,
                     [self.hT_b[tt], Wz_b], [self.pb_b[b1]])
            S.act(A.sZ3[:, tt, :], self.pb[b1][:], AF.Silu, [self.pb_b[b1]], [A.sZ_b])
        Wq, Wq_b = self.load_w(l, qcol + g * 512, 512)
        Wk, Wk_b = self.load_w(l, kcol + g * 512, 512)
        for hh in range(8):
            h = g * 8 + hh
            sl = h % 2
            QA, KA, QA_b, KA_b = A.QA[sl], A.KA[sl], A.QA_b[sl], A.KA_b[sl]
            for tb in range(4):
                hr = [self.hT_b[4 * tb + i] for i in range(4)]
                for kc in range(8):
                    S.mm(self.pb[4][0:64, :], Wq[:, kc, hh * 64:(hh + 1) * 64], hT3[:, kc, tb * 512:(tb + 1) * 512],
                         kc == 0, kc == 7, hr + [Wq_b], [self.pb_b[4]])
                S.act(QA[0:64, tb * 512:(tb + 1) * 512], self.pb[4][0:64, :], AF.Copy, [self.pb_b[4]], [QA_b], scale=0.125)
                for kc in range(8):
                    S.mm(self.pb[5][0:64, :], Wk[:, kc, hh * 64:(hh + 1) * 64], hT3[:, kc, tb * 512:(tb + 1) * 512],
                         kc == 0, kc == 7, hr + [Wk_b], [self.pb_b[5]])
                S.copy("dve", KA[0:64, tb * 512:(tb + 1) * 512], self.pb[5][0:64, :], [self.pb_b[5]], [KA_b])
            if A.kind == "fox":
                self.fox_aug(A, fox, h, QA, KA, QA_b, KA_b)
            else:
                self.moba_aug(A, QA, KA, QA_b, KA_b)
            self.attn_core(A, hh, QA, KA, QA_b, KA_b)
            if hh % 2 == 1:
                pr = hh // 2
                osl = pr % 2
                ch = ychunk0 + pr
                for half in range(2):
                    pbi = 6 + half
                    pv = self.pb[pbi][:].bitcast(BF16)
                    for t8 in range(8):
                        tt = half * 8 + t8
                        S.tr(pv[:, t8 * 128:(t8 + 1) * 128], A.og[osl][:, tt * 128:(tt + 1) * 128], self.ident[:],
                             [A.og_b[osl], self.cst_b], [self.pb_b[pbi]])
                    S.copy("dve", self.yT3[:, ch, half * 1024:(half + 1) * 1024], pv, [self.pb_b[pbi]], [self.yT_b[ch][half]])

    def attn_core(self, A, hh, QA, KA, QA_b, KA_b):
        S = self.S
        osl = (hh // 2) % 2
        og3 = A.og[osl][:].rearrange("p (t n) -> p t n", t=NT)
        for qc in range(4):
            ob = 2 + (A.o_i % 2)
            A.o_i += 1
            first = True
            O3 = self.pb[ob][:, 0:4 * 65].rearrange("p (j d) -> p j d", j=4)
            for kt in range(4 * qc + 4):
                j0 = max(kt, 4 * qc)
                ncols = (4 * qc + 4 - j0) * 128
                sb = A.st_i % 2
                A.st_i += 1
                diag = kt >= 4 * qc
                S.mm(self.pb[sb][:, 0:ncols], KA[:, kt * 128:(kt + 1) * 128], QA[:, j0 * 128:(4 * qc + 4) * 128],
                     True, not diag, [KA_b, QA_b], [self.pb_b[sb]])
                if diag:
                    S.mm(self.pb[sb][:, 0:128], self.ident[:], self.causT[:], False, True, [self.cst_b], [self.pb_b[sb]])
                pi = A.pt_i % 3
                A.pt_i += 1
                S.act(A.PT[pi][:, 0:ncols], self.pb[sb][:, 0:ncols], AF.Exp, [self.pb_b[sb]], [A.PT_b[pi]])
                for j in range(j0, 4 * qc + 4):
                    c0 = (j - j0) * 128
                    jj = j - 4 * qc
                    S.mm(O3[:, jj, :], A.PT[pi][:, c0:c0 + 128], A.Vaug4[:, kt, hh, :], first, kt == j,
                         [A.PT_b[pi], A.V_b], [self.pb_b[ob]], skip=True)
                    first = False
            S.op("dve", lambda e, O3=O3: e.reciprocal(out=A.rec[:, 0:4], in_=O3[:, :, 64]), [self.pb_b[ob]], [A.rec_b])
            for jj in range(4):
                tt = 4 * qc + jj
                S.stt(og3[:, tt, (hh % 2) * 64:(hh % 2) * 64 + 64], O3[:, jj, 0:64], A.rec[:, jj:jj + 1],
                      A.sZ3[:, tt, hh * 64:(hh + 1) * 64], ALU.mult, ALU.mult, [self.pb_b[ob], A.rec_b, A.sZ_b], [A.og_b[osl]])

    def moba_aug(self, A, QA, KA, QA_b, KA_b):
        S = self.S
        S.op("dve", lambda e: e.tensor_reduce(out=A.kb_f[:], in_=KA[0:64, :].rearrange("p (n t) -> p n t", n=8),
                                              axis=AX.X, op=ALU.add), [KA_b], [A.g_b])
        S.ts("dve", A.kbT[:], A.kb_f[:], 1.0 / 256.0, None, ALU.mult, None, [A.g_b], [A.g_b])
        for i in range(8):
            tt = 8 + i
            S.mm(self.pb[7][:, i * 8:(i + 1) * 8], QA[0:64, tt * 128:(tt + 1) * 128], A.kbT[:], True, True,
                 [QA_b, A.g_b], [self.pb_b[7]])
        g3 = self.pb[7][:, 0:64].rearrange("p (t n) -> p t n", t=8)
        for bp in range(4):
            qb = 4 + bp
            S.copy("dve", A.gt3[:, 2 * bp:2 * bp + 2, 0:qb], g3[:, 2 * bp:2 * bp + 2, 0:qb], [self.pb_b[7]], [A.g_b])
        for i in range(8):
            S.op("dve", lambda e, i=i: e.max(out=A.mx3[:, i, :], in_=A.gt3[:, i, :]), [A.g_b], [A.g_b])
        S.tt("dve", A.selb3, A.gt3, A.mx3[:, :, 2:3].to_broadcast([128, 8, 8]), ALU.is_ge, [A.g_b], [A.g_b])
        S.ts("dve", A.selb[:], A.selb[:], -1.0, BIG, ALU.add, ALU.mult, [A.g_b], [A.g_b])
        for bp in range(4):
            qb = 4 + bp
            S.copy("dve", A.Taug3[:, 8 + 2 * bp:10 + 2 * bp, 64:64 + qb], A.selb3[:, 2 * bp:2 * bp + 2, 0:qb], [A.g_b], [A.Taug_b])
        pv = self.pb[7][:].bitcast(BF16)
        for i in range(8):
            tt = 8 + i
            S.tr(pv[0:72, i * 128:(i + 1) * 128], A.Taug3[:, tt, 0:72], self.ident[:], [A.Taug_b, self.cst_b], [self.pb_b[7]])
        S.copy("dve", QA[64:72, 1024:2048], pv[64:72, :], [self.pb_b[7]], [QA_b])

    def fox_prep(self, l):
        S = self.S
        Fx = type("F", (), {})()
        Fx.P = S.sb("Fp", [128, NT * 24 * 3], BF16)
        Fx.N = S.sb("Fn", [128, NT * 24 * 3], BF16)
        Fx.P4 = Fx.P[:].rearrange("p (t h k) -> p t h k", t=NT, h=24)
        Fx.N4 = Fx.N[:].rearrange("p (t h k) -> p t h k", t=NT, h=24)
        Fx.b = Buf("Fpieces")
        with ExitStack() as st:
            S.st, old = st, S.st
            lf = S.sb("lf", [128, NT * 24], F32)
            t1 = S.sb("lf_t1", [128, NT * 24], F32)
            t2 = S.sb("lf_t2", [128, NT * 24], F32)
            fb = S.sb("fb", [128, 24], F32)
            hb = S.sb("lf_hb", [128, NT * 24], BF16)
            b = Buf("lf")
            S.dma("sp", fb[:], self.fgate_b.to_broadcast([128, 24]), [], [b], sem="misc")
            Wf, Wf_b = self.load_w(l, O_F, 24)
            for tt in range(NT):
                for kc in range(8):
                    S.mm(self.pb[7][:, tt * 24:(tt + 1) * 24], self.hT3[:, kc, tt * 128:(tt + 1) * 128], Wf[:, kc, :],
                         kc == 0, kc == 7, [self.hT_b[tt], Wf_b], [self.pb_b[7]])
            lf3 = lf[:].rearrange("p (t h) -> p t h", t=NT)
            S.tt("dve", lf3, self.pb[7][:, 0:NT * 24].rearrange("p (t h) -> p t h", t=NT),
                 fb[:].unsqueeze(1).to_broadcast([128, NT, 24]), ALU.add, [self.pb_b[7], b], [b])
            self.dump("fraw", lf[:], [b])
            S.ts("dve", t2[:], lf[:], -1.0, None, ALU.mult, None, [b], [b])
            S.tt("dve", t1[:], t2[:], lf[:], ALU.max, [b], [b])
            S.act(t1[:], t1[:], AF.Exp, [b], [b], scale=-1.0)
            S.act(t1[:], t1[:], AF.Ln, [b], [b], bias=1.0)
            S.ts("dve", t2[:], lf[:], 0.0, None, ALU.min, None, [b], [b])
            S.tt("dve", lf[:], t2[:], t1[:], ALU.subtract, [b], [b])
            for tt in range(NT):
                for t0 in range(tt + 1):
                    lhs = self.T_f[:] if t0 == tt else self.ones_f[:]
                    S.mm(self.pb[6][:, tt * 24:(tt + 1) * 24], lhs, lf[:, t0 * 24:(t0 + 1) * 24], t0 == 0, t0 == tt,
                         [b, self.cst_b], [self.pb_b[6]])
            self.dump("logf", lf[:], [b])
            S.copy("dve", lf[:], self.pb[6][:, 0:NT * 24], [self.pb_b[6]], [b])
            self.dump("Fcum", lf[:], [b])
            hb3 = hb[:].rearrange("p (t h) -> p t h", t=NT)
            srcs = [lf, t2, lf]
            for k in range(3):
                cur = srcs[k]
                S.copy("dve", hb[:], cur[:], [b], [b])
                S.copy("dve", Fx.P4[:, :, :, k], hb3, [b], [Fx.b])
                S.ts("dve", Fx.N4[:, :, :, k], hb3, -1.0, None, ALU.mult, None, [b], [Fx.b])
                if k < 2:
                    S.copy("dve", t1[:], hb[:], [b], [b])
                    dst = t2 if k == 0 else lf
                    S.tt("dve", dst[:], cur[:], t1[:], ALU.subtract, [b], [b])
            S.st = old
            S.barrier()
        return Fx

    def fox_aug(self, A, Fx, h, QA, KA, QA_b, KA_b):
        S = self.S
        S.copy("pool", A.Taug3[:, :, 64:67], Fx.P4[:, :, h, :], [Fx.b], [A.Taug_b])
        S.copy("pool", A.Taug3[:, :, 96:99], Fx.N4[:, :, h, :], [Fx.b], [A.Taug_b])
        for half in range(2):
            pbi = 6 + half
            pv = self.pb[pbi][:].bitcast(BF16)
            for t8 in range(8):
                tt = half * 8 + t8
                S.tr(pv[0:99, t8 * 128:(t8 + 1) * 128], A.Taug3[:, tt, 0:99], self.ident[:], [A.Taug_b, self.cst_b],
                     [self.pb_b[pbi]])
            S.copy("dve", QA[64:67, half * 1024:(half + 1) * 1024], pv[64:67, :], [self.pb_b[pbi]], [QA_b])
            S.copy("dve", KA[96:99, half * 1024:(half + 1) * 1024], pv[96:99, :], [self.pb_b[pbi]], [KA_b])

    def zero_chunks(self, chunks):
        for ch in chunks:
            for half in range(2):
                self.S.memset("pool", self.yT3[:, ch, half * 1024:(half + 1) * 1024], 0.0, [self.yT_b[ch][half]])

    def phase_a_alloc(self, l):
        S = self.S
        P = type("PA", (), {})()
        P.mod, P.mod_b = self.load_modbc(l, 0, 2)
        P.junk = S.sb("pa_junk", [128, D], BF16)
        P.tmpf = [S.sb("pa_tmpf%d" % i, [128, D], F32) for i in range(2)]
        P.hb = [S.sb("pa_hb%d" % i, [128, D], BF16) for i in range(2)]
        P.jb = Buf("junk")
        P.tb = [Buf("tmpf%d" % i) for i in range(2)]
        P.hbb = [Buf("hb%d" % i) for i in range(2)]
        return P

    def phase_a_sq(self, P, tt, xi, xb):
        sl = tt % 2
        sb_ = self.stat_bs[sl]
        self.S.act(P.junk[:], xi[:], AF.Square, [xb], [P.jb, sb_], accum_out=self.scol(sl, 0))

    def phase_a_rest(self, P, tt, xi, xb):
        self.phase_a_rest1(P, tt)
        self.phase_a_rest2(P, tt, xi, xb)

    def phase_a_rest1(self, P, tt):
        sl = tt % 2
        self.rstd_from_ss(self.scol(sl, 0), D, [], slot=sl)

    def phase_a_rest2(self, P, tt, xi, xb):
        S = self.S
        sl = tt % 2
        sb_ = self.stat_bs[sl]
        rstd = self.scol(sl, 10)
        S.stt(P.tmpf[sl][:], xi[:], rstd, P.mod[:, D:2 * D], ALU.mult, ALU.mult, [xb, sb_, P.mod_b], [P.tb[sl]])
        S.tt("dve", P.hb[sl][:], P.tmpf[sl][:], P.mod[:, 0:D], ALU.add, [P.tb[sl], P.mod_b], [P.hbb[sl]])
        pbi = 6 + (tt % 2)
        pv = self.pb[pbi][:].bitcast(BF16)
        for fc in range(8):
            S.tr(pv[:, fc * 128:(fc + 1) * 128], P.hb[sl][:, fc * 128:(fc + 1) * 128], self.ident[:],
                 [P.hbb[sl], self.cst_b], [self.pb_b[pbi]])
        S.copy("act", self.hT3[:, :, tt * 128:(tt + 1) * 128], pv.rearrange("p (c t) -> p c t", c=8),
               [self.pb_b[pbi]], [self.hT_b[tt]])

    def run_phase_a(self, l, xsrc):
        S = self.S
        with ExitStack() as st:
            S.st, old = st, S.st
            P = self.phase_a_alloc(l)
            xin = [S.sb("xin%d" % i, [128, D], F32) for i in range(3)]
            xb3 = [Buf("xina%d" % i) for i in range(3)]

            def ld3(tt):
                sl = tt % 3
                S.dma("sp", xin[sl][:], xsrc[tt * 128:(tt + 1) * 128, :], [], [xb3[sl]], sem="xina%d" % sl)
            ld3(0)
            ld3(1)
            self.phase_a_sq(P, 0, xin[0], xb3[0])
            for tt in range(NT):
                if tt + 2 < NT:
                    ld3(tt + 2)
                if tt + 1 < NT:
                    self.phase_a_sq(P, tt + 1, xin[(tt + 1) % 3], xb3[(tt + 1) % 3])
                self.phase_a_rest(P, tt, xin[tt % 3], xb3[tt % 3])
            S.st = old
            S.barrier()
        S.barrier()

    def run_phase_c(self, l, xsrc, xdst, fuse_next=False):
        S = self.S
        S.barrier()
        with ExitStack() as st:
            S.st, old = st, S.st
            if fuse_next:
                wt = S.sb("wout", [128, 16 * D], BF16)
                wout = wt[:].rearrange("p (c n) -> p c n", c=16)
            else:
                wout = self.hT[:].rearrange("p (c n) -> p c n", c=16)
            wout_b = Buf("wout")
            res = [S.sb("pc_res%d" % i, [128, D], F32) for i in range(2)]
            self.xin = [S.sb("xin%d" % i, [128, D], F32) for i in range(2)]
            gp, gp_b = self.load_modbc(l, 2, 3)
            PA = self.phase_a_alloc(l + 1) if fuse_next else None
            self.phase_c(l, xsrc, xdst, wout, wout_b, res, gp, gp_b, PA)
            S.st = old
            S.barrier()
        S.barrier()

    def layer0(self):
        S = self.S
        self.run_phase_a(0, self.x)
        with ExitStack() as st:
            S.st, old = st, S.st
            if self.skip_ssd:
                self.zero_chunks(range(0, 8))
            else:
                with ExitStack() as st2:
                    S.st = st2
                    self.ssd_layer(0)
                    S.st = st
                S.barrier()
            A = self.attn_alloc("moba")
            for g in range(2):
                self.attn_group(A, 0, g, E_ZB, E_Q, E_K, E_V, 8 + 4 * g, ngroups=2)
            S.st = old
            S.barrier()
        self.run_phase_c(0, self.x, self.x1, fuse_next=self.fuse_ca)

    def layer1(self):
        S = self.S
        if not self.fuse_ca:
            self.run_phase_a(1, self.x1)
        with ExitStack() as st:
            S.st, old = st, S.st
            if self.skip_s5:
                self.zero_chunks(range(12, 16))
            else:
                with ExitStack() as st2:
                    S.st = st2
                    self.s5_layer(1)
                    S.st = st
                S.barrier()
            Fx = self.fox_prep(1)
            A = self.attn_alloc("fox")
            for g in range(3):
                self.attn_group(A, 1, g, O_ZC, O_Q, O_K, O_V, 4 * g, fox=Fx, ngroups=3)
            self.dump("Fp", Fx.P[:], [Fx.b])
            S.st = old
            S.barrier()
        self.run_phase_c(1, self.x1, self.out)

    def dump(self, name, ap, r):
        if name in self.dbg_out:
            self.S.dma("sp", self.dbg_out[name], ap, r, [], sem="dbg")

    def build(self, stop_after=None, mode="full"):
        self.fuse_ca = (mode == "full")
        self.declare()
        self.alloc_persistent()
        self.consts()
        self.defer_ada1 = (mode == "full") and not self.skip_ssd
        self.adaln([0] if self.defer_ada1 else [0, 1])
        if mode == "l1":
            self.x1 = self.x
        if mode == "s5":
            pass
        if mode == "l0":
            self.x1 = self.out
        if mode not in ("l1", "s5"):
            self.layer0()
            self.dump("yT0", self.yT[:], [b for bb in self.yT_b for b in bb])
        if mode == "s5":
            self.x1 = self.x
            self.run_phase_a(1, self.x1)
            with ExitStack() as st2:
                self.S.st, old = st2, self.S.st
                self.s5_layer(1)
                self.S.st = old
            self.S.barrier()
            self.dump("yT1", self.yT[:], [b for bb in self.yT_b for b in bb])
        elif mode != "l0":
            self.layer1()
            self.dump("yT1", self.yT[:], [b for bb in self.yT_b for b in bb])
        self.S.finish()
        self.st.close()
        return self.nc


def host_inputs(inputs, b):
    f = lambda a: np.ascontiguousarray(np.asarray(a, dtype=np.float32))
    m = {
        "x": f(inputs["x"][b]),
        "c_t": f(np.asarray(inputs["c"][b]).reshape(8, 128).T),
        "ada_w": f(inputs["ada_w"]), "ada_b": f(inputs["ada_b"]),
        "pre_g": f(inputs["pre_g"]), "post_g": f(inputs["post_g"]),
        "even_in_w": f(inputs["even_in_w"][0]), "odd_in_w": f(inputs["odd_in_w"][0]),
        "even_out_w": f(inputs["even_out_w"][0]), "odd_out_w": f(inputs["odd_out_w"][0]),
        "odd_fgate_b": f(inputs["odd_fgate_b"]).reshape(1, 24),
    }
    cw = f(inputs["even_conv_w"][0])
    cwl = cw.T.reshape(12, 128, 4).transpose(1, 0, 2).reshape(128, 48)
    cbl = f(inputs["even_conv_b"][0]).reshape(12, 128).T
    m["conv_p"] = f(np.concatenate([cwl, cbl], axis=1))
    m["ssd_rows"] = f(np.concatenate([inputs["even_dt_bias"][0], inputs["even_a_log"][0], inputs["even_d_skip"][0]])).reshape(1, 48)
    m["even_norm_g"] = f(inputs["even_norm_g"][0]).reshape(1, 1024)
    lre = f(inputs["odd_lam_re"][0]).reshape(16, 128).T
    lim = f(inputs["odd_lam_im"][0]).reshape(16, 128).T
    ldt = np.repeat(f(inputs["odd_log_dt"][0]), 64).reshape(16, 128).T
    m["s5_state"] = f(np.concatenate([lre, lim, ldt], axis=1))
    m["s5_vec"] = f(np.concatenate([f(inputs["odd_d_skip"][0]).reshape(4, 128).T, f(inputs["odd_glu_b"][0]).reshape(4, 128).T], axis=1))
    br = f(inputs["odd_b_re"][0]).reshape(16, 128, 16).transpose(1, 0, 2).reshape(128, 256)
    bi = f(inputs["odd_b_im"][0]).reshape(16, 128, 16).transpose(1, 0, 2).reshape(128, 256)
    m["s5_B"] = f(np.concatenate([br, bi], axis=1))
    cr = f(inputs["odd_c_re"][0]).reshape(16, 2, 16, 64).transpose(1, 3, 0, 2).reshape(128, 256)
    ci = f(inputs["odd_c_im"][0]).reshape(16, 2, 16, 64).transpose(1, 3, 0, 2).reshape(128, 256)
    m["s5_C"] = f(np.concatenate([cr, ci], axis=1))
    m["odd_glu_w"] = f(inputs["odd_glu_w"][0])
    return m


def kernel(**inputs):
    prog = Prog()
    nc = prog.build()
    in_maps = [host_inputs(inputs, b) for b in range(8)]
    res = run_bass_kernel_spmd(nc, in_maps, core_ids=list(range(8)))
    return np.stack([np.asarray(res.results[b]["out"], dtype=np.float32) for b in range(8)], axis=0)


def ssd_layer(self, l=0):
    S = self.S
    hT3 = self.hT3
    xsT = self.yT[:, 8 * L:16 * L].rearrange("p (t n) -> p t n", t=NT)
    xs_b = Buf("xsT")
    Bt = S.sb("Bt", [128, NT * 256], BF16); Bt3 = Bt[:].rearrange("p (t n) -> p t n", t=NT)
    BT = S.sb("BT", [128, 2 * L], BF16); BT3 = BT[:].rearrange("p (g t) -> p g t", g=2)
    CT = S.sb("CT", [128, 2 * L], BF16); CT3 = CT[:].rearrange("p (g t) -> p g t", g=2)
    bc_b = Buf("BC")
    dt = S.sb("dt", [128, NT * 16], F32); dt3 = dt[:].rearrange("p (t h) -> p t h", t=NT)
    adt = S.sb("adt", [128, NT * 16], F32); adt3 = adt[:].rearrange("p (t h) -> p t h", t=NT)
    dt_b = Buf("dt")
    prm = S.sb("ssd_prm", [128, 12 * 4 + 12 + 16 * 3], F32)
    prm_b = Buf("ssd_prm")
    S.dma("sp", prm[:, 0:60], self.conv_p, [], [prm_b], sem="misc")
    S.dma("sp", prm[:, 60:108], self.ssd_rows.to_broadcast([128, 48]), [], [prm_b], sem="misc")
    cw = lambda ch, k: prm[:, ch * 4 + k:ch * 4 + k + 1]
    cb = lambda ch: prm[:, 48 + ch:49 + ch]
    dtb, aneg, Dh = prm[:, 60:76], prm[:, 76:92], prm[:, 92:108]
    S.act(aneg, aneg, AF.Exp, [prm_b], [prm_b])
    S.ts("dve", aneg, aneg, -1.0, None, ALU.mult, None, [prm_b], [prm_b])
    Wd, Wd_b = self.load_w(l, E_DT, 16)
    for tt in range(NT):
        for kc in range(8):
            S.mm(self.pb[7][:, tt * 16:(tt + 1) * 16], hT3[:, kc, tt * 128:(tt + 1) * 128], Wd[:, kc, :], kc == 0, kc == 7,
                 [self.hT_b[tt], Wd_b], [self.pb_b[7]])
    S.tt("dve", dt3, self.pb[7][:, 0:256].rearrange("p (t h) -> p t h", t=NT), dtb.unsqueeze(1).to_broadcast([128, NT, 16]),
         ALU.add, [self.pb_b[7], prm_b], [dt_b])
    S.ts("dve", adt[:], dt[:], -1.0, None, ALU.mult, None, [dt_b], [dt_b])
    S.tt("dve", adt[:], adt[:], dt[:], ALU.max, [dt_b], [dt_b])
    S.act(adt[:], adt[:], AF.Exp, [dt_b], [dt_b], scale=-1.0)
    S.act(adt[:], adt[:], AF.Ln, [dt_b], [dt_b], bias=1.0)
    S.ts("dve", dt[:], dt[:], 0.0, None, ALU.max, None, [dt_b], [dt_b])
    S.tt("dve", dt[:], dt[:], adt[:], ALU.add, [dt_b], [dt_b])
    S.tt("dve", adt3, dt3, aneg.unsqueeze(1).to_broadcast([128, NT, 16]), ALU.mult, [dt_b, prm_b], [dt_b])
    self.dump("dt", dt[:], [dt_b])
    self.dump("adt", adt[:], [dt_b])
    with ExitStack() as st:
        S.st, old = st, S.st
        U = [S.sb("convU%d" % i, [128, 3 + 1024], F32) for i in range(2)]
        U_b = [Buf("U%d" % i) for i in range(2)]
        acc = S.sb("convacc", [128, 1024], F32); acc_b = Buf("acc")
        co = [S.sb("convo%d" % i, [128, 1024], BF16) for i in range(2)]
        co_b = [Buf("co%d" % i) for i in range(2)]
        Ws = [self.load_w(l, E_XS + wt * 512, 512) for wt in range(3)]
        work = [(wt, c4, half) for wt in range(3) for c4 in range(4) for half in range(2)]

        def st_mm(k):
            wt, c4, half = work[k]
            W, W_b = Ws[wt]
            u, ub = U[k % 2], U_b[k % 2]
            pu, pub = U[(k + 1) % 2], U_b[(k + 1) % 2]
            if half == 0:
                S.memset("pool", u[:, 0:3], 0.0, [ub])
            else:
                S.copy("pool", u[:, 0:3], pu[:, 1024:1027], [pub], [ub])
            for t2 in range(2):
                tb = half * 2 + t2
                pbi = 4 + (tb % 2)
                for kc in range(8):
                    S.mm(self.pb[pbi][:], W[:, kc, c4 * 128:(c4 + 1) * 128], hT3[:, kc, tb * 512:(tb + 1) * 512],
                         kc == 0, kc == 7, [self.hT_b[4 * tb + i] for i in range(4)] + [W_b], [self.pb_b[pbi]])
                S.copy("act", u[:, 3 + t2 * 512:3 + (t2 + 1) * 512], self.pb[pbi][:], [self.pb_b[pbi]], [ub])

        def st_conv(k):
            wt, c4, half = work[k]
            ch = wt * 4 + c4
            u, ub = U[k % 2], U_b[k % 2]
            S.ts("pool", acc[:], u[:, 0:1024], cw(ch, 0), cb(ch), ALU.mult, ALU.add, [ub, prm_b], [acc_b])
            for kk in range(1, 4):
                S.stt(acc[:], u[:, kk:kk + 1024], cw(ch, kk), acc[:], ALU.mult, ALU.add, [ub, prm_b, acc_b], [acc_b])
            if ch < 8:
                S.act(co[k % 2][:], acc[:], AF.Silu, [acc_b], [co_b[k % 2]])
            elif ch < 10:
                S.act(BT3[:, ch - 8, half * 1024:(half + 1) * 1024], acc[:], AF.Silu, [acc_b], [bc_b])
            else:
                S.act(CT3[:, ch - 10, half * 1024:(half + 1) * 1024], acc[:], AF.Silu, [acc_b], [bc_b])

        def st_tail(k):
            wt, c4, half = work[k]
            ch = wt * 4 + c4
            if ch >= 10:
                return
            pbi = 6 + (k % 2)
            pv = self.pb[pbi][:].bitcast(BF16)
            if ch < 8:
                o, ob = co[k % 2], co_b[k % 2]
                for t8 in range(8):
                    S.tr(pv[:, t8 * 128:(t8 + 1) * 128], o[:, t8 * 128:(t8 + 1) * 128], self.ident[:], [ob, self.cst_b],
                         [self.pb_b[pbi]])
                S.copy("dve", xsT[:, half * 8:(half + 1) * 8, ch * 128:(ch + 1) * 128],
                       pv.rearrange("p (t n) -> p t n", t=8), [self.pb_b[pbi]], [xs_b])
            else:
                g = ch - 8
                for t8 in range(8):
                    tt = half * 8 + t8
                    S.tr(pv[:, t8 * 128:(t8 + 1) * 128], BT3[:, g, tt * 128:(tt + 1) * 128], self.ident[:],
                         [bc_b, self.cst_b], [self.pb_b[pbi]])
                S.copy("dve", Bt3[:, half * 8:(half + 1) * 8, g * 128:(g + 1) * 128],
                       pv.rearrange("p (t n) -> p t n", t=8), [self.pb_b[pbi]], [bc_b])

        nw = len(work)
        ada = self.adaln_items(1, 1) if self.defer_ada1 else []
        st_mm(0)
        for k in range(nw):
            if k + 1 < nw:
                st_mm(k + 1)
            st_conv(k)
            if k > 0:
                st_tail(k - 1)
            if ada and k % 3 == 1:
                ada.pop(0)()
        st_tail(nw - 1)
        for it in ada:
            it()
        S.st = old
        S.barrier()
    self.dump("xsT", self.yT[:, 8 * L:16 * L], [xs_b])
    self.dump("BT", BT[:], [bc_b])
    self.dump("CT", CT[:], [bc_b])
    self.dump("Bt", Bt[:], [bc_b])
    with ExitStack() as st:
        S.st, old = st, S.st
        Wz0, Wz0_b = self.load_w(l, E_ZA, 512)
        Wz1, Wz1_b = self.load_w(l, E_ZA + 512, 512)
        ng = S.sb("ngbc", [128, 1024], F32); ng_b = Buf("ng")
        S.dma("sp", ng[:], self.even_norm_g.to_broadcast([128, 1024]), [], [ng_b], sem="misc")
        sz = S.sb("ssd_sz", [128, 1024], BF16); sz_b = Buf("sz")
        rhs_all = S.sb("rhs_all", [128, 8 * 128], F32); rhs_b = Buf("rhs_all")
        Ef = S.sb("Ef", [128, 512], F32); Ef_b = Buf("Ef")
        M = [S.sb("Mh%d" % i, [128, 512], BF16) for i in range(2)]; M_b = [Buf("M%d" % i) for i in range(2)]
        CBm = S.sb("CBm", [128, 128], F32); CBm_b = Buf("CBm")
        xdt = S.sb("xdt", [128, 1024], BF16); xdt_b = Buf("xdt")
        xds = S.sb("xds", [128, 1024], BF16); xds_b = Buf("xds")
        t1 = S.sb("ssd_t1", [128, 512], F32); t1_b = Buf("t1")
        yraw = S.sb("yraw", [128, 1024], F32); yraw_b = Buf("yraw")
        St = S.sb("Sstate", [128, 1024], F32); St_b = Buf("S")
        Sb = S.sb("Sbf", [128, 1024], BF16); Sb_b = Buf("Sb")
        tS = S.sb("tmpS", [128, 512], F32); tS_b = Buf("tS")
        ya = S.sb("ssd_ya", [128, 1024], BF16); ya_b = Buf("ya")
        sm = S.sb("ssd_small", [128, 16 * 6], F32); sm_b = Buf("ssd_small")
        acum, tot, eac, ds, dch, wds = [sm[:, i * 16:(i + 1) * 16] for i in range(6)]
        S.memset("pool", St[:], 0.0, [St_b])
        S.memset("pool", Sb[:], 0.0, [Sb_b])
        def front(c):
            k = c % 2
            tl = slice(c * 128, (c + 1) * 128)
            acum, tot, eac, ds, dch, wds = [smk[k][:, i * 16:(i + 1) * 16] for i in range(6)]
            for nb, (Wz, Wz_b) in enumerate([(Wz0, Wz0_b), (Wz1, Wz1_b)]):
                pbi = 4 + nb
                for kc in range(8):
                    S.mm(self.pb[pbi][:], hT3[:, kc, tl], Wz[:, kc, :], kc == 0, kc == 7, [self.hT_b[c], Wz_b], [self.pb_b[pbi]])
                S.act(thb[nb][:], self.pb[pbi][:], AF.Tanh, [self.pb_b[pbi]], [thb_b[nb]], scale=0.5)
                S.stt(szk[k][:, nb * 512:(nb + 1) * 512], thb[nb][:], 1.0, self.pb[pbi][:], ALU.add, ALU.mult,
                      [thb_b[nb], self.pb_b[pbi]], [szk_b[k]])
            S.mm(self.pb[7][:, 0:16], self.T_f[:], adt3[:, c, :], True, True, [dt_b, self.cst_b], [self.pb_b[7]])
            S.mm(self.pb[7][:, 16:32], self.ones_f[:], adt3[:, c, :], True, True, [dt_b, self.cst_b], [self.pb_b[7]])
            S.copy("dve", smk[k][:, 0:32], self.pb[7][:, 0:32], [self.pb_b[7]], [smk_b[k]])
            S.act(eac, acum, AF.Exp, [smk_b[k]], [smk_b[k]])
            S.tt("dve", ds, tot, acum, ALU.subtract, [smk_b[k]], [smk_b[k]])
            S.act(ds, ds, AF.Exp, [smk_b[k]], [smk_b[k]])
            S.act(dch, tot, AF.Exp, [smk_b[k]], [smk_b[k]])
            S.tt("dve", wds, ds, dt3[:, c, :], ALU.mult, [smk_b[k], dt_b], [smk_b[k]])
            xs_c = xsT[:, c, :].rearrange("p (h d) -> p h d", h=16)
            S.tt("pool", xdtk[k][:].rearrange("p (h d) -> p h d", h=16), xs_c, dt3[:, c, :].unsqueeze(2).to_broadcast([128, 16, 64]),
                 ALU.mult, [xs_b, dt_b], [xdtk_b[k]])
            S.tt("dve", xdsk[k][:].rearrange("p (h d) -> p h d", h=16), xs_c, wds.unsqueeze(2).to_broadcast([128, 16, 64]),
                 ALU.mult, [xs_b, smk_b[k]], [xdsk_b[k]])
            for g in range(2):
                S.mm(self.pb[7][:, 128:256], BT3[:, g, tl], CT3[:, g, tl], True, True, [bc_b], [self.pb_b[7]])
                S.tt("dve", CBm[:], self.pb[7][:, 128:256], self.T_f[:], ALU.mult, [self.pb_b[7], self.cst_b], [CBm_b])
                S.tt("pool", rhs_all[:].rearrange("p (h n) -> p h n", h=8), self.T_f[:].unsqueeze(1).to_broadcast([128, 8, 128]),
                     adt3[:, c, g * 8:(g + 1) * 8].unsqueeze(2).to_broadcast([128, 8, 128]), ALU.mult,
                     [dt_b, self.cst_b], [rhs_b])
                for h4 in range(2):
                    pbi = h4
                    for i in range(4):
                        hh = h4 * 4 + i
                        S.mm(self.pb[pbi][:, i * 128:(i + 1) * 128], self.U_f[:], rhs_all[:, hh * 128:(hh + 1) * 128], True, True,
                             [rhs_b, self.cst_b], [self.pb_b[pbi]])
                    S.act(Ef[:], self.pb[pbi][:], AF.Exp, [self.pb_b[pbi]], [Ef_b])
                    mi = k * 4 + g * 2 + h4
                    S.tt("dve", Mk[mi][:].rearrange("p (h n) -> p h n", h=4), Ef[:].rearrange("p (h n) -> p h n", h=4),
                         CBm[:].unsqueeze(1).to_broadcast([128, 4, 128]), ALU.mult, [Ef_b, CBm_b], [Mk_b[mi]])

        def back(c):
            k = c % 2
            tl = slice(c * 128, (c + 1) * 128)
            acum, tot, eac, ds, dch, wds = [smk[k][:, i * 16:(i + 1) * 16] for i in range(6)]
            xs_c = xsT[:, c, :].rearrange("p (h d) -> p h d", h=16)
            for g in range(2):
                gs = slice(g * 512, (g + 1) * 512)
                for hh in range(8):
                    h = g * 8 + hh
                    mi = k * 4 + g * 2 + hh // 4
                    S.mm(self.pb[2][:, hh * 64:(hh + 1) * 64], Mk[mi][:, (hh % 4) * 128:(hh % 4 + 1) * 128],
                         xdtk[k][:, h * 64:(h + 1) * 64], True, True, [Mk_b[mi], xdtk_b[k]], [self.pb_b[2]])
                if c > 0:
                    S.mm(self.pb[3][:], CT3[:, g, tl], Sb[:, gs], True, True, [bc_b, Sb_b], [self.pb_b[3]])
                    S.tt("dve", t1[:].rearrange("p (h d) -> p h d", h=8), self.pb[3][:].rearrange("p (h d) -> p h d", h=8),
                         eac[:, g * 8:(g + 1) * 8].unsqueeze(2).to_broadcast([128, 8, 64]), ALU.mult,
                         [self.pb_b[3], smk_b[k]], [t1_b])
                    S.tt("dve", yraw[:, gs], self.pb[2][:], t1[:], ALU.add, [self.pb_b[2], t1_b], [yraw_b])
                else:
                    S.copy("dve", yraw[:, gs], self.pb[2][:], [self.pb_b[2]], [yraw_b])
                S.mm(self.pb[3][:], Bt3[:, c, g * 128:(g + 1) * 128], xdsk[k][:, gs], True, True, [bc_b, xdsk_b[k]], [self.pb_b[3]])
                S.tt("pool", tS[:].rearrange("p (h d) -> p h d", h=8), St[:, gs].rearrange("p (h d) -> p h d", h=8),
                     dch[:, g * 8:(g + 1) * 8].unsqueeze(2).to_broadcast([128, 8, 64]), ALU.mult, [St_b, smk_b[k]], [tS_b])
                S.tt("dve", St[:, gs], tS[:], self.pb[3][:], ALU.add, [tS_b, self.pb_b[3]], [St_b])
                S.copy("pool", Sb[:, gs], St[:, gs], [St_b], [Sb_b])
            S.tt("dve", xD[:].rearrange("p (h d) -> p h d", h=16), xs_c, Dh.unsqueeze(2).to_broadcast([128, 16, 64]), ALU.mult,
                 [xs_b, prm_b], [xD_b])
            S.tt("dve", yraw[:], yraw[:], xD[:], ALU.add, [yraw_b, xD_b], [yraw_b])
            S.tt("pool", yraw[:], yraw[:], szk[k][:], ALU.mult, [yraw_b, szk_b[k]], [yraw_b])
            S.act(yak[k][:], yraw[:], AF.Square, [yraw_b], [yak_b[k], self.stat_bs[k]], accum_out=self.scol(k, 0))
            S.ts("dve", self.scol(k, 8), self.scol(k, 0), 1.0 / 1024, 4.0 * EPS, ALU.mult, ALU.add, [self.stat_bs[k]], [self.stat_bs[k]])
            S.tt("pool", self.scol(k, 10), self.scol(k, 8), mhalf[:], ALU.pow, [self.stat_bs[k], prm_b], [self.stat_bs[k]])
            rstd = self.scol(k, 10)
            S.stt(yak[k][:], yraw[:], rstd, ng[:], ALU.mult, ALU.mult, [yraw_b, self.stat_bs[k], ng_b], [yak_b[k]])

        def back2(c):
            k = c % 2
            tl = slice(c * 128, (c + 1) * 128)
            pbi = 6
            pv = self.pb[pbi][:].bitcast(BF16)
            for fc in range(8):
                S.tr(pv[:, fc * 128:(fc + 1) * 128], yak[k][:, fc * 128:(fc + 1) * 128], self.ident[:], [yak_b[k], self.cst_b],
                     [self.pb_b[pbi]])
            S.copy("dve", self.yT3[:, 0:8, tl], pv.rearrange("p (c t) -> p c t", c=8), [self.pb_b[pbi]],
                   [self.yT_b[ch][c // 8] for ch in range(8)])

        yak = [ya, S.sb("ssd_ya2", [128, 1024], BF16)]; yak_b = [Buf("ya0"), Buf("ya1")]
        thb = [S.sb("ssd_th%d" % i, [128, 512], BF16) for i in range(2)]; thb_b = [Buf("th0"), Buf("th1")]
        mhalf = S.sb("ssd_mhalf", [128, 1], F32)
        S.memset("pool", mhalf[:], -0.5, [prm_b])
        szk = [sz, S.sb("ssd_sz2", [128, 1024], BF16)]; szk_b = [Buf("sz0"), Buf("sz1")]
        smk = [sm, S.sb("ssd_small2", [128, 16 * 6], F32)]; smk_b = [Buf("sm0"), Buf("sm1")]
        xdtk = [xdt, S.sb("xdt2", [128, 1024], BF16)]; xdtk_b = [Buf("xdt0"), Buf("xdt1")]
        xdsk = [xds, S.sb("xds2", [128, 1024], BF16)]; xdsk_b = [Buf("xds0"), Buf("xds1")]
        Mk = M + [S.sb("Mh%d" % i, [128, 512], BF16) for i in range(2, 8)]; Mk_b = [Buf("M%d" % i) for i in range(8)]
        xD = S.sb("ssd_xD", [128, 1024], BF16); xD_b = Buf("xD")
        front(0)
        for c in range(NT):
            if c + 1 < NT:
                front(c + 1)
            if c > 0:
                back2(c - 1)
            back(c)
        back2(NT - 1)
        S.st = old
        S.barrier()
    S.barrier()


Prog.ssd_layer = ssd_layer


TWO_PI = 6.283185307179586
C1_2PI = 6.28125
C2_2PI = TWO_PI - 6.28125
MAGIC = 12582912.0


def s5_layer(self, l=1):
    S = self.S
    hT3 = self.hT3
    ydg3 = self.yT3[:, 12:16, :]
    ydg_b = Buf("ydg")
    prm = S.sb("s5_prm", [128, 48 + 8], F32); prm_b = Buf("s5prm")
    S.dma("sp", prm[:, 0:48], self.s5_state, [], [prm_b], sem="misc")
    S.dma("sp", prm[:, 48:56], self.s5_vec, [], [prm_b], sem="misc")
    lr, li, ldt = prm[:, 0:16], prm[:, 16:32], prm[:, 32:48]
    dsk, glb = prm[:, 48:52], prm[:, 52:56]
    BbT = S.sb("s5_BbT", [128, 2 * 16 * 128], BF16); BbT_b = Buf("BbT")
    BbT4 = BbT[:].rearrange("p (r j s) -> p r j s", r=2, j=16)
    CTp = S.sb("s5_CTp", [128, 2 * 16 * 128], BF16); CT_b = Buf("CTp")
    cs = S.sb("s5_cs", [128, 16 * 16], F32); cs_b = Buf("s5cs")
    _st_setup = ExitStack()
    S.st, _old_setup = _st_setup, S.st
    Bst = S.sb("s5_Bst", [128, 2 * 256], F32); Cst = S.sb("s5_Cst", [128, 2 * 256], F32)
    S.dma("sp", Bst[:], self.s5_B, [], [prm_b], sem="misc")
    S.dma("sp", Cst[:], self.s5_C, [], [prm_b], sem="misc")
    v = lambda i: cs[:, i * 16:(i + 1) * 16]
    dtv, mag, ang, cosA, sinA, ar1, ai, qr, qi, c128, s128, tA, tB, tC = [v(i) for i in range(14)]

    def reduce_inplace(X, K, r, w):
        S.ts("dve", K, X, 1.0 / TWO_PI, MAGIC, ALU.mult, ALU.add, r, w)
        S.ts("dve", K, K, -MAGIC, None, ALU.add, None, w, w)
        S.stt(X, K, -C1_2PI, X, ALU.mult, ALU.add, w, w)
        S.stt(X, K, -C2_2PI, X, ALU.mult, ALU.add, w, w)
        S.ts("dve", X, X, 3.14159, -3.14159, ALU.min, ALU.max, w, w)

    R, W = [prm_b, cs_b], [cs_b]
    S.act(dtv, ldt, AF.Exp, [prm_b], W)
    S.tt("dve", tA, lr, dtv, ALU.mult, R, W)
    S.act(mag, tA, AF.Exp, R, W)
    S.tt("dve", ang, li, dtv, ALU.mult, R, W)
    S.copy("dve", tA, ang, R, W)
    reduce_inplace(tA, tC, R, W)
    S.act(sinA, tA, AF.Sin, R, W)
    S.ts("dve", tA, ang, 1.5707963267948966, None, ALU.add, None, R, W)
    reduce_inplace(tA, tC, R, W)
    S.act(cosA, tA, AF.Sin, R, W)
    S.ts("dve", tA, ang, 128.0, None, ALU.mult, None, R, W)
    S.copy("dve", tB, tA, R, W)
    reduce_inplace(tA, tC, R, W)
    S.act(s128, tA, AF.Sin, R, W)
    S.ts("dve", tB, tB, 1.5707963267948966, None, ALU.add, None, R, W)
    reduce_inplace(tB, tC, R, W)
    S.act(c128, tB, AF.Sin, R, W)
    S.tt("dve", s128, s128, mag, ALU.mult, R, W)
    S.tt("dve", c128, c128, mag, ALU.mult, R, W)
    S.tt("dve", ar1, mag, cosA, ALU.mult, R, W)
    S.ts("dve", ar1, ar1, -1.0, None, ALU.add, None, R, W)
    S.tt("dve", ai, mag, sinA, ALU.mult, R, W)
    S.tt("dve", tA, lr, lr, ALU.mult, R, W)
    S.tt("dve", tB, li, li, ALU.mult, R, W)
    S.tt("dve", tA, tA, tB, ALU.add, R, W)
    S.op("dve", lambda e: e.reciprocal(out=tC, in_=tA), R, W)
    S.tt("dve", qr, ar1, lr, ALU.mult, R, W)
    S.tt("dve", tA, ai, li, ALU.mult, R, W)
    S.tt("dve", qr, qr, tA, ALU.add, R, W)
    S.tt("dve", qr, qr, tC, ALU.mult, R, W)
    S.tt("dve", qi, ai, lr, ALU.mult, R, W)
    S.tt("dve", tA, ar1, li, ALU.mult, R, W)
    S.tt("dve", qi, qi, tA, ALU.subtract, R, W)
    S.tt("dve", qi, qi, tC, ALU.mult, R, W)
    Bst3 = [Bst[:, i * 256:(i + 1) * 256].rearrange("p (j c) -> p j c", j=16) for i in range(2)]
    Cst3 = [Cst[:, i * 256:(i + 1) * 256].rearrange("p (j c) -> p j c", j=16) for i in range(2)]
    bb = S.sb("s5_bb", [128, 2 * 256], F32); bb_b = Buf("s5bb")
    bb3 = [bb[:, i * 256:(i + 1) * 256].rearrange("p (j c) -> p j c", j=16) for i in range(2)]
    tmp = S.sb("s5_tmp", [128, 256], F32)
    tmp3 = tmp[:].rearrange("p (j c) -> p j c", j=16)
    qrb = qr.unsqueeze(2).to_broadcast([128, 16, 16]); qib = qi.unsqueeze(2).to_broadcast([128, 16, 16])
    RB = [prm_b, cs_b, bb_b]
    S.tt("dve", bb3[0], Bst3[0], qrb, ALU.mult, RB, [bb_b])
    S.tt("dve", tmp3, Bst3[1], qib, ALU.mult, RB, [bb_b])
    S.tt("dve", bb3[0], bb3[0], tmp3, ALU.subtract, RB, [bb_b])
    S.tt("dve", bb3[1], Bst3[1], qrb, ALU.mult, RB, [bb_b])
    S.tt("dve", tmp3, Bst3[0], qib, ALU.mult, RB, [bb_b])
    S.tt("dve", bb3[1], bb3[1], tmp3, ALU.add, RB, [bb_b])
    bpad = S.sb("s5_bpad", [128, 2 * 16 * 128], BF16)
    bpad6 = bpad[:].rearrange("p (r q m2 m a c) -> p r q m2 m a c", r=2, q=4, m2=4, m=4, a=2)
    S.memset("pool", bpad[:], 0.0, [bb_b])
    for ri in range(2):
        src5 = bb[:, ri * 256:(ri + 1) * 256].rearrange("p (q m c) -> p q m c", q=4, m=4)
        for a in range(2):
            for jm in range(4):
                S.copy("dve", bpad6[a * 64:(a + 1) * 64, ri, :, jm, jm, a, :], src5[a * 64:(a + 1) * 64, :, jm, :], [bb_b], [bb_b])
    for ri in range(2):
        for j4 in range(4):
            pbi = 4 + (ri * 4 + j4) % 4
            pv = self.pb[pbi][:].bitcast(BF16)
            for jj in range(4):
                j = j4 * 4 + jj
                S.tr(pv[:, jj * 128:(jj + 1) * 128], bpad[:, (ri * 16 + j) * 128:(ri * 16 + j + 1) * 128], self.ident[:],
                     [bb_b, self.cst_b], [self.pb_b[pbi]])
            S.copy("dve", BbT[:, (ri * 16 + j4 * 4) * 128:(ri * 16 + j4 * 4 + 4) * 128], pv[:, 0:512], [self.pb_b[pbi]], [BbT_b])
    CTp5 = CTp[:].rearrange("p (r j m a c) -> p r j m a c", r=2, j=16, m=4, a=2)
    S.memset("pool", CTp[:], 0.0, [CT_b])
    for ri in range(2):
        for a in range(2):
            for jm in range(4):
                srcv = Cst3[ri][a * 64:(a + 1) * 64].rearrange("p (q m) c -> p q m c", m=4)[:, :, jm, :]
                dstv = CTp5[a * 64:(a + 1) * 64, ri].rearrange("p (q m2) m a c -> p q m2 m a c", m2=4)[:, :, jm, jm, a, :]
                if ri == 0:
                    S.copy("dve", dstv, srcv, [prm_b], [CT_b])
                else:
                    S.ts("dve", dstv, srcv, -1.0, None, ALU.mult, None, [prm_b], [CT_b])
    CTp4 = CTp[:].rearrange("p (r j n) -> p r j n", r=2, j=16)
    S.st = _old_setup
    _st_setup.close()
    S.barrier()
    bidx = S.sb("s5_bidx", [128, 128], F32)
    S.op("dve", lambda e: e.tensor_tensor_scan(out=bidx[:], data0=self.ones_f[:], data1=self.ones_f[:], initial=-1.0,
                                                op0=ALU.mult, op1=ALU.add), [self.cst_b], [cs_b])
    stop = getattr(self, "s5_stop", 9)
    if stop <= 1:
        self.zero_chunks(range(12, 16))
        S.barrier()
        return
    with ExitStack() as st:
        S.st, old = st, S.st
        uT = S.sb("s5_uT", [128, 2 * L], BF16); uT3 = uT[:].rearrange("p (c t) -> p c t", c=2); uT_b = Buf("uT")
        cosT = S.sb("s5_cosT", [128, 1024], F32); sinT = S.sb("s5_sinT", [128, 1024], F32); rhoT = S.sb("s5_rhoT", [128, 1024], F32)
        tab_b = Buf("s5tab")
        wr = S.sb("s5_wr", [128, 1024], F32); wi = S.sb("s5_wi", [128, 1024], F32)
        tf = [S.sb("s5_tf%d" % i, [128, 512], F32) for i in range(4)]; tf_b = [Buf("tf%d" % i) for i in range(4)]
        rr = [S.sb("s5_rr%d" % i, [128, 1024], F32) for i in range(2)]; rim = [S.sb("s5_ri%d" % i, [128, 1024], F32) for i in range(2)]
        rr_b = [Buf("rr%d" % i) for i in range(2)]; ri_b = [Buf("ri%d" % i) for i in range(2)]
        tbw = [S.sb("s5_tb%d" % i, [128, 1024], BF16) for i in range(4)]; tbw_b = [Buf("tb%d" % i) for i in range(4)]
        kk = rr[0]
        wr_b, wi_b = Buf("wr"), Buf("wi")
        sr = S.sb("s5_sr", [128, 1024], BF16); si = S.sb("s5_si", [128, 1024], BF16)
        yt = S.sb("s5_yt", [128, 256], F32)
        yg = S.sb("s5_yg", [128, 256], F32)
        ini = S.sb("s5_ini", [128, 80], F32)
        CCSS = S.sb("s5_ccss", [128, 32], F32)
        Dg = S.sb("s5_Dg", [128, 2 * 128], BF16)
        w_b, t_b, r_b, s_b, y_b, ini_b = Buf("w"), Buf("t"), Buf("r"), Buf("s"), Buf("yt"), Buf("ini")
        v3 = lambda t: t[:].rearrange("p (j b) -> p j b", j=8)
        for ps in range(2):
            js = slice(ps * 8, (ps + 1) * 8)
            Wu, Wu_b = self.load_w(l, O_U + ps * 256, 256)
            for c2 in range(2):
                for tb in range(4):
                    pbi = 4 + (tb % 2)
                    for kc in range(8):
                        S.mm(self.pb[pbi][:], Wu[:, kc, c2 * 128:(c2 + 1) * 128], hT3[:, kc, tb * 512:(tb + 1) * 512], kc == 0, kc == 7,
                             [self.hT_b[4 * tb + i] for i in range(4)] + [Wu_b], [self.pb_b[pbi]])
                    S.copy("act", uT3[:, c2, tb * 512:(tb + 1) * 512], self.pb[pbi][:], [self.pb_b[pbi]], [uT_b])
            TW = [tab_b]
            S.tt("dve", v3(sinT), bidx[:].unsqueeze(1).to_broadcast([128, 8, 128]), ang[:, js].unsqueeze(2).to_broadcast([128, 8, 128]),
                 ALU.mult, [cs_b], TW)
            S.ts("dve", cosT[:], sinT[:], 1.5707963267948966, None, ALU.add, None, TW, TW)
            reduce_inplace(sinT[:], kk[:], TW, TW)
            reduce_inplace(cosT[:], kk[:], TW, TW)
            S.act(sinT[:], sinT[:], AF.Sin, TW, TW)
            S.act(cosT[:], cosT[:], AF.Sin, TW, TW)
            S.copy("dve", v3(rhoT), mag[:, js].unsqueeze(2).to_broadcast([128, 8, 128]), [cs_b], TW)
            S.memset("pool", v3(rhoT)[:, :, 0:1], 0.0, TW)
            S.memset("pool", ini[:], 0.0, [ini_b])
            S.copy("dve", CCSS[:, 0:8], c128[:, js], [cs_b], [cs_b])
            S.copy("dve", CCSS[:, 8:16], c128[:, js], [cs_b], [cs_b])
            S.ts("dve", CCSS[:, 16:24], s128[:, js], -1.0, None, ALU.mult, None, [cs_b], [cs_b])
            S.copy("dve", CCSS[:, 24:32], s128[:, js], [cs_b], [cs_b])
            for c2 in range(2):
                S.ts("dve", Dg[:, c2 * 128:(c2 + 1) * 128], self.ident[:], dsk[:, 2 * ps + c2:2 * ps + c2 + 1], None, ALU.mult, None,
                     [self.cst_b, prm_b], [cs_b])
            if stop <= 2:
                continue
            def st_bu(a):
                tl = slice(a * 128, (a + 1) * 128)
                for ri in range(2):
                    for j8 in range(8):
                        j = ps * 8 + j8
                        jq = j // 4
                        pbi = 2 * ri + j8 // 4
                        S.mm(self.pb[pbi][:, (j8 % 4) * 128:(j8 % 4 + 1) * 128], BbT4[:, ri, j, :],
                             uT3[:, jq - 2 * ps, tl], True, True, [BbT_b, uT_b], [self.pb_b[pbi]])

            def st_fwd(a):
                for hf in range(2):
                    cs_ = slice(hf * 512, (hf + 1) * 512)
                    S.tt("dve", tf[0][:], self.pb[hf][:], cosT[:, cs_], ALU.mult, [self.pb_b[hf], tab_b], [tf_b[0]])
                    S.tt("dve", tf[1][:], self.pb[2 + hf][:], sinT[:, cs_], ALU.mult, [self.pb_b[2 + hf], tab_b], [tf_b[1]])
                    S.tt("dve", wr[:, cs_], tf[0][:], tf[1][:], ALU.add, [tf_b[0], tf_b[1]], [wr_b])
                    S.tt("dve", tf[2][:], self.pb[2 + hf][:], cosT[:, cs_], ALU.mult, [self.pb_b[2 + hf], tab_b], [tf_b[2]])
                    S.tt("dve", tf[3][:], self.pb[hf][:], sinT[:, cs_], ALU.mult, [self.pb_b[hf], tab_b], [tf_b[3]])
                    S.tt("dve", wi[:, cs_], tf[2][:], tf[3][:], ALU.subtract, [tf_b[2], tf_b[3]], [wi_b])

            def st_scan(a):
                k = a % 2
                if a > 0:
                    S.tt("dve", ini[:, 48:64], ini[:, 0:16], CCSS[:, 0:16], ALU.mult, [ini_b, cs_b], [ini_b])
                    S.tt("dve", ini[:, 64:80], ini[:, 16:32], CCSS[:, 16:32], ALU.mult, [ini_b, cs_b], [ini_b])
                    S.tt("dve", ini[:, 48:64], ini[:, 48:64], ini[:, 64:80], ALU.add, [ini_b], [ini_b])
                    S.tt("dve", v3(wr)[:, :, 0], v3(wr)[:, :, 0], ini[:, 48:56], ALU.add, [wr_b, ini_b], [wr_b])
                    S.tt("dve", v3(wi)[:, :, 0], v3(wi)[:, :, 0], ini[:, 56:64], ALU.add, [wi_b, ini_b], [wi_b])
                S.op("dve", lambda e: e.tensor_tensor_scan(out=rr[k][:], data0=rhoT[:], data1=wr[:], initial=0.0, op0=ALU.mult,
                                                           op1=ALU.add), [wr_b, tab_b], [rr_b[k]])
                S.op("dve", lambda e: e.tensor_tensor_scan(out=rim[k][:], data0=rhoT[:], data1=wi[:], initial=0.0, op0=ALU.mult,
                                                           op1=ALU.add), [wi_b, tab_b], [ri_b[k]])
                S.copy("pool", ini[:, 0:48].rearrange("p (a b) -> p a b", b=24)[:, :, 0:8],
                       v3(rr[k])[:, :, 127].unsqueeze(1).to_broadcast([128, 2, 8]), [rr_b[k]], [ini_b])
                S.copy("pool", ini[:, 8:24].rearrange("p (a b) -> p a b", b=8),
                       v3(rim[k])[:, :, 127].unsqueeze(1).to_broadcast([128, 2, 8]), [ri_b[k]], [ini_b])

            def st_bwdmul(a):
                k = a % 2
                S.tt("dve", tbw[0][:], rr[k][:], cosT[:], ALU.mult, [rr_b[k], tab_b], [tbw_b[0]])
                S.stt(tbw[1][:], rim[k][:], -1.0, sinT[:], ALU.mult, ALU.mult, [ri_b[k], tab_b], [tbw_b[1]])
                S.tt("dve", tbw[2][:], rr[k][:], sinT[:], ALU.mult, [rr_b[k], tab_b], [tbw_b[2]])
                S.tt("dve", tbw[3][:], rim[k][:], cosT[:], ALU.mult, [ri_b[k], tab_b], [tbw_b[3]])

            def st_out(a):
                tl = slice(a * 128, (a + 1) * 128)
                for c2 in range(2):
                    n = 0
                    for jm in range(4):
                        j8 = c2 * 4 + jm
                        for ri, ti in ((0, 0), (0, 1), (1, 2), (1, 3)):
                            S.mm(self.pb[6][:, c2 * 128:(c2 + 1) * 128], CTp4[:, ri, ps * 8 + j8, :], tbw[ti][:, j8 * 128:(j8 + 1) * 128],
                                 n == 0, False, [CT_b, tbw_b[ti]], [self.pb_b[6]])
                            n += 1
                    S.mm(self.pb[6][:, c2 * 128:(c2 + 1) * 128], Dg[:, c2 * 128:(c2 + 1) * 128], uT3[:, c2, tl], False, True,
                         [cs_b, uT_b], [self.pb_b[6]])
                x = self.pb[6][:, 0:256]
                S.act(yg[:], x, AF.Square, [self.pb_b[6]], [y_b])
                S.ts("dve", yg[:], yg[:], 0.044715, 1.0, ALU.mult, ALU.add, [y_b], [y_b])
                S.tt("dve", yg[:], yg[:], x, ALU.mult, [y_b, self.pb_b[6]], [y_b])
                S.act(yt[:], yg[:], AF.Sigmoid, [y_b], [y_b], scale=1.5957691216057308)
                S.tt("dve", ydg3[:, 2 * ps:2 * ps + 2, tl], yt[:].rearrange("p (c t) -> p c t", c=2),
                     x.rearrange("p (c t) -> p c t", c=2), ALU.mult, [y_b, self.pb_b[6]], [ydg_b])

            na = NT if stop > 3 else 1
            st_bu(0); st_fwd(0); st_scan(0)
            for a in range(na):
                if a + 1 < na:
                    st_bu(a + 1)
                    st_fwd(a + 1)
                st_bwdmul(a)
                if a + 1 < na:
                    st_scan(a + 1)
                st_out(a)
        S.st = old
        S.barrier()
    if stop <= 4:
        for ch in range(12, 16):
            for half in range(2):
                self.yT_b[ch][half].w = ydg_b.w
        S.barrier()
        return
    with ExitStack() as st:
        S.st, old = st, S.st
        sg = [S.sb("s5_sg%d" % i, [128, 512], BF16) for i in range(2)]; sg_b = [Buf("sg%d" % i) for i in range(2)]
        szd = [S.sb("s5_szd%d" % i, [128, 512], BF16) for i in range(2)]; szd_b = [Buf("szd%d" % i) for i in range(2)]
        yo = [S.sb("s5_yo%d" % i, [128, 512], BF16) for i in range(2)]; yo_b = [Buf("yo%d" % i) for i in range(2)]
        Wzd, Wzd_b = self.load_w(l, O_ZD, 512)
        Wg = S.sb("s5_Wg", [128, 4 * 512], BF16); Wg_b = Buf("Wg")
        Wg3 = Wg[:].rearrange("p (c n) -> p c n", c=4)
        S.dma("pool", Wg3, self.glu_w.rearrange("(c p) n -> p c n", p=128), [], [Wg_b], sem="wg")
        k = 0
        for tb in range(4):
            ts_ = slice(tb * 512, (tb + 1) * 512)
            outs = []
            for oc in range(4):
                i = k % 2
                k += 1
                pbi = 4 + i
                for kc in range(4):
                    S.mm(self.pb[pbi][:], Wg3[:, kc, oc * 128:(oc + 1) * 128], ydg3[:, kc, ts_], kc == 0, kc == 3,
                         [Wg_b, ydg_b], [self.pb_b[pbi]])
                S.act(sg[i][:], self.pb[pbi][:], AF.Sigmoid, [self.pb_b[pbi], prm_b], [sg_b[i]], bias=glb[:, oc:oc + 1])
                pz = 6 + i
                for kc in range(8):
                    S.mm(self.pb[pz][:], Wzd[:, kc, oc * 128:(oc + 1) * 128], hT3[:, kc, ts_], kc == 0, kc == 7,
                         [self.hT_b[4 * tb + i2] for i2 in range(4)] + [Wzd_b], [self.pb_b[pz]])
                S.act(szd[i][:], self.pb[pz][:], AF.Silu, [self.pb_b[pz]], [szd_b[i]])
                S.tt("dve", sg[i][:], sg[i][:], szd[i][:], ALU.mult, [sg_b[i], szd_b[i]], [sg_b[i]])
                if oc < 3:
                    o = S.sb("s5_hold%d_%d" % (tb, oc), [128, 512], BF16)
                    ob = Buf("hold")
                    S.tt("dve", o[:], sg[i][:], ydg3[:, oc, ts_], ALU.mult, [sg_b[i], ydg_b], [ob])
                    outs.append((oc, o, ob))
                else:
                    S.tt("dve", ydg3[:, oc, ts_], sg[i][:], ydg3[:, oc, ts_], ALU.mult, [sg_b[i], ydg_b], [ydg_b])
            for oc, o, ob in outs:
                S.copy("pool", ydg3[:, oc, ts_], o[:], [ob], [ydg_b])
        S.st = old
        S.barrier()
    for ch in range(12, 16):
        for half in range(2):
            self.yT_b[ch][half].w = ydg_b.w
    S.barrier()


Prog.s5_layer = s5_layer


def _prep_items(self, A, l, g, hh, Wq, Wq_b, Wk, Wk_b, fox):
    S = self.S
    hT3 = self.hT3
    h = g * 8 + hh
    sl = h % 2
    QA, KA, QA_b, KA_b = A.QA[sl], A.KA[sl], A.QA_b[sl], A.KA_b[sl]
    items = []

    def proj(tb, which):
        def f():
            hr = [self.hT_b[4 * tb + i] for i in range(4)]
            if which == 0:
                pbi = 4
                for kc in range(8):
                    S.mm(self.pb[pbi][0:64, :], Wq[:, kc, hh * 64:(hh + 1) * 64], hT3[:, kc, tb * 512:(tb + 1) * 512],
                         kc == 0, kc == 7, hr + [Wq_b], [self.pb_b[pbi]])
                S.ts("dve", QA[0:64, tb * 512:(tb + 1) * 512], self.pb[pbi][0:64, :], 0.125, None, ALU.mult, None,
                     [self.pb_b[pbi]], [QA_b])
            else:
                pbi = 5
                for kc in range(8):
                    S.mm(self.pb[pbi][0:64, :], Wk[:, kc, hh * 64:(hh + 1) * 64], hT3[:, kc, tb * 512:(tb + 1) * 512],
                         kc == 0, kc == 7, hr + [Wk_b], [self.pb_b[pbi]])
                S.copy("dve", KA[0:64, tb * 512:(tb + 1) * 512], self.pb[pbi][0:64, :], [self.pb_b[pbi]], [KA_b])
        return f
    for tb in range(4):
        items.append(proj(tb, 1))
    for tb in range(4):
        items.append(proj(tb, 0))
    if A.kind == "fox":
        def aug1():
            S.copy("pool", A.Taug3[:, :, 64:67], fox.P4[:, :, h, :], [fox.b], [A.Taug_b])
            S.copy("pool", A.Taug3[:, :, 96:99], fox.N4[:, :, h, :], [fox.b], [A.Taug_b])

        def aug2(half):
            def f():
                pbi = 6 + half
                pv = self.pb[pbi][:].bitcast(BF16)
                for t8 in range(8):
                    tt = half * 8 + t8
                    S.tr(pv[0:99, t8 * 128:(t8 + 1) * 128], A.Taug3[:, tt, 0:99], self.ident[:], [A.Taug_b, self.cst_b],
                         [self.pb_b[pbi]])
                S.copy("dve", QA[64:67, half * 1024:(half + 1) * 1024], pv[64:67, :], [self.pb_b[pbi]], [QA_b])
                S.copy("dve", KA[96:99, half * 1024:(half + 1) * 1024], pv[96:99, :], [self.pb_b[pbi]], [KA_b])
            return f
        items.insert(0, aug1)
        items.append(aug2(0))
        items.append(aug2(1))
    else:
        def gate1():
            S.op("dve", lambda e: e.tensor_reduce(out=A.kb_f[:], in_=KA[0:64, :].rearrange("p (n t) -> p n t", n=8),
                                                  axis=AX.X, op=ALU.add), [KA_b], [A.g_b])
            S.ts("dve", A.kbT[:], A.kb_f[:], 1.0 / 256.0, None, ALU.mult, None, [A.g_b], [A.g_b])

        def gate2():
            for i in range(8):
                tt = 8 + i
                S.mm(self.pb[7][:, i * 8:(i + 1) * 8], QA[0:64, tt * 128:(tt + 1) * 128], A.kbT[:], True, True,
                     [QA_b, A.g_b], [self.pb_b[7]])
            g3 = self.pb[7][:, 0:64].rearrange("p (t n) -> p t n", t=8)
            for bp in range(4):
                qb = 4 + bp
                S.copy("dve", A.gt3[:, 2 * bp:2 * bp + 2, 0:qb], g3[:, 2 * bp:2 * bp + 2, 0:qb], [self.pb_b[7]], [A.g_b])
            for i in range(8):
                S.op("dve", lambda e, i=i: e.max(out=A.mx3[:, i, :], in_=A.gt3[:, i, :]), [A.g_b], [A.g_b])
            S.tt("dve", A.selb3, A.gt3, A.mx3[:, :, 2:3].to_broadcast([128, 8, 8]), ALU.is_ge, [A.g_b], [A.g_b])
            S.ts("dve", A.selb[:], A.selb[:], -1.0, BIG, ALU.add, ALU.mult, [A.g_b], [A.g_b])
            for bp in range(4):
                qb = 4 + bp
                S.copy("dve", A.Taug3[:, 8 + 2 * bp:10 + 2 * bp, 64:64 + qb], A.selb3[:, 2 * bp:2 * bp + 2, 0:qb], [A.g_b],
                       [A.Taug_b])

        def gate3():
            pv = self.pb[7][:].bitcast(BF16)
            for i in range(8):
                tt = 8 + i
                S.tr(pv[0:72, i * 128:(i + 1) * 128], A.Taug3[:, tt, 0:72], self.ident[:], [A.Taug_b, self.cst_b], [self.pb_b[7]])
            S.copy("dve", QA[64:72, 1024:2048], pv[64:72, :], [self.pb_b[7]], [QA_b])
        items.insert(4, gate1)
        items.append(gate2)
        items.append(gate3)
    return items


def _core_items(self, A, hh, sl):
    S = self.S
    QA, KA, QA_b, KA_b = A.QA[sl], A.KA[sl], A.QA_b[sl], A.KA_b[sl]
    osl = (hh // 2) % 2
    og3 = A.og[osl][:].rearrange("p (t n) -> p t n", t=NT)
    stA, stB = [], []
    for qc in range(4):
        ob = 2 + (A.o_i % 2)
        A.o_i += 1
        O3 = self.pb[ob][:, 0:4 * 65].rearrange("p (j d) -> p j d", j=4)
        nk = 4 * qc + 4
        for kt in range(nk):
            j0 = max(kt, 4 * qc)
            ncols = (4 * qc + 4 - j0) * 128
            sb = A.st_i % 2
            A.st_i += 1
            pi = A.pt_i % 3
            A.pt_i += 1
            diag = kt >= 4 * qc

            def fa(kt=kt, j0=j0, ncols=ncols, sb=sb, pi=pi, diag=diag, qc=qc):
                S.mm(self.pb[sb][:, 0:ncols], KA[:, kt * 128:(kt + 1) * 128], QA[:, j0 * 128:(4 * qc + 4) * 128],
                     True, not diag, [KA_b, QA_b], [self.pb_b[sb]])
                if diag:
                    S.mm(self.pb[sb][:, 0:128], self.ident[:], self.causT[:], False, True, [self.cst_b], [self.pb_b[sb]])
                S.act(A.PT[pi][:, 0:ncols], self.pb[sb][:, 0:ncols], AF.Exp, [self.pb_b[sb]], [A.PT_b[pi]])

            def fb(kt=kt, j0=j0, pi=pi, qc=qc, ob=ob, O3=O3, nk=nk):
                for j in range(j0, 4 * qc + 4):
                    c0 = (j - j0) * 128
                    jj = j - 4 * qc
                    S.mm(O3[:, jj, :], A.PT[pi][:, c0:c0 + 128], A.Vaug4[:, kt, hh, :], (kt == 0 and j == j0), kt == j,
                         [A.PT_b[pi], A.V_b], [self.pb_b[ob]], skip=True)
                if kt == nk - 1:
                    S.op("dve", lambda e: e.reciprocal(out=A.rec[:, 0:4], in_=O3[:, :, 64]), [self.pb_b[ob]], [A.rec_b])
                    for jj in range(4):
                        tt = 4 * qc + jj
                        S.stt(og3[:, tt, (hh % 2) * 64:(hh % 2) * 64 + 64], O3[:, jj, 0:64], A.rec[:, jj:jj + 1],
                              A.sZ3[:, tt, hh * 64:(hh + 1) * 64], ALU.mult, ALU.mult, [self.pb_b[ob], A.rec_b, A.sZ_b],
                              [A.og_b[osl]])
            stA.append(fa)
            stB.append(fb)
    return stA, stB


def attn_group2(self, A, l, g, zcol, qcol, kcol, vcol, ychunk0, fox=None):
    S = self.S
    hT3 = self.hT3
    Wv, Wv_b = self.load_w(l, vcol + g * 512, 512)
    Wz, Wz_b = self.load_w(l, zcol + g * 512, 512)
    Wq, Wq_b = self.load_w(l, qcol + g * 512, 512)
    for tt in range(NT):
        b0, b1 = 4 + 2 * (tt % 2), 5 + 2 * (tt % 2)
        for kc in range(8):
            S.mm(self.pb[b0][:], hT3[:, kc, tt * 128:(tt + 1) * 128], Wv[:, kc, :], kc == 0, kc == 7,
                 [self.hT_b[tt], Wv_b], [self.pb_b[b0]])
        S.copy("dve", A.Vaug4[:, tt, :, 0:64], self.pb[b0][:].rearrange("p (h d) -> p h d", h=8), [self.pb_b[b0]], [A.V_b])
        for kc in range(8):
            S.mm(self.pb[b1][:], hT3[:, kc, tt * 128:(tt + 1) * 128], Wz[:, kc, :], kc == 0, kc == 7,
                 [self.hT_b[tt], Wz_b], [self.pb_b[b1]])
        S.act(A.sZ3[:, tt, :], self.pb[b1][:], AF.Silu, [self.pb_b[b1]], [A.sZ_b])
    Wk, Wk_b = self.load_w(l, kcol + g * 512, 512)

    def og_items(hh):
        pr = hh // 2
        osl = pr % 2
        ch = ychunk0 + pr
        its = []
        for half in range(2):
            def f(half=half):
                pbi = 6 + half
                pv = self.pb[pbi][:].bitcast(BF16)
                for t8 in range(8):
                    tt = half * 8 + t8
                    S.tr(pv[:, t8 * 128:(t8 + 1) * 128], A.og[osl][:, tt * 128:(tt + 1) * 128], self.ident[:],
                         [A.og_b[osl], self.cst_b], [self.pb_b[pbi]])
                S.copy("dve", self.yT3[:, ch, half * 1024:(half + 1) * 1024], pv, [self.pb_b[pbi]], [self.yT_b[ch][half]])
            its.append(f)
        return its

    for it in _prep_items(self, A, l, g, 0, Wq, Wq_b, Wk, Wk_b, fox):
        it()
    LOOK = 2
    for hh in range(8):
        sl = (g * 8 + hh) % 2
        stA, stB = _core_items(self, A, hh, sl)
        side = []
        if hh > 0 and hh % 2 == 0:
            side += og_items(hh - 1)
        if hh < 7:
            side += _prep_items(self, A, l, g, hh + 1, Wq, Wq_b, Wk, Wk_b, fox)
        n = len(stA)
        for i in range(min(LOOK, n)):
            stA[i]()
        for i in range(n):
            stB[i]()
            if i + LOOK < n:
                stA[i + LOOK]()
            if side and i % 2 == 1:
                side.pop(0)()
        for it in side:
            it()
    for it in og_items(7):
        it()


Prog.attn_group = attn_group2


def attn_alloc2(self, kind):
    S = self.S
    A = type("A", (), {})()
    A.kind = kind
    A.Vaug = S.sb("Vaug", [128, NT * 8 * 65], BF16)
    A.Vaug4 = A.Vaug[:].rearrange("p (t h d) -> p t h d", t=NT, h=8)
    A.V_b = Buf("Vaug")
    A.sZ = S.sb("sZ", [128, NT * 512], BF16)
    A.sZ3 = A.sZ[:].rearrange("p (t n) -> p t n", t=NT)
    A.sZ_b = Buf("sZ")
    A.og = [S.sb("og%d" % i, [128, NT * 128], BF16) for i in range(2)]
    A.og_b = [Buf("og%d" % i) for i in range(2)]
    A.QAe = S.sb("QAe", [128, L], BF16); A.KAe = S.sb("KAe", [128, L], BF16)
    A.QAe_b, A.KAe_b = Buf("QAe"), Buf("KAe")
    A.QAo = [S.sb("QAo%d" % i, [128, L], BF16) for i in range(2)]
    A.KAo = [S.sb("KAo%d" % i, [128, L], BF16) for i in range(2)]
    A.QAo_b = [Buf("QAo%d" % i) for i in range(2)]
    A.KAo_b = [Buf("KAo%d" % i) for i in range(2)]
    A.Tg = [S.sb("Taug%d" % i, [128, NT * 128], BF16) for i in range(2)]
    A.Tg3 = [t[:].rearrange("p (t n) -> p t n", t=NT) for t in A.Tg]
    A.Tg_b = [Buf("TaugE"), Buf("TaugO")]
    A.PT = [S.sb("PT%d" % i, [128, 512], BF16) for i in range(4)]
    A.PT_b = [Buf("PT%d" % i) for i in range(4)]
    A.rec = S.sb("rec", [128, 8], F32)
    A.rec_b = Buf("rec")
    A.pt_i = A.st_i = A.o_i = 0
    S.memset("pool", A.Vaug[:], 1.0, [A.V_b])
    for i in range(2):
        S.memset("pool", A.Tg[i][:], 0.0, [A.Tg_b[i]])
    allq = [(A.QAe, A.QAe_b), (A.KAe, A.KAe_b)] + [(A.QAo[i], A.QAo_b[i]) for i in range(2)] + [(A.KAo[i], A.KAo_b[i]) for i in range(2)]
    for t, b in allq:
        S.memset("pool", t[:], 0.0, [b])
    A.aoff = [64, 0]
    A.qoff = [0, 64]
    if kind == "fox":
        S.memset("pool", A.QAe[96:99, :], 1.0, [A.QAe_b])
        S.memset("pool", A.KAe[64:67, :], 1.0, [A.KAe_b])
        for i in range(2):
            S.memset("pool", A.QAo[i][32:35, :], 1.0, [A.QAo_b[i]])
            S.memset("pool", A.KAo[i][0:3, :], 1.0, [A.KAo_b[i]])
    else:
        for par in range(2):
            ao = A.aoff[par]
            for n in range(8):
                S.memset("pool", A.Tg3[par][:, 2 * n:2 * n + 2, ao + n:ao + n + 1], 1.0, [A.Tg_b[par]])
            dsts = [(A.KAe, A.KAe_b)] if par == 0 else [(A.KAo[i], A.KAo_b[i]) for i in range(2)]
            for half in range(2):
                pbi = 6 + half
                pv = self.pb[pbi][:].bitcast(BF16)
                for t8 in range(8):
                    tt = half * 8 + t8
                    S.tr(pv[0:72, t8 * 128:(t8 + 1) * 128], A.Tg3[par][:, tt, 0:72], self.ident[:], [A.Tg_b[par], self.cst_b],
                         [self.pb_b[pbi]])
                for t, b in dsts:
                    S.copy("dve", t[ao:ao + 8, half * 1024:(half + 1) * 1024], pv[ao:ao + 8, :], [self.pb_b[pbi]], [b])
            S.memset("pool", A.Tg[par][:], 0.0, [A.Tg_b[par]])
        A.gt = S.sb("gt", [128, 64], F32); A.gt3 = A.gt[:].rearrange("p (t n) -> p t n", t=8)
        A.mx = S.sb("mx", [128, 64], F32); A.mx3 = A.mx[:].rearrange("p (t n) -> p t n", t=8)
        A.selb = S.sb("selb", [128, 64], F32); A.selb3 = A.selb[:].rearrange("p (t n) -> p t n", t=8)
        A.kb_f = S.sb("kb_f", [128, 8], F32)
        A.kbT = S.sb("kbT", [128, 8], BF16)
        A.g_b = Buf("gate")
        S.memset("pool", A.gt[:], -BIG, [A.g_b])
    return A


def _head_bufs(A, gh):
    if gh % 2 == 0:
        return A.QAe, A.KAe, A.QAe_b, A.KAe_b, 0
    k = (gh // 2) % 2
    return A.QAo[k], A.KAo[k], A.QAo_b[k], A.KAo_b[k], 1


def _prep_pair_items(self, A, l, g, pr, Wq, Wq_b, Wk, Wk_b, fox):
    S = self.S
    hT3 = self.hT3
    heads = [g * 8 + 2 * pr, g * 8 + 2 * pr + 1]
    hb = [_head_bufs(A, gh) for gh in heads]
    items = []

    def proj(tb, which, part):
        def f():
            hr = [self.hT_b[4 * tb + i] for i in range(4)]
            W, W_b = (Wq, Wq_b) if which == 0 else (Wk, Wk_b)
            pbi = 5 if tb % 2 == 0 else 7
            for kc in range(4 * part, 4 * part + 4):
                S.mm(self.pb[pbi][:], W[:, kc, pr * 128:(pr + 1) * 128], hT3[:, kc, tb * 512:(tb + 1) * 512],
                     kc == 0, kc == 7, hr + [W_b], [self.pb_b[pbi]])
            if part == 0:
                return
            for par in range(2):
                QA, KA, QA_b, KA_b, _ = hb[par]
                r0 = 64 * par
                if which == 0:
                    S.ts("dve", QA[r0:r0 + 64, tb * 512:(tb + 1) * 512], self.pb[pbi][r0:r0 + 64, :], 0.125, None, ALU.mult, None,
                         [self.pb_b[pbi]], [QA_b])
                else:
                    S.copy("dve", KA[r0:r0 + 64, tb * 512:(tb + 1) * 512], self.pb[pbi][r0:r0 + 64, :], [self.pb_b[pbi]], [KA_b])
        return f
    for tb in range(4):
        items.append(proj(tb, 1, 0))
        items.append(proj(tb, 1, 1))
    if A.kind != "fox":
        def gate1(par):
            def f():
                QA, KA, QA_b, KA_b, _ = hb[par]
                r0 = 64 * par
                S.op("dve", lambda e: e.tensor_reduce(out=A.kb_f[r0:r0 + 64, :], in_=KA[r0:r0 + 64, :].rearrange("p (n t) -> p n t", n=8),
                                                      axis=AX.X, op=ALU.add), [KA_b], [A.g_b])
                S.ts("dve", A.kbT[r0:r0 + 64, :], A.kb_f[r0:r0 + 64, :], 1.0 / 256.0, None, ALU.mult, None, [A.g_b], [A.g_b])
            return f
    for tb in range(4):
        items.append(proj(tb, 0, 0))
        items.append(proj(tb, 0, 1))
    for par in range(2):
        gh = heads[par]
        QA, KA, QA_b, KA_b, _ = hb[par]
        ao, r0 = A.aoff[par], 64 * par
        Tg3, Tg_b = A.Tg3[par], A.Tg_b[par]
        if A.kind == "fox":
            def aug1(gh=gh, ao=ao, Tg3=Tg3, Tg_b=Tg_b):
                S.copy("pool", Tg3[:, :, ao:ao + 3], fox.P4[:, :, gh, :], [fox.b], [Tg_b])
                S.copy("pool", Tg3[:, :, ao + 32:ao + 35], fox.N4[:, :, gh, :], [fox.b], [Tg_b])

            def aug2(half, ao=ao, Tg3=Tg3, Tg_b=Tg_b, QA=QA, KA=KA, QA_b=QA_b, KA_b=KA_b):
                def f():
                    pbi = 6
                    pv = self.pb[pbi][:].bitcast(BF16)
                    for t8 in range(8):
                        tt = half * 8 + t8
                        S.tr(pv[0:ao + 35, t8 * 128:(t8 + 1) * 128], Tg3[:, tt, 0:ao + 35], self.ident[:], [Tg_b, self.cst_b],
                             [self.pb_b[pbi]])
                    S.copy("dve", QA[ao:ao + 3, half * 1024:(half + 1) * 1024], pv[ao:ao + 3, :], [self.pb_b[pbi]], [QA_b])
                    S.copy("dve", KA[ao + 32:ao + 35, half * 1024:(half + 1) * 1024], pv[ao + 32:ao + 35, :], [self.pb_b[pbi]], [KA_b])
                return f
            items.insert(0, aug1)
            items.append(aug2(0))
            items.append(aug2(1))
        else:
            def gate(par=par, ao=ao, r0=r0, Tg3=Tg3, Tg_b=Tg_b, QA=QA, KA=KA, QA_b=QA_b, KA_b=KA_b):
                S.op("dve", lambda e: e.tensor_reduce(out=A.kb_f[r0:r0 + 64, :], in_=KA[r0:r0 + 64, :].rearrange("p (n t) -> p n t", n=8),
                                                      axis=AX.X, op=ALU.add), [KA_b], [A.g_b])
                S.ts("dve", A.kbT[r0:r0 + 64, :], A.kb_f[r0:r0 + 64, :], 1.0 / 256.0, None, ALU.mult, None, [A.g_b], [A.g_b])
                for i in range(8):
                    tt = 8 + i
                    S.mm(self.pb[6][:, i * 8:(i + 1) * 8], QA[r0:r0 + 64, tt * 128:(tt + 1) * 128], A.kbT[r0:r0 + 64, :], True, True,
                         [QA_b, A.g_b], [self.pb_b[6]])
                g3 = self.pb[6][:, 0:64].rearrange("p (t n) -> p t n", t=8)
                for bp in range(4):
                    qb = 4 + bp
                    S.copy("dve", A.gt3[:, 2 * bp:2 * bp + 2, 0:qb], g3[:, 2 * bp:2 * bp + 2, 0:qb], [self.pb_b[6]], [A.g_b])
                for i in range(8):
                    S.op("dve", lambda e, i=i: e.max(out=A.mx3[:, i, :], in_=A.gt3[:, i, :]), [A.g_b], [A.g_b])
                S.tt("dve", A.selb3, A.gt3, A.mx3[:, :, 2:3].to_broadcast([128, 8, 8]), ALU.is_ge, [A.g_b], [A.g_b])
                S.ts("dve", A.selb[:], A.selb[:], -1.0, BIG, ALU.add, ALU.mult, [A.g_b], [A.g_b])
                for bp in range(4):
                    qb = 4 + bp
                    S.copy("dve", Tg3[:, 8 + 2 * bp:10 + 2 * bp, ao:ao + qb], A.selb3[:, 2 * bp:2 * bp + 2, 0:qb], [A.g_b], [Tg_b])

            def gate3(ao=ao, Tg3=Tg3, Tg_b=Tg_b, QA=QA, QA_b=QA_b):
                pv = self.pb[6][:].bitcast(BF16)
                for i in range(8):
                    tt = 8 + i
                    S.tr(pv[0:ao + 8, i * 128:(i + 1) * 128], Tg3[:, tt, 0:ao + 8], self.ident[:], [Tg_b, self.cst_b], [self.pb_b[6]])
                S.copy("dve", QA[ao:ao + 8, 1024:2048], pv[ao:ao + 8, :], [self.pb_b[6]], [QA_b])
            items.append(gate)
            items.append(gate3)
    return items


def _core_items2(self, A, hh, QA, KA, QA_b, KA_b):
    S = self.S
    osl = (hh // 2) % 2
    og3 = A.og[osl][:].rearrange("p (t n) -> p t n", t=NT)
    stA, stB = [], []
    for qc in range(4):
        ob = 2 + (A.o_i % 2)
        A.o_i += 1
        O3 = self.pb[ob][:, 0:4 * 65].rearrange("p (j d) -> p j d", j=4)
        nk = 4 * qc + 4
        for kt in range(nk):
            j0 = max(kt, 4 * qc)
            ncols = (4 * qc + 4 - j0) * 128
            sb = (0, 1, 4)[A.st_i % 3]
            A.st_i += 1
            pi = A.pt_i % 4
            A.pt_i += 1
            diag = kt >= 4 * qc

            def fa(kt=kt, j0=j0, ncols=ncols, sb=sb, pi=pi, diag=diag, qc=qc):
                S.mm(self.pb[sb][:, 0:ncols], KA[:, kt * 128:(kt + 1) * 128], QA[:, j0 * 128:(4 * qc + 4) * 128],
                     True, not diag, [KA_b, QA_b], [self.pb_b[sb]])
                if diag:
                    S.mm(self.pb[sb][:, 0:128], self.ident[:], self.causT[:], False, True, [self.cst_b], [self.pb_b[sb]])
                S.act(A.PT[pi][:, 0:ncols], self.pb[sb][:, 0:ncols], AF.Exp, [self.pb_b[sb]], [A.PT_b[pi]])

            def fb(kt=kt, j0=j0, pi=pi, qc=qc, ob=ob, O3=O3, nk=nk):
                for j in range(j0, 4 * qc + 4):
                    c0 = (j - j0) * 128
                    jj = j - 4 * qc
                    S.mm(O3[:, jj, :], A.PT[pi][:, c0:c0 + 128], A.Vaug4[:, kt, hh, :], (kt == 0 and j == j0), kt == j,
                         [A.PT_b[pi], A.V_b], [self.pb_b[ob]], skip=True)
                if kt == nk - 1:
                    S.op("dve", lambda e: e.reciprocal(out=A.rec[:, 0:4], in_=O3[:, :, 64]), [self.pb_b[ob]], [A.rec_b])
                    for jj in range(4):
                        tt = 4 * qc + jj
                        S.stt(og3[:, tt, (hh % 2) * 64:(hh % 2) * 64 + 64], O3[:, jj, 0:64], A.rec[:, jj:jj + 1],
                              A.sZ3[:, tt, hh * 64:(hh + 1) * 64], ALU.mult, ALU.mult, [self.pb_b[ob], A.rec_b, A.sZ_b],
                              [A.og_b[osl]])
            stA.append(fa)
            stB.append(fb)
    return stA, stB


def attn_group3(self, A, l, g, zcol, qcol, kcol, vcol, ychunk0, fox=None, ngroups=1):
    S = self.S
    hT3 = self.hT3
    Wv, Wv_b = self.load_w(l, vcol + g * 512, 512)
    Wz, Wz_b = self.load_w(l, zcol + g * 512, 512)
    Wq, Wq_b = self.load_w(l, qcol + g * 512, 512)
    for tt in range(NT):
        b0, b1 = 4 + 2 * (tt % 2), 5 + 2 * (tt % 2)
        for kc in range(8):
            S.mm(self.pb[b0][:], hT3[:, kc, tt * 128:(tt + 1) * 128], Wv[:, kc, :], kc == 0, kc == 7,
                 [self.hT_b[tt], Wv_b], [self.pb_b[b0]])
        S.copy("dve", A.Vaug4[:, tt, :, 0:64], self.pb[b0][:].rearrange("p (h d) -> p h d", h=8), [self.pb_b[b0]], [A.V_b])
        for kc in range(8):
            S.mm(self.pb[b1][:], hT3[:, kc, tt * 128:(tt + 1) * 128], Wz[:, kc, :], kc == 0, kc == 7,
                 [self.hT_b[tt], Wz_b], [self.pb_b[b1]])
        S.act(A.sZ3[:, tt, :], self.pb[b1][:], AF.Silu, [self.pb_b[b1]], [A.sZ_b])
    Wk, Wk_b = self.load_w(l, kcol + g * 512, 512)

    def og_items(hh):
        pr = hh // 2
        osl = pr % 2
        ch = ychunk0 + pr
        its = []
        for half in range(2):
            def f(half=half):
                pbi = 6
                pv = self.pb[pbi][:].bitcast(BF16)
                for t8 in range(8):
                    tt = half * 8 + t8
                    S.tr(pv[:, t8 * 128:(t8 + 1) * 128], A.og[osl][:, tt * 128:(tt + 1) * 128], self.ident[:],
                         [A.og_b[osl], self.cst_b], [self.pb_b[pbi]])
                S.copy("dve", self.yT3[:, ch, half * 1024:(half + 1) * 1024], pv, [self.pb_b[pbi]], [self.yT_b[ch][half]])
            its.append(f)
        return its

    for it in _prep_pair_items(self, A, l, g, 0, Wq, Wq_b, Wk, Wk_b, fox):
        it()
    LOOK = 3
    for hh in range(8):
        QA, KA, QA_b, KA_b, par = _head_bufs(A, g * 8 + hh)
        stA, stB = _core_items2(self, A, hh, QA, KA, QA_b, KA_b)
        side = []
        if hh > 0 and hh % 2 == 0:
            side += og_items(hh - 1)
        if hh % 2 == 1 and hh < 7:
            side += _prep_pair_items(self, A, l, g, hh // 2 + 1, Wq, Wq_b, Wk, Wk_b, fox)
        if hh == 6 and g == ngroups - 1 and ngroups > 1:
            side.append(lambda: self.prefetch_wout(l))
        n = len(stA)
        for i in range(min(LOOK, n)):
            stA[i]()
        for i in range(n):
            stB[i]()
            if i + LOOK < n:
                stA[i + LOOK]()
            if side:
                side.pop(0)()
        for it in side:
            it()
    for it in og_items(7):
        it()


Prog.attn_alloc = attn_alloc2
Prog.attn_group = attn_group3
```
